# Optimizing a Trainium2 kernel written in Bass

```python
import jax, jax.numpy as jnp
from jax import lax
import numpy as np

D_MODEL = 1024
BATCH = 32
SEQ = 256
DEPTH = 2
DEC_BATCH = 4
DEC_SEQ = 1024
PAST_LEN = 256

GRID_W = 64
NA_HEADS = 8
NA_HEAD_DIM = 64
NA_WIDTH = NA_HEADS * NA_HEAD_DIM
NA_WIN_ROWS = 8
NA_WIN_COLS = 16
SSD_HEADS = 16
SSD_HEAD_DIM = 64
SSD_D_INNER = SSD_HEADS * SSD_HEAD_DIM
SSD_GROUPS = 2
SSD_STATE = 128
SSD_CONV = 5
SSD_XBC = SSD_D_INNER + 2 * SSD_GROUPS * SSD_STATE
RET_HEADS = 4
RET_QK_DIM = 256
RET_V_DIM = 512
RET_QK_W = RET_HEADS * RET_QK_DIM
RET_V_W = RET_HEADS * RET_V_DIM
SCAN_CHUNK = 64
FFN_HIDDEN = ((8 * D_MODEL + 3 * 256 - 1) // (3 * 256)) * 256
ROPE_BASE = 10000.0
EPS = 1e-6
L0_SPLITS = (NA_WIDTH, 2 * NA_WIDTH, 3 * NA_WIDTH, 3 * NA_WIDTH + SSD_D_INNER,
             3 * NA_WIDTH + SSD_D_INNER + SSD_XBC)
L0_IN = 3 * NA_WIDTH + SSD_D_INNER + SSD_XBC + 2 * SSD_HEADS
L0_MIX = NA_WIDTH + SSD_D_INNER
L1_SPLITS = (RET_QK_W, 2 * RET_QK_W, 2 * RET_QK_W + RET_V_W)
L1_IN = 2 * RET_QK_W + 2 * RET_V_W

kernel_name = "hybrid_na_ssd_retention_diffusion_step"

F32 = jnp.float32


def rmsnorm(x, w):
    x32 = x.astype(F32)
    y = x32 * lax.rsqrt(jnp.mean(x32 * x32, axis=-1, keepdims=True) + EPS)
    return (y * w.astype(F32)).astype(x.dtype)


def adaln(cond, mod_w, mod_b):
    mod = jax.nn.silu(cond) @ mod_w + mod_b
    if mod.ndim == 2:
        mod = mod[:, None, :]
    return jnp.split(mod, 6, axis=-1)


def swiglu(h, w1, w3, w2):
    return (jax.nn.silu(h @ w1) * (h @ w3)) @ w2


def axial_rope(x):
    l, dh = x.shape[1], x.shape[-1]
    t = jnp.arange(l)
    row = (t // GRID_W).astype(F32)
    col = (t % GRID_W).astype(F32)
    half = dh // 2
    freqs = ROPE_BASE ** (-jnp.arange(0, half, 2, dtype=F32) / half)
    ang = jnp.concatenate([row[:, None] * freqs, col[:, None] * freqs], axis=-1)
    cos, sin = jnp.cos(ang)[:, None, :], jnp.sin(ang)[:, None, :]
    xr = x.astype(F32).reshape(x.shape[:-1] + (dh // 2, 2))
    x1, x2 = xr[..., 0], xr[..., 1]
    out = jnp.stack([x1 * cos - x2 * sin, x1 * sin + x2 * cos], axis=-1)
    return out.reshape(x.shape).astype(x.dtype)


def chunked_scan(q, k, v, log_a, s0):
    b, l, h, dk = q.shape
    dv = v.shape[-1]
    nc = l // SCAN_CHUNK

    def to_chunks(t):
        return jnp.moveaxis(t.astype(F32).reshape((b, nc, SCAN_CHUNK) + t.shape[2:]), 1, 0)

    lower = jnp.tril(jnp.ones((SCAN_CHUNK, SCAN_CHUNK), bool))[None, :, :, None]

    def step(s, inp):
        qi, ki, vi, ai = inp
        cs = jnp.cumsum(ai, axis=1)
        diff = cs[:, :, None, :] - cs[:, None, :, :]
        decay = jnp.exp(jnp.where(lower, diff, -jnp.inf))
        scores = jnp.einsum('bihd,bjhd->bijh', qi, ki) * decay
        y = jnp.einsum('bijh,bjhe->bihe', scores, vi)
        y = y + jnp.einsum('bihd,bhde->bihe', qi * jnp.exp(cs)[..., None], s)
        tail = jnp.exp(cs[:, -1:, :] - cs)
        s_new = (jnp.exp(cs[:, -1, :])[:, :, None, None] * s
                 + jnp.einsum('bjhd,bjhe->bhde', ki * tail[..., None], vi))
        return s_new, y

    s_fin, ys = lax.scan(step, s0.astype(F32), (to_chunks(q), to_chunks(k), to_chunks(v), to_chunks(log_a)))
    return jnp.moveaxis(ys, 0, 1).reshape(b, l, h, dv), s_fin


def bidir_scan(q, k, v_f, v_b, a_f, a_b, s0):
    flip = lambda t: jnp.flip(t, axis=1)
    y_f, s_f = chunked_scan(q, k, v_f, a_f, s0[:, 0])
    y_b, s_b = chunked_scan(flip(q), flip(k), flip(v_b), flip(a_b), s0[:, 1])
    return y_f + flip(y_b), jnp.stack([s_f, s_b], axis=1)


def depthwise_conv(x, w, b):
    pad = SSD_CONV // 2
    out = lax.conv_general_dilated(x, w[:, None, :].astype(x.dtype), window_strides=(1,),
                                   padding=[(pad, pad)], dimension_numbers=('NWC', 'WIO', 'NWC'),
                                   feature_group_count=x.shape[-1])
    return out + b


def ssd_mixer(z, xbc, dt_raw, conv_w, conv_b, a_log, dt_bias, d_skip, norm_w, s0):
    b, l, _ = z.shape
    xbc = jax.nn.silu(depthwise_conv(xbc, conv_w, conv_b)).astype(F32)
    xs, bs, cs = jnp.split(xbc, (SSD_D_INNER, SSD_D_INNER + SSD_GROUPS * SSD_STATE), axis=-1)
    rep = SSD_HEADS // SSD_GROUPS
    xh = xs.reshape(b, l, SSD_HEADS, SSD_HEAD_DIM)
    bh = jnp.repeat(bs.reshape(b, l, SSD_GROUPS, SSD_STATE), rep, axis=2)
    ch = jnp.repeat(cs.reshape(b, l, SSD_GROUPS, SSD_STATE), rep, axis=2)
    dt = jax.nn.softplus(dt_raw.astype(F32) + dt_bias.astype(F32))
    log_a = dt * (-jnp.exp(a_log.astype(F32)))
    y, s_fin = bidir_scan(ch, bh, xh * dt[:, :, 0, :, None], xh * dt[:, :, 1, :, None],
                          log_a[:, :, 0], log_a[:, :, 1], s0)
    y = y + d_skip.astype(F32)[:, None] * xh
    y = y.reshape(b, l, SSD_D_INNER) * jax.nn.silu(z.astype(F32))
    yg = y.reshape(b, l, SSD_GROUPS, SSD_D_INNER // SSD_GROUPS)
    yg = yg * lax.rsqrt(jnp.mean(yg * yg, axis=-1, keepdims=True) + EPS)
    return yg.reshape(b, l, SSD_D_INNER) * norm_w.astype(F32), s_fin


def na_context(q, k, v):
    s = jnp.einsum('blhd,bmhd->bhlm', q, k).astype(F32) * NA_HEAD_DIM ** -0.5
    p = jax.nn.softmax(s, axis=-1).astype(v.dtype)
    return jnp.einsum('bhlm,bmhd->blhd', p, v)


def na_latent(q, k, v, k_ctx, v_ctx, rel_bias):
    b, l, h, d = q.shape
    rows = l // GRID_W
    wr = min(NA_WIN_ROWS, rows)
    r = jnp.arange(rows)
    r0 = jnp.clip(r - wr // 2, 0, rows - wr)
    band_rows = r0[:, None] + jnp.arange(wr)
    kg = k.reshape(b, rows, GRID_W, h, d)[:, band_rows]
    vg = v.reshape(b, rows, GRID_W, h, d)[:, band_rows]
    qg = q.reshape(b, rows, GRID_W, h, d)
    scale = NA_HEAD_DIM ** -0.5
    s_loc = jnp.einsum('brchd,brswhd->bhrcsw', qg, kg).astype(F32) * scale
    col = jnp.arange(GRID_W)
    c0 = jnp.clip(col - NA_WIN_COLS // 2, 0, GRID_W - NA_WIN_COLS)
    col_in = (col[None, :] >= c0[:, None]) & (col[None, :] < c0[:, None] + NA_WIN_COLS)
    dr_idx = band_rows - r[:, None] + NA_WIN_ROWS - 1
    dc_idx = jnp.clip(col[None, :] - col[:, None] + NA_WIN_COLS - 1, 0, 2 * NA_WIN_COLS - 2)
    bias = rel_bias.astype(F32)[:, dr_idx[:, None, :, None], dc_idx[None, :, None, :]]
    s_loc = jnp.where(col_in[:, None, :], s_loc + bias[None], -jnp.inf)
    s_ctx = jnp.einsum('brchd,bmhd->bhrcm', qg, k_ctx).astype(F32) * scale
    n_loc = wr * GRID_W
    s_all = jnp.concatenate([s_loc.reshape(b, h, rows, GRID_W, n_loc), s_ctx], axis=-1)
    p = jax.nn.softmax(s_all, axis=-1).astype(v.dtype)
    p_loc = p[..., :n_loc].reshape(b, h, rows, GRID_W, wr, GRID_W)
    out = (jnp.einsum('bhrcsw,brswhd->brchd', p_loc, vg)
           + jnp.einsum('bhrcm,bmhd->brchd', p[..., n_loc:], v_ctx))
    return out.reshape(b, l, h, d)


def even_mixer(h, w_in, w_out, na_bias, conv_w, conv_b, a_log, dt_bias, d_skip, norm_w, cache):
    b, l, _ = h.shape
    q, k, v, z, xbc, dt_raw = jnp.split(h @ w_in, L0_SPLITS, axis=-1)
    heads = lambda t: t.reshape(b, l, NA_HEADS, NA_HEAD_DIM)
    q, k, v = heads(q), heads(k), heads(v)
    if cache is None:
        att = na_context(q, k, v)
        s0 = jnp.zeros((b, 2, SSD_HEADS, SSD_STATE, SSD_HEAD_DIM), F32)
    else:
        k_ctx, v_ctx, s0 = cache
        att = na_latent(q, k, v, k_ctx, v_ctx, na_bias)
    ssd, s_fin = ssd_mixer(z, xbc, dt_raw.reshape(b, l, 2, SSD_HEADS), conv_w, conv_b,
                           a_log, dt_bias, d_skip, norm_w, s0)
    mixed = jnp.concatenate([att.reshape(b, l, NA_WIDTH), ssd.astype(h.dtype)], axis=-1)
    return mixed @ w_out, (k, v, s_fin)


def odd_mixer(h, w_in, w_out, ret_decay, ret_norm_w, cache):
    b, l, _ = h.shape
    q, k, v, g = jnp.split(h @ w_in, L1_SPLITS, axis=-1)
    q = q.reshape(b, l, RET_HEADS, RET_QK_DIM)
    k = k.reshape(b, l, RET_HEADS, RET_QK_DIM) * RET_QK_DIM ** -0.5
    v = v.reshape(b, l, RET_HEADS, RET_V_DIM)
    if cache is None:
        s0 = jnp.zeros((b, 2, RET_HEADS, RET_QK_DIM, RET_V_DIM), F32)
    else:
        q, k, s0 = axial_rope(q), axial_rope(k), cache
    log_g = jax.nn.log_sigmoid(ret_decay.astype(F32))
    a_f = jnp.broadcast_to(log_g[0], (b, l, RET_HEADS))
    a_b = jnp.broadcast_to(log_g[1], (b, l, RET_HEADS))
    y, s_fin = bidir_scan(q, k, v, v, a_f, a_b, s0)
    y = y * lax.rsqrt(jnp.mean(y * y, axis=-1, keepdims=True) + EPS)
    y = y.reshape(b, l, RET_V_W) * ret_norm_w.astype(F32) * jax.nn.silu(g.astype(F32))
    return y.astype(h.dtype) @ w_out, s_fin


def block(x, cond, mixer, norm1_w, norm2_w, mod_w, mod_b, w1, w3, w2):
    sh1, sc1, g1, sh2, sc2, g2 = adaln(cond, mod_w, mod_b)
    out, aux = mixer(rmsnorm(x, norm1_w) * (1 + sc1) + sh1)
    x = x + g1 * out
    x = x + g2 * swiglu(rmsnorm(x, norm2_w) * (1 + sc2) + sh2, w1, w3, w2)
    return x, aux


def setup_inputs(seed: int = 0) -> dict:
    key = jax.random.key(seed)
    ks = iter(jax.random.split(key, 64))
    D = D_MODEL
    nrm = lambda shape, scale: jax.random.normal(next(ks), shape, F32) * scale
    uni = lambda shape, lo, hi: jax.random.uniform(next(ks), shape, F32, lo, hi)
    inp = {}
    inp['x_prompt'] = nrm((BATCH, SEQ, D), 1.0)
    inp['x_sample'] = nrm((DEC_BATCH, DEC_SEQ, D), 1.0)
    inp['cache_l0_na_k'] = nrm((DEC_BATCH, PAST_LEN, NA_HEADS, NA_HEAD_DIM), 1.0)
    inp['cache_l0_na_v'] = nrm((DEC_BATCH, PAST_LEN, NA_HEADS, NA_HEAD_DIM), 1.0)
    inp['state_l0_ssd'] = nrm((DEC_BATCH, 2, SSD_HEADS, SSD_STATE, SSD_HEAD_DIM), 0.1)
    inp['state_l1_ret'] = nrm((DEC_BATCH, 2, RET_HEADS, RET_QK_DIM, RET_V_DIM), 0.1)
    inp['c'] = nrm((DEC_BATCH, D), 1.0)
    inp['c_ctx'] = nrm((D,), 1.0)
    inp['l0_norm1_w'] = 1.0 + nrm((D,), 0.02)
    inp['l0_norm2_w'] = 1.0 + nrm((D,), 0.02)
    inp['l0_mod_w'] = nrm((D, 6 * D), 0.5 * D ** -0.5)
    inp['l0_mod_b'] = nrm((6 * D,), 0.02)
    inp['l0_w_in'] = jnp.concatenate([nrm((D, L0_IN - 2 * SSD_HEADS), D ** -0.5),
                                      nrm((D, 2 * SSD_HEADS), 0.1 * D ** -0.5)], axis=1)
    inp['l0_w_out'] = nrm((L0_MIX, D), L0_MIX ** -0.5)
    inp['l0_na_bias'] = nrm((NA_HEADS, 2 * NA_WIN_ROWS - 1, 2 * NA_WIN_COLS - 1), 0.02)
    inp['l0_conv_w'] = nrm((SSD_CONV, SSD_XBC), SSD_CONV ** -0.5)
    inp['l0_conv_b'] = nrm((SSD_XBC,), 0.02)
    inp['l0_ssd_a_log'] = jnp.log(uni((2, SSD_HEADS), 1.0, 16.0))
    dt0 = jnp.exp(uni((2, SSD_HEADS), float(np.log(1e-3)), float(np.log(1e-1))))
    inp['l0_ssd_dt_bias'] = dt0 + jnp.log(-jnp.expm1(-dt0))
    inp['l0_ssd_d'] = 1.0 + nrm((SSD_HEADS,), 0.1)
    inp['l0_ssd_norm_w'] = 1.0 + nrm((SSD_D_INNER,), 0.02)
    inp['l0_ffn_w1'] = nrm((D, FFN_HIDDEN), D ** -0.5)
    inp['l0_ffn_w3'] = nrm((D, FFN_HIDDEN), D ** -0.5)
    inp['l0_ffn_w2'] = nrm((FFN_HIDDEN, D), FFN_HIDDEN ** -0.5)
    inp['l1_norm1_w'] = 1.0 + nrm((D,), 0.02)
    inp['l1_norm2_w'] = 1.0 + nrm((D,), 0.02)
    inp['l1_mod_w'] = nrm((D, 6 * D), 0.5 * D ** -0.5)
    inp['l1_mod_b'] = nrm((6 * D,), 0.02)
    inp['l1_w_in'] = nrm((D, L1_IN), D ** -0.5)
    inp['l1_w_out'] = nrm((RET_V_W, D), RET_V_W ** -0.5)
    gamma = 1.0 - 2.0 ** (-5.0 - np.arange(RET_HEADS, dtype=np.float32))
    logit = np.log(gamma) - np.log1p(-gamma)
    inp['l1_ret_decay'] = jnp.asarray(logit, F32)[None, :] + nrm((2, RET_HEADS), 0.1)
    inp['l1_ret_norm_w'] = 1.0 + nrm((RET_V_W,), 0.02)
    inp['l1_ffn_w1'] = nrm((D, FFN_HIDDEN), D ** -0.5)
    inp['l1_ffn_w3'] = nrm((D, FFN_HIDDEN), D ** -0.5)
    inp['l1_ffn_w2'] = nrm((FFN_HIDDEN, D), FFN_HIDDEN ** -0.5)
    inp['final_norm_w'] = 1.0 + nrm((D,), 0.02)
    return inp


def reference(x_prompt, x_sample, cache_l0_na_k, cache_l0_na_v, state_l0_ssd, state_l1_ret, c, c_ctx,
              l0_norm1_w, l0_norm2_w, l0_mod_w, l0_mod_b, l0_w_in, l0_w_out, l0_na_bias, l0_conv_w, l0_conv_b,
              l0_ssd_a_log, l0_ssd_dt_bias, l0_ssd_d, l0_ssd_norm_w, l0_ffn_w1, l0_ffn_w3, l0_ffn_w2,
              l1_norm1_w, l1_norm2_w, l1_mod_w, l1_mod_b, l1_w_in, l1_w_out, l1_ret_decay, l1_ret_norm_w,
              l1_ffn_w1, l1_ffn_w3, l1_ffn_w2, final_norm_w):
    mixers = [
        lambda h, cache: even_mixer(h, l0_w_in, l0_w_out, l0_na_bias, l0_conv_w, l0_conv_b, l0_ssd_a_log,
                                    l0_ssd_dt_bias, l0_ssd_d, l0_ssd_norm_w, cache),
        lambda h, cache: odd_mixer(h, l1_w_in, l1_w_out, l1_ret_decay, l1_ret_norm_w, cache),
    ]
    block_w = [
        (l0_norm1_w, l0_norm2_w, l0_mod_w, l0_mod_b, l0_ffn_w1, l0_ffn_w3, l0_ffn_w2),
        (l1_norm1_w, l1_norm2_w, l1_mod_w, l1_mod_b, l1_ffn_w1, l1_ffn_w3, l1_ffn_w2),
    ]
    caches = [(cache_l0_na_k, cache_l0_na_v, state_l0_ssd), state_l1_ret]

    xc = x_prompt
    ctx_out = []
    for i in range(DEPTH):
        xc, aux = block(xc, c_ctx, lambda h: mixers[i](h, None), *block_w[i])
        ctx_out.append(aux)

    xs = x_sample
    for i in range(DEPTH):
        xs, _ = block(xs, c, lambda h: mixers[i](h, caches[i]), *block_w[i])

    y_prompt = rmsnorm(xc, final_norm_w)
    y_sample = rmsnorm(xs, final_norm_w)
    (new_l0_na_k, new_l0_na_v, new_l0_ssd), new_l1_ret = ctx_out[0], ctx_out[1]
    return (y_prompt, y_sample, new_l0_na_k, new_l0_na_v, new_l0_ssd, new_l1_ret)
```

```python
import numpy as np
import concourse.bass as bass
import concourse.mybir as mybir
from concourse.bass_utils import run_bass_kernel_spmd

F32 = mybir.dt.float32
BF16 = mybir.dt.bfloat16
AF = mybir.ActivationFunctionType
ALU = mybir.AluOpType

D = 1024
NT = 1024
KT = 8
FFN_H = 2816
EPS = 1e-6
L0_IN = 4128
L1_IN = 6144


class Sched:
    ENG = ("pe", "act", "dve", "pool", "sp")

    def __init__(self, nc):
        self.nc = nc
        self.q = {k: [] for k in self.ENG}
        self.esem = {k: nc.alloc_semaphore(name=f"s_{k}") for k in self.ENG}
        self.ecnt = {k: 0 for k in self.ENG}
        self.open = {k: False for k in self.ENG}
        self.known = {k: {} for k in self.ENG}
        self.pending = {k: [] for k in self.ENG}
        self.lastw = {}
        self.readers = {}
        self.dsem = {}
        self.dcnt = {}
        self.dpersist = set()
        self.out_sems = set()

    def _need(self, eng, reads, writes, use_pending=True):
        need = {}

        def add(sv, kind):
            sem, val, owner = sv
            if owner == eng and (eng == "pe" or kind == "war"):
                return
            nm = sem.name
            if nm not in need or need[nm][1] < val:
                need[nm] = (sem, val)

        for k in reads:
            if k in self.lastw:
                add(self.lastw[k], "raw")
        for k in writes:
            if k in self.lastw:
                add(self.lastw[k], "waw")
            for sv in self.readers.get(k, {}).values():
                add(sv, "war")
        if use_pending and self.pending[eng]:
            for (sem, val) in self.pending[eng]:
                nm = sem.name
                if nm not in need or need[nm][1] < val:
                    need[nm] = (sem, val)
            self.pending[eng] = []
        out = []
        kn = self.known[eng]
        for nm, (sem, val) in need.items():
            if kn.get(nm, 0) < val:
                kn[nm] = val
                out.append((sem, val))
        return out

    def op(self, eng, fn, reads=(), writes=(), inc=True, persistent=False):
        waits = self._need(eng, reads, writes, use_pending=not persistent)
        if eng == "pool" and self.ecnt[eng] > 0:
            nm = self.esem[eng].name
            if self.known[eng].get(nm, 0) < self.ecnt[eng]:
                self.known[eng][nm] = self.ecnt[eng]
                waits.append((self.esem[eng], self.ecnt[eng]))
        if inc:
            self.ecnt[eng] += 1
            val = self.ecnt[eng]
            self.open[eng] = False
        else:
            val = self.ecnt[eng] + 1
            self.open[eng] = True
        sem = self.esem[eng]
        self.q[eng].append((waits, fn, sem, 1 if inc else 0))
        sv = (sem, val, eng)
        for k in writes:
            self.lastw[k] = sv
            self.readers[k] = {}
        for k in reads:
            self.readers.setdefault(k, {})[eng] = sv

    def dma(self, qeng, out, in_, reads=(), writes=(), semkey=None, is_output=False, persistent=False, **kw):
        waits = self._need(qeng, reads, writes, use_pending=not persistent)
        sk = semkey if semkey is not None else (tuple(writes) + tuple(reads))
        if sk not in self.dsem:
            self.dsem[sk] = self.nc.alloc_semaphore(name=f"d{len(self.dsem)}")
            self.dcnt[sk] = 0
        if persistent:
            self.dpersist.add(sk)
        sem = self.dsem[sk]
        self.dcnt[sk] += 16
        val = self.dcnt[sk]
        if is_output:
            self.out_sems.add(sk)

        def fn(e, out=out, in_=in_, kw=kw):
            return e.dma_start(out=out, in_=in_, **kw)

        self.q[qeng].append((waits, fn, sem, 16))
        sv = (sem, val, "dma:" + str(sk))
        for k in writes:
            self.lastw[k] = sv
            self.readers[k] = {}
        for k in reads:
            self.readers.setdefault(k, {})["dma:" + str(sk)] = sv

    def barrier(self):
        tg = [(self.esem[e], self.ecnt[e]) for e in ("pe", "act", "dve") if self.ecnt[e] > 0]
        tg += [(self.dsem[k], self.dcnt[k]) for k in self.dsem if k not in self.dpersist]
        tg += [(self.esem["pool"], self.ecnt["pool"])] if self.ecnt["pool"] > 0 else []
        for e in ("pe", "act", "dve", "sp", "pool"):
            self.pending[e] = list(tg)

    def finish(self):
        nc = self.nc
        for e in self.ENG:
            assert not self.open[e], e
        fin = [(self.dsem[sk], self.dcnt[sk]) for sk in self.out_sems]
        with nc.Block() as block:
            def emit(name):
                def body(e):
                    for waits, fn, sem, inc in self.q[name]:
                        for (s, v) in waits:
                            e.wait_ge(s, v)
                        inst = fn(e)
                        if inc:
                            inst.then_inc(sem, inc)
                    if name == "sp":
                        for (s, v) in fin:
                            e.wait_ge(s, v)
                return body
            block.tensor(emit("pe"))
            block.scalar(emit("act"))
            block.vector(emit("dve"))
            block.gpsimd(emit("pool"))
            block.sync(emit("sp"))


class B:
    def __init__(self, dbg=None, stages=99, plan=None):
        self.plan = plan
        self.wrec = []
        self.wissued = 0
        self.dbg = dbg or []
        self.stages = stages
        nc = self.nc = bass.Bass("TRN2", target_bir_lowering=False)
        self.S = Sched(nc)
        self.din = {}
        self.dout = {}
        self.psn = 0
        self.ps = [nc.alloc_psum_tensor(f"psb{i}", [128, 512], F32) for i in range(8)]
        self.wn = 0
        self.wslot = [nc.alloc_sbuf_tensor(f"wslot{i}", [128, 2048], BF16) for i in range(self.NSLOT)]

    def inp(self, name, shape):
        t = self.nc.dram_tensor(name, list(shape), F32, kind="ExternalInput").ap()
        self.din[name] = t
        return t

    def outp(self, name, shape, dt=F32):
        t = self.nc.dram_tensor(name, list(shape), dt, kind="ExternalOutput").ap()
        self.dout[name] = t
        return t

    def sb(self, name, shape, dt=F32):
        return self.nc.alloc_sbuf_tensor("s_" + name, list(shape), dt)

    def tmps(self, *specs):
        import contextlib

        @contextlib.contextmanager
        def cm():
            with contextlib.ExitStack() as st:
                yield [st.enter_context(self.tmp(*sp)) for sp in specs]
        return cm()

    def tmp(self, name, shape, dt=F32):
        self.tn = getattr(self, "tn", 0) + 1
        return self.nc.sbuf_tensor(f"t_{name}_{self.tn}", list(shape), dt)

    def PS(self):
        b = self.psn % 8
        self.psn += 1
        return self.ps[b], ("ps", b)

    def MM(self, out, lhsT, rhs, start=True, stop=True, r=(), w=(), inc=True):
        self.S.op("pe", lambda e: e.matmul(out, lhsT=lhsT, rhs=rhs, start=start, stop=stop), reads=r, writes=w, inc=inc)

    def TR(self, out, in_, ident, r=(), w=(), inc=True):
        self.S.op("pe", lambda e: e.transpose(out=out, in_=in_, identity=ident), reads=r, writes=w, inc=inc)

    def ACT(self, out, in_, func, r=(), w=(), **kw):
        self.S.op("act", lambda e: e.activation(out=out, in_=in_, func=func, **kw), reads=r, writes=w)

    def TT(self, eng, out, in0, in1, op, r=(), w=()):
        self.S.op(eng, lambda e: e.tensor_tensor(out=out, in0=in0, in1=in1, op=op), reads=r, writes=w)

    def TS(self, eng, out, in0, s1, s2, op0, op1=None, r=(), w=()):
        if op1 is None:
            self.S.op(eng, lambda e: e.tensor_scalar(out=out, in0=in0, scalar1=s1, scalar2=None, op0=op0), reads=r, writes=w)
        else:
            self.S.op(eng, lambda e: e.tensor_scalar(out=out, in0=in0, scalar1=s1, scalar2=s2, op0=op0, op1=op1), reads=r, writes=w)

    def STT(self, out, in0, scalar, in1, op0, op1, r=(), w=()):
        self.S.op("dve", lambda e: e.scalar_tensor_tensor(out=out, in0=in0, scalar=scalar, in1=in1, op0=op0, op1=op1), reads=r, writes=w)

    def CP(self, eng, out, in_, r=(), w=()):
        if eng == "act":
            self.S.op("act", lambda e: e.copy(out=out, in_=in_), reads=r, writes=w)
        else:
            self.S.op(eng, lambda e: e.tensor_copy(out=out, in_=in_), reads=r, writes=w)

    def RECIP(self, out, in_, r=(), w=()):
        self.S.op("dve", lambda e: e.reciprocal(out=out, in_=in_), reads=r, writes=w)

    def MEMSET(self, eng, ap, val, w=()):
        self.S.op(eng, lambda e: e.memset(ap, val), writes=w)

    def LOAD(self, out, in_, w, **kw):
        self.S.dma("sp", out, in_, writes=w, **kw)

    def STORE(self, out, in_, r, key=None):
        self.S.dma("sp", out, in_, reads=r, is_output=True, semkey=key)

    def dump(self, name, ap, shape, r):
        if name in self.dbg:
            o = self.outp("dbg_" + name, shape, ap.dtype)
            self.S.dma("sp", o, ap, reads=r, is_output=True, semkey=("dbg", name))

    NSLOT, LA = 4, 2

    def _wissue(self, j):
        name, k0, kt, c0, nco = self.plan[j]
        w3 = self.din[name].rearrange("(kt p) c -> p kt c", p=128)[:, k0:k0 + kt, c0:c0 + nco]
        sl = j % self.NSLOT
        slot = self.wslot[sl][:, 0:kt * nco].rearrange("p (k c) -> p k c", k=kt)
        self.S.dma("pool", slot, w3, writes=[("wslot", sl)], persistent=True)

    def wload(self, desc):
        name, k0, kt, c0, nco = desc
        assert kt * nco <= 2048
        i = self.wn
        self.wn += 1
        self.wrec.append(desc)
        if self.plan is None:
            self.plan_tmp = getattr(self, "plan_tmp", [])
            self.plan_tmp.append(desc)
            plan_saved, self.plan = self.plan, self.plan_tmp
            self._wissue(i)
            self.plan = plan_saved
        else:
            assert self.plan[i] == desc, (i, desc, self.plan[i])
            while self.wissued <= min(i + self.LA, len(self.plan) - 1):
                self._wissue(self.wissued)
                self.wissued += 1
        sl = i % self.NSLOT
        slot = self.wslot[sl][:, 0:kt * nco].rearrange("p (k c) -> p k c", k=kt)
        return slot, [("wslot", sl)]

    @staticmethod
    def wview(W, k0, nk, c0, nco):
        return (W.name, k0, nk, c0, nco)

    def proj_fm(self, *a, **kw):
        for _ in self.proj_fm_it(*a, **kw):
            pass

    def proj_fm_it(self, W, c0, ntile, inT, in_keys, nk, evac, tblocks=(0, 1)):
        per = max(1, 2048 // (nk * 128))
        m = 0
        while m < ntile:
            n = min(per, ntile - m)
            slot, wk = self.wload(self.wview(W, 0, nk, c0 + m * 128, n * 128))
            for j in range(n):
                for tb in tblocks:
                    ps, pk = self.PS()
                    for kt in range(nk):
                        self.MM(ps[:, :], slot[:, kt, j * 128:(j + 1) * 128], inT[:, kt, tb * 512:(tb + 1) * 512],
                                start=(kt == 0), stop=(kt == nk - 1), r=wk + list(in_keys), w=[pk], inc=(kt == nk - 1))
                    evac(m + j, tb, ps[:, :], pk)
                    yield
            m += n

    def proj_tm(self, W, c0, nco, inT, in_keys, evac, nk=8):
        assert nk * nco <= 2048
        slot, wk = self.wload(self.wview(W, 0, nk, c0, nco))
        for tt in range(NT // 128):
            ps, pk = self.PS()
            for kt in range(nk):
                self.MM(ps[:, 0:nco], inT[:, kt, tt * 128:(tt + 1) * 128], slot[:, kt, :],
                        start=(kt == 0), stop=(kt == nk - 1), r=wk + list(in_keys), w=[pk], inc=(kt == nk - 1))
            evac(tt, ps[:, 0:nco], pk)


ALL_STAGES = ("l0", "ffn0", "l1", "ffn1")


def build(dbg=None, stages=ALL_STAGES, plan=None):
    if plan is None:
        plan = build(dbg=dbg, stages=stages, plan=[]).wrec
        return build(dbg=dbg, stages=stages, plan=plan)
    b = B(dbg, stages, plan if plan else None)
    nc, S = b.nc, b.S
    I = {}
    for nm, shp in [("xg", (2, NT, D)), ("condT", (128, 8, 2)), ("ident", (128, 128)),
                    ("l0_norm1", (128, 8)), ("l0_norm2", (128, 8)), ("l1_norm1", (128, 8)), ("l1_norm2", (128, 8)),
                    ("final_norm", (128, 8)), ("l0_mod_b", (128, 48)), ("l1_mod_b", (128, 48)),
                    ("l0_mod_w", (D, 6 * D)), ("l1_mod_w", (D, 6 * D)),
                    ("l0_w_in", (D, L0_IN)), ("l0_w_out", (1536, D)), ("l1_w_in", (D, L1_IN)), ("l1_w_out", (2048, D)),
                    ("l0_ffn_w1", (D, FFN_H)), ("l0_ffn_w3", (D, FFN_H)), ("l0_ffn_w2", (FFN_H, D)),
                    ("l1_ffn_w1", (D, FFN_H)), ("l1_ffn_w3", (D, FFN_H)), ("l1_ffn_w2", (FFN_H, D)),
                    ("k_ctx", (256, 512)), ("v_ctx", (256, 512)), ("na_biasG", (128, 8, 16, 64)), ("na_mask", (128, 1024)),
                    ("ssd_state", (2, 16, 128, 64)), ("tri", (128, 5, 128)), ("l0_conv_w", (128, 12, 5)), ("l0_conv_b", (128, 12)),
                    ("l0_dtbias", (1, 32)), ("l0_alog", (1, 32)), ("l0_dskip", (128, 8)), ("l0_ssdnw", (128, 8)),
                    ("ret_decay_b", (1, 8)), ("ret_idx", (128, 4, 128)), ("ret_row", (128, 2, 128)), ("ret_col", (128, 2)),
                    ("ret_state", (2, 4, 256, 512)), ("l1_ret_norm", (128, 16)), ("rope_cos", (128, 2, NT)),
                    ("rope_sin", (128, 2, NT)), ("permP", (128, 128))]:
        I[nm] = b.inp(nm, shp)
    y_out = b.outp("y", (2, NT, D))
    new_ret = b.outp("new_ret", (4, 2, 4, 256, 512))
    new_k = b.outp("new_k", (NT, 512))
    new_v = b.outp("new_v", (NT, 512))
    new_ssd = b.outp("new_ssd", (4, 2, 16, 128, 64))

    ident = b.sb("c_ident", [128, 128], F32)
    identb = b.sb("c_identb", [128, 128], BF16)
    onesb = b.sb("c_onesb", [128, 128], BF16)
    b.LOAD(ident[:, :], I["ident"][:, :], w=["ident"])
    b.CP("dve", identb[:, :], ident[:, :], r=["ident"], w=["identb"])
    b.MEMSET("dve", onesb[:, :], 1.0, w=["onesb"])
    vecs = {}
    for nm, n in [("l0_norm1", 8), ("l0_norm2", 8), ("l1_norm1", 8), ("l1_norm2", 8), ("final_norm", 8),
                  ("l0_mod_b", 48), ("l1_mod_b", 48)]:
        vecs[nm] = b.sb("v_" + nm, [128, n], F32)
        b.LOAD(vecs[nm][:, :], I[nm][:, :], w=["v_" + nm])

    condT = b.sb("condT", [128, 8, 2], F32)
    scb = b.sb("scb", [128, 8, 2], BF16)
    b.LOAD(condT[:, :, :], I["condT"][:, :, :], w=["condT"])
    b.ACT(scb[:, :, :], condT[:, :, :], AF.Silu, r=["condT"], w=["scb"])
    mod = [b.sb(f"mod{l}", [128, 48, 2], F32) for l in range(2)]
    modA = [[[b.sb(f"modA{l}{g}{j}", [128, 8], F32) for j in range(2)] for g in range(2)] for l in range(2)]
    def adaln_it(l):
        W = I[f"l{l}_mod_w"]

        def mk_modA(j):
            sc_i, nw = [(1, f"l{l}_norm1"), (4, f"l{l}_norm2")][j]
            for g in range(2):
                b.S.op("dve", lambda e, o=modA[l][g][j][:, :], i0=mod[l][:, sc_i * 8:(sc_i + 1) * 8, g], i1=vecs[nw][:, :]:
                       e.scalar_tensor_tensor(out=o, in0=i0, scalar=1.0, in1=i1, op0=ALU.add, op1=ALU.mult),
                       reads=[f"mod{l}", "v_" + nw], writes=[f"modA{l}{g}{j}"])
        for c in range(24):
            slot, wk = b.wload(b.wview(W, 0, 8, c * 256, 256))
            for j in range(2):
                ft = c * 2 + j
                ps, pk = b.PS()
                for kt in range(8):
                    b.MM(ps[:, 0:2], slot[:, kt, j * 128:(j + 1) * 128], scb[:, kt, :], start=(kt == 0), stop=(kt == 7),
                         r=wk + ["scb"], w=[pk], inc=(kt == 7))
                b.TS("dve", mod[l][:, ft, :], ps[:, 0:2], vecs[f"l{l}_mod_b"][:, ft:ft + 1], None, ALU.add,
                     r=[pk, f"v_l{l}_mod_b"], w=[f"mod{l}"])
            if c == 7:
                mk_modA(0)
            if c == 23:
                mk_modA(1)
            yield

    def modv(l, g, idx):
        return mod[l][:, idx * 8:(idx + 1) * 8, g]

    xT = b.sb("xT", [128, 8, NT], F32)

    def load_x(g):
        with b.tmp("xtok", [128, 2, D], F32) as xtok:
            for tt in range(8):
                bi = tt % 2
                b.LOAD(xtok[:, bi, :], I["xg"][g, tt * 128:(tt + 1) * 128, :], w=[("xtok", bi)])
                for half in range(2):
                    ps, pk = b.PS()
                    for q in range(4):
                        kt = half * 4 + q
                        b.TR(ps[:, q * 128:(q + 1) * 128], xtok[:, bi, kt * 128:(kt + 1) * 128], ident[:, :],
                             r=[("xtok", bi), "ident"], w=[pk], inc=(q == 3))
                    eng = "dve" if half == 0 else "act"
                    b.CP(eng, xT[:, half * 4:(half + 1) * 4, tt * 128:(tt + 1) * 128],
                         ps[:, :].rearrange("p (q t) -> p q t", q=4), r=[pk], w=[("xT", tt)])
            S.barrier()

    XK = [("xT", tt) for tt in range(8)]

    def rstd_block(src, src_keys, tb, nkt, rs, rs_key, inv_n):
        ps, pk = b.PS()
        with b.tmp("sqt", [128, 2, 512], BF16) as sq:
            for kt in range(nkt):
                bi = kt % 2
                b.ACT(sq[:, bi, :], src[:, kt, tb * 512:(tb + 1) * 512], AF.Square, r=src_keys, w=[("sq", bi)])
                b.MM(ps[:, :], onesb[:, :], sq[:, bi, :], start=(kt == 0), stop=(kt == nkt - 1),
                     r=[("sq", bi), "onesb"], w=[pk], inc=True)
        b.ACT(rs, ps[:, :], AF.Sqrt, r=[pk], w=[rs_key], scale=inv_n, bias=EPS)
        b.RECIP(rs, rs, r=[rs_key], w=[rs_key])

    def norm_mod(l, g, j, hT, hkey):
        A = modA[l][g][j]
        Bv = modv(l, g, 0 if j == 0 else 3)
        with b.tmp("rs", [128, 512], F32) as rs, b.tmp("ntmp", [128, 2, 512], F32) as tmp:
            for tb in range(2):
                rstd_block(xT, XK, tb, 8, rs[:, :], ("rs", tb), 1.0 / D)
                for kt in range(8):
                    bi = kt % 2
                    b.TT("dve", tmp[:, bi, :], xT[:, kt, tb * 512:(tb + 1) * 512], rs[:, :], ALU.mult,
                         r=XK + [("rs", tb)], w=[("ntmp", bi)])
                    b.ACT(hT[:, kt, tb * 512:(tb + 1) * 512], tmp[:, bi, :], AF.Identity,
                          r=[("ntmp", bi), f"modA{l}{g}{j}", f"mod{l}"], w=[hkey], scale=A[:, kt:kt + 1], bias=Bv[:, kt:kt + 1])
        S.barrier()

    def resid_evac(l, g, gi):
        gate = modv(l, g, gi)

        def ev(m, tb, ps, pk):
            b.STT(xT[:, m, tb * 512:(tb + 1) * 512], ps, gate[:, m:m + 1], xT[:, m, tb * 512:(tb + 1) * 512],
                  ALU.mult, ALU.add, r=[pk, f"mod{l}"] + XK, w=XK)
        return ev

    def ffn(l, g, other=None):
        def tick():
            if other is not None:
                try:
                    next(other)
                except StopIteration:
                    pass
        with b.tmp("h2T", [128, 8, NT], BF16) as h2T, b.tmp("gT", [128, 22, NT], BF16) as gT, \
                b.tmp("s1", [128, 2, 512], F32) as s1:
            norm_mod(l, g, 1, h2T, "h2T")
            W1, W3, W2 = I[f"l{l}_ffn_w1"], I[f"l{l}_ffn_w3"], I[f"l{l}_ffn_w2"]
            cnt = [0]
            for c in range(11):
                s_1, k1 = b.wload(b.wview(W1, 0, 8, c * 256, 256))
                s_3, k3 = b.wload(b.wview(W3, 0, 8, c * 256, 256))
                for j in range(2):
                    m = c * 2 + j
                    for tb in range(2):
                        p1, pk1 = b.PS()
                        p3, pk3 = b.PS()
                        for kt in range(8):
                            b.MM(p1[:, :], s_1[:, kt, j * 128:(j + 1) * 128], h2T[:, kt, tb * 512:(tb + 1) * 512],
                                 start=(kt == 0), stop=(kt == 7), r=k1 + ["h2T"], w=[pk1], inc=(kt == 7))
                        for kt in range(8):
                            b.MM(p3[:, :], s_3[:, kt, j * 128:(j + 1) * 128], h2T[:, kt, tb * 512:(tb + 1) * 512],
                                 start=(kt == 0), stop=(kt == 7), r=k3 + ["h2T"], w=[pk3], inc=(kt == 7))
                        bi = cnt[0] % 2
                        cnt[0] += 1
                        b.ACT(s1[:, bi, :], p1[:, :], AF.Silu, r=[pk1], w=[("s1", bi)])
                        b.TT("dve", gT[:, m, tb * 512:(tb + 1) * 512], s1[:, bi, :], p3[:, :], ALU.mult,
                             r=[("s1", bi), pk3], w=[("gT", m)])
                tick()
                tick()
            ev = resid_evac(l, g, 5)
            GK = [("gT", m) for m in range(22)]
            for mo in range(8):
                sa, ka = b.wload(b.wview(W2, 0, 11, mo * 128, 128))
                sb_, kb = b.wload(b.wview(W2, 11, 11, mo * 128, 128))
                for tb in range(2):
                    ps, pk = b.PS()
                    for kt in range(22):
                        sl, kk = (sa, ka) if kt < 11 else (sb_, kb)
                        b.MM(ps[:, :], sl[:, kt % 11, :], gT[:, kt, tb * 512:(tb + 1) * 512], start=(kt == 0), stop=(kt == 21),
                             r=kk + GK, w=[pk], inc=(kt == 21))
                    ev(mo, tb, ps[:, :], pk)
                tick()
            if other is not None:
                for _ in other:
                    pass
            S.barrier()

    def final_out(g, gnext=None):
        fw = vecs["final_norm"]
        with b.tmp("rs", [128, 512], F32) as rs, b.tmp("ytmp", [128, 2, 512], F32) as tmp, \
                b.tmp("ytok", [128, 2, D], F32) as ytok, b.tmp("xtokn", [128, 2, D if gnext is not None else 2], F32) as xtok:
            for tb in range(2):
                rstd_block(xT, XK, tb, 8, rs[:, :], ("rs", tb), 1.0 / D)
                for kt in range(8):
                    b.STT(xT[:, kt, tb * 512:(tb + 1) * 512], xT[:, kt, tb * 512:(tb + 1) * 512], fw[:, kt:kt + 1], rs[:, :],
                          ALU.mult, ALU.mult, r=XK + ["v_final_norm", ("rs", tb)], w=XK)
            for tt in range(8):
                bi = tt % 2
                for half in range(2):
                    ps, pk = b.PS()
                    for q in range(4):
                        kt = half * 4 + q
                        b.TR(ps[:, q * 128:(q + 1) * 128], xT[:, kt, tt * 128:(tt + 1) * 128], ident[:, :],
                             r=[("xT", tt), "ident"], w=[pk], inc=(q == 3))
                    eng = "dve" if half == 0 else "act"
                    b.CP(eng, ytok[:, bi, half * 512:(half + 1) * 512], ps[:, :], r=[pk], w=[("ytok", bi, half)])
                b.STORE(y_out[g, tt * 128:(tt + 1) * 128, :], ytok[:, bi, :], r=[("ytok", bi, 0), ("ytok", bi, 1)], key=("ytok", bi))
                if gnext is not None:
                    if tt == 0:
                        b.LOAD(xtok[:, 0, :], I["xg"][gnext, 0:128, :], w=[("xtokn", 0)])
                    if tt + 1 < 8:
                        b.LOAD(xtok[:, (tt + 1) % 2, :], I["xg"][gnext, (tt + 1) * 128:(tt + 2) * 128, :], w=[("xtokn", (tt + 1) % 2)])
                    for half in range(2):
                        ps, pk = b.PS()
                        for q in range(4):
                            kt = half * 4 + q
                            b.TR(ps[:, q * 128:(q + 1) * 128], xtok[:, bi, kt * 128:(kt + 1) * 128], ident[:, :],
                                 r=[("xtokn", bi), "ident"], w=[pk], inc=(q == 3))
                        eng = "act" if half == 0 else "dve"
                        b.CP(eng, xT[:, half * 4:(half + 1) * 4, tt * 128:(tt + 1) * 128],
                             ps[:, :].rearrange("p (q t) -> p q t", q=4), r=[pk], w=[("xT", tt)])
            S.barrier()

    l0c = {}
    if "l0" in stages:
        l0c["tri"] = b.sb("tri", [128, 5, 128], F32)
        l0c["trib"] = b.sb("trib", [128, 5, 128], BF16)
        l0c["conv_w"] = b.sb("convw", [128, 12, 5], F32)
        l0c["conv_b"] = b.sb("convb", [128, 12], F32)
        l0c["dtbias"] = b.sb("dtbias", [128, 32], F32)
        l0c["negA"] = b.sb("negA", [128, 32], F32)
        l0c["dskip"] = b.sb("dskip", [128, 8], F32)
        l0c["ssdnw"] = b.sb("ssdnw", [128, 8], F32)
        b.LOAD(l0c["tri"][:, :, :], I["tri"][:, :, :], w=["tri"])
        b.CP("dve", l0c["trib"][:, :, :], l0c["tri"][:, :, :], r=["tri"], w=["tri"])
        l0c["trir"] = b.sb("trir", [128, 5, 128], mybir.dt.float32r)
        b.CP("dve", l0c["trir"][:, :, :], l0c["tri"][:, :, :], r=["tri"], w=["tri"])
        b.LOAD(l0c["conv_w"][:, :, :], I["l0_conv_w"][:, :, :], w=["l0c"], semkey="l0c1")
        b.LOAD(l0c["conv_b"][:, :], I["l0_conv_b"][:, :], w=["l0c"], semkey="l0c2")
        b.LOAD(l0c["dtbias"][:, :], I["l0_dtbias"][0:1, :].partition_broadcast(128), w=["l0c"], semkey="l0c3")
        b.LOAD(l0c["negA"][:, :], I["l0_alog"][0:1, :].partition_broadcast(128), w=["l0c"], semkey="l0c4")
        b.LOAD(l0c["dskip"][:, :], I["l0_dskip"][:, :], w=["l0c"], semkey="l0c5")
        b.LOAD(l0c["ssdnw"][:, :], I["l0_ssdnw"][:, :], w=["l0c"], semkey="l0c6")
        b.ACT(l0c["negA"][:, :], l0c["negA"][:, :], AF.Exp, r=["l0c"], w=["l0c"])
        b.TS("dve", l0c["negA"][:, :], l0c["negA"][:, :], -1.0, None, ALU.mult, r=["l0c"], w=["l0c"])

    ret = {}
    if "l1" in stages:
        lgb = b.sb("lgb", [128, 8], F32)
        ridx = b.sb("ridx", [128, 4, 128], F32)
        rrow = b.sb("rrow", [128, 2, 128], F32)
        rcol = b.sb("rcol", [128, 2], F32)
        ret["normw"] = b.sb("retnw", [128, 16], F32)
        ret["permP"] = b.sb("permP", [128, 128], F32)
        ret["dtot"] = b.sb("dtot", [128, 4, 128], BF16)
        ret["ecs"] = b.sb("ecs", [128, 8, 128], BF16)
        ret["tail"] = b.sb("rtail", [128, 8], F32)
        ret["etot"] = b.sb("retot", [128, 8], F32)
        b.LOAD(lgb[:, :], I["ret_decay_b"][0:1, :].partition_broadcast(128), w=["lgb"])
        b.LOAD(ridx[:, :, :], I["ret_idx"][:, :, :], w=["ridx"])
        b.LOAD(rrow[:, :, :], I["ret_row"][:, :, :], w=["rrow"])
        b.LOAD(rcol[:, :], I["ret_col"][:, :], w=["rcol"])
        b.LOAD(ret["normw"][:, :], I["l1_ret_norm"][:, :], w=["retnw"])
        b.LOAD(ret["permP"][:, :], I["permP"][:, :], w=["permP"])
        ret["permPr"] = b.sb("permPr", [128, 128], mybir.dt.float32r)
        b.CP("dve", ret["permPr"][:, :], ret["permP"][:, :], r=["permP"], w=["permP"])
        b.ACT(lgb[:, :], lgb[:, :], AF.Exp, r=["lgb"], w=["lgb"], scale=-1.0)
        b.ACT(lgb[:, :], lgb[:, :], AF.Ln, r=["lgb"], w=["lgb"], bias=1.0)
        b.TS("dve", lgb[:, :], lgb[:, :], -1.0, None, ALU.mult, r=["lgb"], w=["lgb"])
        with b.tmp("rtmp", [128, 2, 128], F32) as rtmp:
            for hd in range(4):
                for d in range(2):
                    b.ACT(rtmp[:, d, :], ridx[:, d, :], AF.Exp, r=["ridx", "lgb"], w=[("rtmp", d)], scale=lgb[:, d * 4 + hd:d * 4 + hd + 1])
                    b.TT("dve", rtmp[:, d, :], rtmp[:, d, :], ridx[:, 2 + d, :], ALU.mult, r=[("rtmp", d), "ridx"], w=[("rtmp", d)])
                    b.ACT(ret["ecs"][:, d * 4 + hd, :], rrow[:, d, :], AF.Exp, r=["rrow", "lgb"], w=["retecs"], scale=lgb[:, d * 4 + hd:d * 4 + hd + 1])
                b.TT("dve", ret["dtot"][:, hd, :], rtmp[:, 0, :], rtmp[:, 1, :], ALU.add, r=[("rtmp", 0), ("rtmp", 1)], w=["retdtot"])
            for d in range(2):
                b.ACT(ret["tail"][:, d * 4:(d + 1) * 4], lgb[:, d * 4:(d + 1) * 4], AF.Exp, r=["lgb", "rcol"], w=["rettail"], scale=rcol[:, d:d + 1])
            b.ACT(ret["etot"][:, :], lgb[:, :], AF.Exp, r=["lgb"], w=["retetot"], scale=128.0)
            S.barrier()

    b.env = dict(ret=ret, new_ret=new_ret, l0c=l0c, new_k=new_k, new_v=new_v, new_ssd=new_ssd, I=I, vecs=vecs, ident=ident, identb=identb, onesb=onesb, mod=mod, modv=modv, xT=xT, XK=XK,
                 rstd_block=rstd_block, norm_mod=norm_mod, resid_evac=resid_evac)

    groups = [0] if 'g0' in stages else [1] if 'g1' in stages else [0, 1]
    import itertools
    load_x(groups[0])
    ada0 = adaln_it(0)
    for _ in range(8):
        next(ada0)
    ada1 = itertools.chain(ada0, adaln_it(1))
    for gi, g in enumerate(groups):
        if "l0" in stages:
            layer0(b, g, other=ada1 if gi == 0 else None)
        if gi == 0:
            for _ in ada1:
                pass
        if "ffn0" in stages:
            ffn(0, g)
        if "l1" in stages:
            layer1(b, g)
        if "ffn1" in stages:
            ffn(1, g)
        final_out(g, groups[gi + 1] if gi + 1 < len(groups) else None)
    S.finish()
    return b


def layer0(b, g, other=None):
    nc, S, E = b.nc, b.S, b.env

    def tick():
        if other is not None:
            try:
                next(other)
            except StopIteration:
                pass
    I, xT, XK, onesb, identb, ident = E["I"], E["xT"], E["XK"], E["onesb"], E["identb"], E["ident"]
    is_s = (g == 1)
    W = I["l0_w_in"]
    Wo = I["l0_w_out"]
    C0 = E["l0c"]
    nseq, CS, L = (1, 8, 1024) if is_s else (4, 2, 256)
    gate = E["modv"](0, g, 2)
    cnt = [0]

    def nxt():
        cnt[0] += 1
        return cnt[0] % 2
    with b.tmp("mixT", [128, 12, NT], BF16) as mixT:
        if "nona" in b.stages or "nossd" in b.stages:
            b.MEMSET("dve", mixT[:, :, :], 0.0, w=["mixT"])
        with b.tmp("szT", [128, 8, NT], BF16) as szT, b.tmp("xcT", [128, 12, NT], BF16) as xcT, \
                b.tmp("dta", [128, 2, 8, 32], F32) as dta, b.tmp("teall", [128, 8, 64], F32) as teall, \
                b.tmp("ahl", [128, 8, 2, 32], BF16) as ahl:
            dt_tok, a_tok = dta[:, 0, :, :], dta[:, 1, :, :]
            with b.tmp("qT", [128, 4, NT], BF16) as qT, b.tmp("kT", [128, 4, NT], BF16) as kT, \
                    b.tmp("vtok", [128, 8, 512], BF16) as vtok, b.tmp("ostg", [128, 2, 256], F32) as ostg, \
                    b.tmp("Eb", [128, 2, 512], BF16) as Eb, b.tmp("rden", [128, 2, 256], F32) as rden:
                with b.tmp("hT", [128, 8, NT], BF16) as hT, b.tmp("raw", [128, 2, 1024 + 4 * nseq], BF16) as raw, b.tmp("DG", [128, 2, 5, 128], BF16) as DG, \
                        b.tmp("dtt", [128, 4, 32], F32) as dtt:
                    E["norm_mod"](0, g, 0, hT, "hT")

                    def cp_evac(dst, dkey):
                        def ev(m, tb, ps, pk):
                            b.CP("act" if (m + tb) % 2 else "dve", dst[:, m, tb * 512:(tb + 1) * 512], ps, r=[pk], w=[dkey])
                        return ev
                    if "noqk" not in b.stages:
                        b.proj_fm(W, 0, 4, hT, ["hT"], 8, cp_evac(qT, "qT"))
                        b.proj_fm(W, 512, 4, hT, ["hT"], 8, cp_evac(kT, "kT"))
                    for half in range(2 if "nov" not in b.stages else 0):
                        def v_evac(tt, ps, pk, half=half):
                            if is_s:
                                b.CP("act", vtok[:, tt, half * 256:(half + 1) * 256], ps, r=[pk], w=[("vtok", tt)])
                            else:
                                i = nxt()
                                b.CP("dve", ostg[:, i, :], ps, r=[pk], w=[("ostg", i)])
                                b.CP("act", vtok[:, tt, half * 256:(half + 1) * 256], ostg[:, i, :], r=[("ostg", i)], w=[("vtok", tt)])
                                b.STORE(E["new_v"][tt * 128:(tt + 1) * 128, half * 256:(half + 1) * 256], ostg[:, i, :], r=[("ostg", i)], key=("ostg", i))
                        b.proj_tm(W, 1024 + half * 256, 256, hT, ["hT"], v_evac)
                    if not is_s:
                        for half in range(2 if "nok" not in b.stages else 0):
                            def k_evac(tt, ps, pk, half=half):
                                i = nxt()
                                b.CP("dve", ostg[:, i, :], ps, r=[pk], w=[("ostg", i)])
                                b.STORE(E["new_k"][tt * 128:(tt + 1) * 128, half * 256:(half + 1) * 256], ostg[:, i, :], r=[("ostg", i)], key=("ostg", i))
                            b.proj_tm(W, 512 + half * 256, 256, hT, ["hT"], k_evac)

                    def z_evac(m, tb, ps, pk):
                        b.ACT(szT[:, m, tb * 512:(tb + 1) * 512], ps, AF.Silu, r=[pk], w=["szT"])
                    if "noz" not in b.stages:
                        b.proj_fm(W, 1536, 8, hT, ["hT"], 8, z_evac)
                    for i in range(2):
                        b.MEMSET("dve", raw[:, i, :], 0.0, w=[("raw", i)])
                    spb = nseq // 2 if nseq > 1 else 1

                    def x_evac(m, tb, ps, pk):
                        i = m % 2
                        r3 = raw[:, i, :].rearrange("p (s l) -> p s l", s=nseq)
                        if nseq == 1:
                            b.CP("act", raw[:, i, 2 + tb * 512:2 + (tb + 1) * 512], ps, r=[pk], w=[("raw", i)])
                        else:
                            b.CP("act", r3[:, tb * 2:(tb + 1) * 2, 2:2 + L], ps.rearrange("p (s l) -> p s l", s=2), r=[pk], w=[("raw", i)])
                        if tb == 1:
                            b.TT("dve", DG[:, i, :, :], identb[:, :].unsqueeze(1).to_broadcast([128, 5, 128]),
                                 C0["conv_w"][:, m, :].unsqueeze(2).to_broadcast([128, 5, 128]), ALU.mult, r=["identb", "l0c"], w=[("DG", i)])
                            for t2 in range(2):
                                pc, pkc = b.PS()
                                if nseq == 1:
                                    for k in range(5):
                                        b.MM(pc[:, :], DG[:, i, k, :], raw[:, i, t2 * 512 + k:t2 * 512 + k + 512], start=(k == 0), stop=(k == 4),
                                             r=[("raw", i), ("DG", i)], w=[pkc], inc=(k == 4))
                                else:
                                    for sq in range(2):
                                        for k in range(5):
                                            b.MM(pc[:, sq * 256:(sq + 1) * 256], DG[:, i, k, :], r3[:, t2 * 2 + sq, k:k + L], start=(k == 0), stop=(k == 4),
                                                 r=[("raw", i), ("DG", i)], w=[pkc], inc=(sq == 1 and k == 4))
                                b.ACT(xcT[:, m, t2 * 512:(t2 + 1) * 512], pc[:, :], AF.Silu, r=[pkc, "l0c"], w=["xcT"], bias=C0["conv_b"][:, m:m + 1])
                    if "nox" not in b.stages:
                        b.proj_fm(W, 2560, 12, hT, ["hT"], 8, x_evac)

                    def dt_evac(tt, ps, pk):
                        b.TT("dve", dt_tok[:, tt, :], ps, C0["dtbias"][:, :], ALU.add, r=[pk, "l0c"], w=["dt"])
                        u_, s_, s2, p_ = dtt[:, 0, :], dtt[:, 1, :], dtt[:, 2, :], dtt[:, 3, :]
                        K_ = ["dtt"]
                        b.ACT(u_, dt_tok[:, tt, :], AF.Exp, r=["dt"], w=K_)
                        b.TS("dve", s_, u_, 2.0, None, ALU.add, r=K_, w=K_)
                        b.RECIP(s_, s_, r=K_, w=K_)
                        b.TT("dve", s_, s_, u_, ALU.mult, r=K_, w=K_)
                        b.TT("dve", s2, s_, s_, ALU.mult, r=K_, w=K_)
                        b.TS("dve", p_, s2, 1.0 / 11.0, 1.0 / 9.0, ALU.mult, ALU.add, r=K_, w=K_)
                        for cf in (1.0 / 7.0, 1.0 / 5.0, 1.0 / 3.0, 1.0):
                            b.TT("dve", p_, p_, s2, ALU.mult, r=K_, w=K_)
                            b.TS("dve", p_, p_, cf, None, ALU.add, r=K_, w=K_)
                        b.TT("dve", p_, p_, s_, ALU.mult, r=K_, w=K_)
                        b.TS("dve", dt_tok[:, tt, :], p_, 2.0, None, ALU.mult, r=K_, w=["dt"])
                        b.TT("dve", a_tok[:, tt, :], dt_tok[:, tt, :], C0["negA"][:, :], ALU.mult, r=["dt", "l0c"], w=["dt"])
                        b.CP("dve", ahl[:, tt, 0, :], a_tok[:, tt, :], r=["dt"], w=["dt"])
                        b.TT("dve", ahl[:, tt, 1, :], a_tok[:, tt, :], ahl[:, tt, 0, :], ALU.subtract, r=["dt"], w=["dt"])
                    if "nodt" not in b.stages:
                        b.proj_tm(W, 4096, 32, hT, ["hT"], dt_evac)
                    b.dump(f"h{g}", hT[:, :, :], (128, 8, NT), ["hT"])
                    b.dump(f"xc{g}", xcT[:, :, :], (128, 12, NT), ["xcT"])
                    b.dump(f"sz{g}", szT[:, :, :], (128, 8, NT), ["szT"])
                    b.dump(f"dt{g}", dta[:, :, :, :], (128, 2, 8, 32), ["dt"])
                    S.barrier()
                if not is_s:
                    def ctx_s1(s, h, i):
                        hp, ht = (h % 2) * 64, h // 2
                        tq = slice(s * 256, (s + 1) * 256)
                        ps, pk = b.PS()
                        for c in range(2):
                            tkk = slice(s * 256 + c * 128, s * 256 + (c + 1) * 128)
                            b.MM(ps[:, c * 256:(c + 1) * 256], kT[hp:hp + 64, ht, tkk], qT[hp:hp + 64, ht, tq], r=["kT", "qT"], w=[pk], inc=(c == 1))
                        b.ACT(Eb[:, i, :], ps[:, :], AF.Exp, r=[pk], w=[("Eb", i)], scale=0.125)

                    def ctx_s2(s, h, i):
                        hp, ht = (h % 2) * 64, h // 2
                        tq = slice(s * 256, (s + 1) * 256)
                        po, pko = b.PS()
                        for c in range(2):
                            b.MM(po[hp:hp + 64, 0:256], vtok[:, s * 2 + c, h * 64:(h + 1) * 64], Eb[:, i, c * 256:(c + 1) * 256], start=(c == 0), stop=(c == 1),
                                 r=[("vtok", s * 2 + c), ("Eb", i)], w=[pko], inc=False)
                        for c in range(2):
                            b.MM(po[hp:hp + 64, 256:512], onesb[:, 0:64], Eb[:, i, c * 256:(c + 1) * 256], start=(c == 0), stop=(c == 1),
                                 r=["onesb", ("Eb", i)], w=[pko], inc=(c == 1))
                        b.ACT(rden[hp:hp + 64, i, :], po[hp:hp + 64, 256:512], AF.Ln, r=[pko], w=[("rden", i)])
                        b.ACT(rden[hp:hp + 64, i, :], rden[hp:hp + 64, i, :], AF.Exp, r=[("rden", i)], w=[("rden", i)], scale=-1.0)
                        b.TT("dve", mixT[hp:hp + 64, ht, tq], po[hp:hp + 64, 0:256], rden[hp:hp + 64, i, :], ALU.mult, r=[pko, ("rden", i)], w=["mixT"])
                    its = [(s, h) for s in range(4 if "nona" not in b.stages else 0) for h in range(8)]
                    for n in range(len(its) + 1):
                        if n < len(its):
                            ctx_s1(its[n][0], its[n][1], n % 2)
                        if n >= 1:
                            ctx_s2(its[n - 1][0], its[n - 1][1], (n - 1) % 2)
                        tick()
                else:
                    with b.tmp("kcT", [128, 4, 256], BF16) as kcT, b.tmp("vc", [128, 2, 512], BF16) as vc, \
                            b.tmp("expB", [128, 8, 16, 64], BF16) as expB:
                      with b.tmp("ctmp", [128, 2, 1024], F32) as ctmp, b.tmp("namask", [128, 1024], F32) as nmask:
                        for c in range(2):
                            b.LOAD(ctmp[:, 0, 0:512], I["k_ctx"][c * 128:(c + 1) * 128, :], w=[("ctmp", 0)])
                            b.LOAD(ctmp[:, 1, 0:512], I["v_ctx"][c * 128:(c + 1) * 128, :], w=[("ctmp", 1)])
                            b.CP("dve", vc[:, c, :], ctmp[:, 1, 0:512], r=[("ctmp", 1)], w=["vc"])
                            ps, pk = b.PS()
                            for m in range(4):
                                b.TR(ps[:, m * 128:(m + 1) * 128], ctmp[:, 0, m * 128:(m + 1) * 128], ident[:, :], r=[("ctmp", 0), "ident"], w=[pk], inc=(m == 3))
                            b.CP("act", kcT[:, :, c * 128:(c + 1) * 128], ps[:, :].rearrange("p (m t) -> p m t", m=4), r=[pk], w=["kcT"])
                        b.LOAD(nmask[:, :], I["na_mask"][:, :], w=["na_mask"])
                        for h in range(8):
                            i = h % 2
                            b.LOAD(ctmp[:, i, :], I["na_biasG"][:, h].rearrange("p d c -> p (d c)"), w=[("ctmp", i)])
                            b.ACT(ctmp[:, i, :], ctmp[:, i, :], AF.Exp, r=[("ctmp", i)], w=[("ctmp", i)])
                            b.TT("dve", expB[:, h, :, :].rearrange("p d c -> p (d c)"), ctmp[:, i, :], nmask[:, :], ALU.mult,
                                 r=[("ctmp", i), "na_mask"], w=["expB"])
                        S.barrier()
                      if True:
                        def lat_info(r_):
                            r0 = min(max(r_ - 4, 0), 8)
                            kcs = list(range(r0 // 2, (r0 + 7) // 2 + 1))
                            return r0, kcs, len(kcs), 2 * kcs[0] - r_ + 8

                        def lat_s1(r_, h, i):
                            r0, kcs, nl, d0 = lat_info(r_)
                            hp, ht = (h % 2) * 64, h // 2
                            tq = slice(r_ * 64, (r_ + 1) * 64)
                            ps, pk = b.PS()
                            n = nl + 2
                            for mi, kc in enumerate(kcs):
                                b.MM(ps[:, mi * 64:(mi + 1) * 64], kT[hp:hp + 64, ht, kc * 128:(kc + 1) * 128], qT[hp:hp + 64, ht, tq], r=["kT", "qT"], w=[pk], inc=False)
                            for cc in range(2):
                                b.MM(ps[:, (nl + cc) * 64:(nl + cc + 1) * 64], kcT[hp:hp + 64, ht, cc * 128:(cc + 1) * 128], qT[hp:hp + 64, ht, tq], r=["kcT", "qT"], w=[pk], inc=(cc == 1))
                            b.ACT(Eb[:, i, 0:n * 64], ps[:, 0:n * 64], AF.Exp, r=[pk], w=[("Eb", i)], scale=0.125)
                            b.TT("dve", Eb[:, i, 0:nl * 64].rearrange("p (m c) -> p m c", m=nl), Eb[:, i, 0:nl * 64].rearrange("p (m c) -> p m c", m=nl),
                                 expB[:, h, d0:d0 + 2 * nl - 1:2, :], ALU.mult, r=[("Eb", i), "expB"], w=[("Eb", i)])

                        def lat_s2(r_, h, i):
                            r0, kcs, nl, d0 = lat_info(r_)
                            hp, ht = (h % 2) * 64, h // 2
                            tq = slice(r_ * 64, (r_ + 1) * 64)
                            n = nl + 2
                            po, pko = b.PS()
                            for which in range(2):
                                for mi in range(n):
                                    if mi < nl:
                                        kc = kcs[mi]
                                        e0 = r0 <= 2 * kc <= r0 + 7
                                        e1 = r0 <= 2 * kc + 1 <= r0 + 7
                                        lo, hi = (0 if e0 else 64), (128 if e1 else 64)
                                        lhs = vtok[lo:hi, kc, h * 64:(h + 1) * 64] if which == 0 else onesb[lo:hi, 0:64]
                                        rk = ("vtok", kc)
                                    else:
                                        lo, hi = 0, 128
                                        lhs = vc[:, mi - nl, h * 64:(h + 1) * 64] if which == 0 else onesb[:, 0:64]
                                        rk = "vc"
                                    b.MM(po[hp:hp + 64, which * 64:(which + 1) * 64], lhs, Eb[lo:hi, i, mi * 64:(mi + 1) * 64], start=(mi == 0), stop=(mi == n - 1),
                                         r=[rk, "onesb", ("Eb", i)], w=[pko], inc=(which == 1 and mi == n - 1))
                            b.ACT(rden[hp:hp + 64, i, 0:64], po[hp:hp + 64, 64:128], AF.Ln, r=[pko], w=[("rden", i)])
                            b.ACT(rden[hp:hp + 64, i, 0:64], rden[hp:hp + 64, i, 0:64], AF.Exp, r=[("rden", i)], w=[("rden", i)], scale=-1.0)
                            b.TT("dve", mixT[hp:hp + 64, ht, tq], po[hp:hp + 64, 0:64], rden[hp:hp + 64, i, 0:64], ALU.mult, r=[pko, ("rden", i)], w=["mixT"])
                        its = [(r_, h) for r_ in range(16 if "nona" not in b.stages else 0) for h in range(8)]
                        for n_ in range(len(its) + 1):
                            if n_ < len(its):
                                lat_s1(its[n_][0], its[n_][1], n_ % 2)
                            if n_ >= 1:
                                lat_s2(its[n_ - 1][0], its[n_ - 1][1], (n_ - 1) % 2)
                            if n_ % 4 == 0:
                                tick()
                        S.barrier()
                S.barrier()
            b.dump(f"att{g}", mixT[:, 0:4, :], (128, 4, NT), ["mixT"])
            with b.tmps(("xtok", [128, 2, 1024], BF16), ("Btok", [128, 2, 256], BF16), ("Sst", [128, 2, 1024], F32),
                        ("Sfb", [128, 1024], BF16), ("Sbin", [128, CS, 1024], BF16), ("R1", [128, 2, 2, 512], mybir.dt.float32r),
                        ("DE", [128, 2, 2, 2, 512], BF16), ("MC", [128, 2, 2, 2, 512], BF16), ("vd", [128, 2, 2, 256], BF16),
                        ("vt", [128, 1, 1024], BF16), ("Gm", [128, 2, 2, 2, 128], BF16), ("yz", [128, 2, 512], F32),
                        ("sq", [128, 2, 512], BF16), ("rs", [128, 2, 128], F32), ("wdt", [128, 2, 32], F32),
                        ("stmp", [128, 1, 512], F32)) as (xtok2, Btok2, Sst, Sfb, Sbin, R1, DE, MC, vd, vtl, Gm, yz, sqb, rs, wdt, stmp):
                tri, trib = C0["tri"], C0["trib"]
                tokc = [0]

                def tok_tiles(c):
                    tokc[0] += 1
                    i = tokc[0] % 2
                    tk = slice(c * 128, (c + 1) * 128)
                    ps, pk = b.PS()
                    psb = ps[:, :].bitcast(BF16)
                    for m in range(8):
                        b.TR(psb[:, m * 128:(m + 1) * 128], xcT[:, m, tk], identb[:, :], r=["xcT", "identb"], w=[pk], inc=(m == 7))
                    b.CP("act" if c % 2 else "dve", xtok2[:, i, :], psb[:, :], r=[pk], w=[("xtok", i)])
                    ps, pk = b.PS()
                    psb = ps[:, :].bitcast(BF16)
                    for m in range(2):
                        b.TR(psb[:, m * 128:(m + 1) * 128], xcT[:, 8 + m, tk], identb[:, :], r=["xcT", "identb"], w=[pk], inc=(m == 1))
                    b.CP("dve" if c % 2 else "act", Btok2[:, i, :], psb[:, 0:256], r=[pk], w=[("Btok", i)])
                    return i
                for c in range(8 if "notails" not in b.stages else 0):
                    pt, pkt = b.PS()
                    for hl in range(2):
                        b.MM(pt[:, 0:16], trib[:, 3, :], ahl[:, c, hl, 0:16], start=(hl == 0), stop=(hl == 1), r=["dt", "tri"], w=[pkt], inc=False)
                    for hl in range(2):
                        b.MM(pt[:, 16:32], trib[:, 2, :], ahl[:, c, hl, 16:32], start=(hl == 0), stop=(hl == 1), r=["dt", "tri"], w=[pkt], inc=False)
                    for hl in range(2):
                        b.MM(pt[:, 32:64], trib[:, 4, :], ahl[:, c, hl, 0:32], start=(hl == 0), stop=(hl == 1), r=["dt", "tri"], w=[pkt], inc=(hl == 1))
                    b.ACT(teall[:, c, :], pt[:, 0:64], AF.Exp, r=[pkt], w=[("te", c)])

                def upd(d, c, bi):
                    i = nxt()
                    vi = 0
                    b.TT("dve", wdt[:, i, 0:16], dt_tok[:, c, d * 16:(d + 1) * 16], teall[:, c, d * 16:(d + 1) * 16], ALU.mult, r=["dt", ("te", c)], w=[("wdt", i)])
                    b.TT("dve", vtl[:, vi, :].rearrange("p (h v) -> p h v", h=16), xtok2[:, bi, :].rearrange("p (h v) -> p h v", h=16),
                         wdt[:, i, 0:16].unsqueeze(2).to_broadcast([128, 16, 64]), ALU.mult, r=[("xtok", bi), ("wdt", i)], w=[("vtl", vi)])
                    for grp in range(2):
                        gs = slice(grp * 512, (grp + 1) * 512)
                        ps, pk = b.PS()
                        b.MM(ps[:, :], Btok2[:, bi, grp * 128:(grp + 1) * 128], vtl[:, vi, gs], r=[("Btok", bi), ("vtl", vi)], w=[pk])
                        j = 0
                        b.TT("dve", stmp[:, j, :].rearrange("p (h v) -> p h v", h=8), Sst[:, d, gs].rearrange("p (h v) -> p h v", h=8),
                             teall[:, c, 32 + d * 16 + grp * 8:32 + d * 16 + grp * 8 + 8].unsqueeze(2).to_broadcast([128, 8, 64]), ALU.mult,
                             r=[("S", d), ("te", c)], w=[("stmp", j)])
                        b.TT("dve", Sst[:, d, gs], stmp[:, j, :], ps[:, :], ALU.add, r=[("stmp", j), pk], w=[("S", d)])
                for s in range(nseq if "nossd" not in b.stages else 0):
                    if is_s:
                        for d in range(2):
                            b.LOAD(Sst[:, d, :].rearrange("p (h v) -> p h v", h=16), I["ssd_state"][d].rearrange("h n v -> n h v"), w=[("S", d)])
                    else:
                        b.MEMSET("dve", Sst[:, 0, :], 0.0, w=[("S", 0)])
                        b.MEMSET("dve", Sst[:, 1, :], 0.0, w=[("S", 1)])
                    for cc in reversed(range(CS)):
                        c = s * CS + cc
                        b.CP("act", Sbin[:, cc, :], Sst[:, 1, :], r=[("S", 1)], w=[("Sbin", cc)])
                        bi = tok_tiles(c)
                        upd(1, c, bi)
                    if not is_s:
                        b.STORE(E["new_ssd"][s, 1].rearrange("h n v -> n h v"), Sst[:, 1, :].rearrange("p (h v) -> p h v", h=16), r=[("S", 1)], key=("So", 1))
                    info = {}

                    def stageA(cc, qd, par):
                        c = s * CS + cc
                        tk = slice(c * 128, (c + 1) * 128)
                        grp, cp = qd // 2, cc % 2
                        if qd == 0:
                            info[cc] = tok_tiles(c)
                            for g2 in range(2):
                                pg, pkg = b.PS()
                                b.MM(pg[:, 0:128], xcT[:, 8 + g2, tk], xcT[:, 10 + g2, tk], r=["xcT"], w=[pkg])
                                for d in range(2):
                                    b.TT("dve", Gm[:, cp, g2, d, :], pg[:, 0:128], trib[:, d, :], ALU.mult, r=[pkg, "tri"], w=[("Gm", cp, g2, d)])
                        bi = info[cc]
                        pds = {}
                        for d in range(2):
                            a4 = a_tok[:, c, d * 16 + qd * 4:d * 16 + qd * 4 + 4]
                            b.TT("dve", R1[:, par, d, :].rearrange("p (h t) -> p h t", h=4), a4.unsqueeze(2).to_broadcast([128, 4, 128]),
                                 tri[:, d:d + 1, :].to_broadcast([128, 4, 128]), ALU.mult, r=["dt", "tri"], w=[("R1", par, d)])
                            b.TT("dve", vd[:, par, d, :].rearrange("p (h v) -> p h v", h=4), xtok2[:, bi, qd * 256:(qd + 1) * 256].rearrange("p (h v) -> p h v", h=4),
                                 dt_tok[:, c, d * 16 + qd * 4:d * 16 + qd * 4 + 4].unsqueeze(2).to_broadcast([128, 4, 64]), ALU.mult,
                                 r=[("xtok", bi), "dt"], w=[("vd", par, d)])
                        for d in range(2):
                            pd, pkd = b.PS()
                            b.MM(pd[:, :], C0["trir"][:, 3 - d, :], R1[:, par, d, :], r=[("R1", par, d), "tri"], w=[pkd])
                            pc, pkc = b.PS()
                            b.MM(pc[:, :], C0["trir"][:, 4, :], R1[:, par, d, :], r=[("R1", par, d), "tri"], w=[pkc])
                            pds[d] = (pd, pkd, pc, pkc)
                        for d in range(2):
                            pd, pkd, pc, pkc = pds[d]
                            b.ACT(DE[:, par, d, 0, :], pd[:, :], AF.Exp, r=[pkd], w=[("DE", par, d, 0)])
                            b.ACT(DE[:, par, d, 1, :], pc[:, :], AF.Exp, r=[pkc], w=[("DE", par, d, 1)])

                    def stageB(cc, qd, par):
                        c = s * CS + cc
                        tk = slice(c * 128, (c + 1) * 128)
                        grp, cp, q2 = qd // 2, cc % 2, qd % 2
                        if qd == 0:
                            b.CP("act", Sfb[:, :], Sst[:, 0, :], r=[("S", 0)], w=["Sfb"])
                        for d in range(2):
                            b.TT("dve", MC[:, par, d, 0, :].rearrange("p (h t) -> p h t", h=4), DE[:, par, d, 0, :].rearrange("p (h t) -> p h t", h=4),
                                 Gm[:, cp, grp, d:d + 1, :].to_broadcast([128, 4, 128]), ALU.mult, r=[("DE", par, d, 0), ("Gm", cp, grp, d)], w=[("MC", par, d, 0)])
                            b.TT("dve", MC[:, par, d, 1, :].rearrange("p (h t) -> p h t", h=4), DE[:, par, d, 1, :].rearrange("p (h t) -> p h t", h=4),
                                 xcT[:, 10 + grp:11 + grp, tk].to_broadcast([128, 4, 128]), ALU.mult, r=[("DE", par, d, 1), "xcT"], w=[("MC", par, d, 1)])
                        py, pky = b.PS()
                        for hh in range(4):
                            h = qd * 4 + hh
                            hp = (h % 2) * 64
                            o = py[hp:hp + 64, (hh // 2) * 128:(hh // 2 + 1) * 128]
                            hs = slice(hh * 128, (hh + 1) * 128)
                            for d in range(2):
                                st = (Sfb[:, h * 64:(h + 1) * 64], "Sfb") if d == 0 else (Sbin[:, cc, h * 64:(h + 1) * 64], ("Sbin", cc))
                                b.MM(o, vd[:, par, d, hh * 64:(hh + 1) * 64], MC[:, par, d, 0, hs], start=(d == 0), stop=False,
                                     r=[("vd", par, d), ("MC", par, d, 0)], w=[pky], inc=False)
                                b.MM(o, st[0], MC[:, par, d, 1, hs], start=False, stop=(d == 1), r=[st[1], ("MC", par, d, 1)], w=[pky], inc=(d == 1))
                        for mm in range(2):
                            m = qd * 2 + mm
                            b.STT(yz[:, grp, (q2 * 2 + mm) * 128:(q2 * 2 + mm + 1) * 128], xcT[:, m, tk], C0["dskip"][:, m:m + 1], py[:, mm * 128:(mm + 1) * 128],
                                  ALU.mult, ALU.add, r=["xcT", "l0c", pky], w=[("yz", grp, q2)])
                        b.TT("dve", yz[:, grp, q2 * 256:(q2 + 1) * 256].rearrange("p (m t) -> p m t", m=2), yz[:, grp, q2 * 256:(q2 + 1) * 256].rearrange("p (m t) -> p m t", m=2),
                             szT[:, qd * 2:qd * 2 + 2, tk], ALU.mult, r=[("yz", grp, q2), "szT"], w=[("yz", grp, q2)])
                        if q2 == 1:
                            b.ACT(sqb[:, grp, :], yz[:, grp, :], AF.Square, r=[("yz", grp, 0), ("yz", grp, 1)], w=[("sqb", grp)])
                            pn, pkn = b.PS()
                            for t in range(4):
                                b.MM(pn[:, 0:128], onesb[:, :], sqb[:, grp, t * 128:(t + 1) * 128], start=(t == 0), stop=(t == 3), r=[("sqb", grp), "onesb"], w=[pkn], inc=(t == 3))
                            b.ACT(rs[:, grp, :], pn[:, 0:128], AF.Ln, r=[pkn], w=[("rs0", grp)], scale=1.0 / 512.0, bias=EPS)
                            b.ACT(rs[:, grp, :], rs[:, grp, :], AF.Exp, r=[("rs0", grp)], w=[("rs0", grp)], scale=-0.5)
                            for t in range(4):
                                m = grp * 4 + t
                                b.STT(mixT[:, 4 + m, tk], yz[:, grp, t * 128:(t + 1) * 128], C0["ssdnw"][:, m:m + 1], rs[:, grp, :], ALU.mult, ALU.mult,
                                      r=[("yz", grp, 0), ("yz", grp, 1), "l0c", ("rs0", grp)], w=["mixT"])
                        if qd == 3:
                            upd(0, c, info[cc])
                    units = [(cc, qd) for cc in range(CS) for qd in range(4)]
                    for n_ in range(len(units) + 1):
                        if n_ < len(units):
                            stageA(units[n_][0], units[n_][1], n_ % 2)
                        if n_ >= 1:
                            stageB(units[n_ - 1][0], units[n_ - 1][1], (n_ - 1) % 2)
                        tick()
                    if not is_s:
                        b.STORE(E["new_ssd"][s, 0].rearrange("h n v -> n h v"), Sst[:, 0, :].rearrange("p (h v) -> p h v", h=16), r=[("S", 0)], key=("So", 0))
                S.barrier()
            S.barrier()
        b.dump(f"mix{g}", mixT[:, :, :], (128, 12, NT), ["mixT"])
        for m in range(8):
            slot, wk = b.wload(b.wview(Wo, 0, 12, m * 128, 128))
            for tb in range(2):
                ps, pk = b.PS()
                for kt in range(12):
                    b.MM(ps[:, :], slot[:, kt, :], mixT[:, kt, tb * 512:(tb + 1) * 512], start=(kt == 0), stop=(kt == 11), r=wk + ["mixT"], w=[pk], inc=(kt == 11))
                b.STT(xT[:, m, tb * 512:(tb + 1) * 512], ps[:, :], gate[:, m:m + 1], xT[:, m, tb * 512:(tb + 1) * 512], ALU.mult, ALU.add,
                      r=[pk, "mod0"] + XK, w=XK)
    S.barrier()


def layer1(b, g):
    nc, S, E = b.nc, b.S, b.env
    I, xT, XK, onesb, identb = E["I"], E["xT"], E["XK"], E["onesb"], E["identb"]
    is_s = (g == 1)
    W = I["l1_w_in"]
    Wo = I["l1_w_out"]
    R = E["ret"]
    nseq, CS = (1, 8) if is_s else (4, 2)
    gate = E["modv"](1, g, 2)
    with b.tmp("hT", [128, 8, NT], BF16) as hT, b.tmp("ropec", [128, 2, NT if is_s else 2], F32) as ropec, \
            b.tmp("ropes", [128, 2, NT if is_s else 2], F32) as ropes:
        if is_s:
            R = dict(R)
            R["cos"], R["sin"] = ropec, ropes
            b.LOAD(ropec[:, :, :], I["rope_cos"][:, :, :], w=["rope"], semkey="ropec")
            b.LOAD(ropes[:, :, :], I["rope_sin"][:, :, :], w=["rope"], semkey="ropes")
        E["norm_mod"](1, g, 0, hT, "hT")
        for hd in range(4):
            with b.tmps(("qT", [128, 2, NT], BF16), ("kT", [128, 2, NT], BF16), ("vtok", [128, 8, 512], BF16),
                        ("sgT", [128, 4, NT], BF16), ("ktok", [128, 8, 256], BF16), ("ygT", [128, 4, NT], BF16),
                        ("Sst", [128, 2, 2, 512], F32), ("Sfin", [128, 8, 2, 512], BF16), ("Sbin", [128, 8, 2, 512], BF16),
                        ("rt", [128, 6, 512 if is_s else 2], F32), ("sm", [128, 3, 768], BF16), ("sq", [128, 3, 512], BF16),
                        ("rs", [128, 3, 128], F32), ("yt", [128, 3, 512], F32),
                        ("ktl", [128, 2, 256], BF16), ("rawr", [128, 2, 512 if is_s else 2], mybir.dt.float32r)) as (qT, kT, vtok, sgT, ktok, ygT, Sst, Sfin, Sbin, rt, sm, sqb, rs, yt, ktl, rawr):
                cnt = [0]

                def qk_evac(dst, dkey, scl):
                    def ev(m, tb, ps, pk):
                        sl = slice(tb * 512, (tb + 1) * 512)
                        if not is_s:
                            b.ACT(dst[:, m, sl], ps, AF.Copy, r=[pk], w=[dkey], scale=scl)
                            return
                        i = cnt[0] % 2
                        cnt[0] += 1
                        raw, t1, t2 = rawr[:, i, :], rt[:, i * 3 + 1, :], rt[:, i * 3 + 2, :]
                        b.ACT(raw, ps, AF.Copy, r=[pk], w=[("rt", i, 0)], scale=scl)
                        p2, pk2 = b.PS()
                        b.MM(p2[:, :], R["permPr"][:, :], raw, r=[("rt", i, 0), "permP"], w=[pk2])
                        b.TT("dve", t1, raw, R["cos"][:, m, sl], ALU.mult, r=[("rt", i, 0), "rope"], w=[("rt", i, 1)])
                        b.TT("dve", t2, p2[:, :], R["sin"][:, m, sl], ALU.mult, r=[pk2, "rope"], w=[("rt", i, 2)])
                        b.TT("dve", dst[:, m, sl], t1, t2, ALU.add, r=[("rt", i, 1), ("rt", i, 2)], w=[dkey])
                    return ev
                b.proj_fm(W, hd * 256, 2, hT, ["hT"], 8, qk_evac(qT, "qT", 1.0))
                b.proj_fm(W, 1024 + hd * 256, 2, hT, ["hT"], 8, qk_evac(kT, "kT", 1.0 / 16.0))
                for half in range(2):
                    def v_evac(tt, ps, pk, half=half):
                        b.CP("act" if tt % 2 else "dve", vtok[:, tt, half * 256:(half + 1) * 256], ps, r=[pk], w=[("vtok", tt)])
                    b.proj_tm(W, 2048 + hd * 512 + half * 256, 256, hT, ["hT"], v_evac)
                def g_evac(m, tb, ps, pk):
                    i = cnt[0] % 2
                    cnt[0] += 1
                    b.ACT(yt[:, i, :], ps, AF.Silu, r=[pk], w=[("yt", i)])
                    b.ACT(sgT[:, m, tb * 512:(tb + 1) * 512], yt[:, i, :], AF.Copy, r=[("yt", i), "retnw"], w=["sgT"],
                          scale=R["normw"][:, hd * 4 + m:hd * 4 + m + 1])
                g_it = b.proj_fm_it(W, 4096 + hd * 512, 4, hT, ["hT"], 8, g_evac)
                for c in range(8):
                    ps, pk = b.PS()
                    psb = ps[:, :].bitcast(BF16)
                    for kt in range(2):
                        b.TR(psb[:, kt * 128:(kt + 1) * 128], kT[:, kt, c * 128:(c + 1) * 128], identb[:, :], r=["kT", "identb"], w=[pk], inc=(kt == 1))
                    b.CP("act" if c % 2 else "dve", ktok[:, c, :], psb[:, 0:256], r=[pk], w=[("ktok", c)])
                Sf, Sb = Sst[:, 0, :, :], Sst[:, 1, :, :]
                EtF, EtB = R["etot"][:, hd:hd + 1], R["etot"][:, 4 + hd:5 + hd]

                def upd(d, c, tailcol, et):
                    Sd = Sst[:, d, :, :]
                    i = cnt[0] % 2
                    cnt[0] += 1
                    b.ACT(ktl[:, i, :], ktok[:, c, :], AF.Copy, r=[("ktok", c), "rettail"], w=[("ktl", i)], scale=tailcol)
                    for kt in range(2):
                        ps, pk = b.PS()
                        b.MM(ps[:, :], ktl[:, i, kt * 128:(kt + 1) * 128], vtok[:, c, :], r=[("ktl", i), ("vtok", c)], w=[pk])
                        b.STT(Sd[:, kt, :], Sd[:, kt, :], et, ps[:, :], ALU.mult, ALU.add, r=[("S", d), pk, "retetot"], w=[("S", d)])

                def state_steps():
                    for s in range(nseq):
                        if is_s:
                            for d in range(2):
                                b.LOAD(Sst[:, d, :, :], I["ret_state"][d, hd].rearrange("(kt p) v -> p kt v", p=128), w=[("S", d)])
                        else:
                            b.MEMSET("dve", Sst[:, 0, :, :], 0.0, w=[("S", 0)])
                            b.MEMSET("dve", Sst[:, 1, :, :], 0.0, w=[("S", 1)])
                        yield
                        for k in range(CS):
                            cb, cf = s * CS + CS - 1 - k, s * CS + k
                            b.CP("act", Sbin[:, cb, :, :], Sb, r=[("S", 1)], w=[("Sbin", cb)])
                            upd(1, cb, R["tail"][:, 4 + hd:5 + hd], EtB)
                            b.CP("act", Sfin[:, cf, :, :], Sf, r=[("S", 0)], w=[("Sfin", cf)])
                            upd(0, cf, R["tail"][:, hd:hd + 1], EtF)
                            yield
                        if not is_s:
                            b.STORE(E["new_ret"][s, 1, hd].rearrange("(kt p) v -> p kt v", p=128), Sb, r=[("S", 1)], key=("So", 1))
                            b.STORE(E["new_ret"][s, 0, hd].rearrange("(kt p) v -> p kt v", p=128), Sf, r=[("S", 0)], key=("So", 0))
                its_ = [g_it, state_steps()]
                while its_:
                    for it_ in list(its_):
                        try:
                            next(it_)
                        except StopIteration:
                            its_.remove(it_)
                if True:
                    s = 0
                    CSs = CS
                    CS = 8
                    pys = {}

                    def stage1(cc):
                        c = s * CS + cc
                        tk = slice(c * 128, (c + 1) * 128)
                        i = c % 3
                        pg, pkg = b.PS()
                        for kt in range(2):
                            b.MM(pg[:, 0:128], kT[:, kt, tk], qT[:, kt, tk], start=(kt == 0), stop=(kt == 1), r=["kT", "qT"], w=[pkg], inc=(kt == 1))
                        M = sm[:, i, 0:128]
                        qdf = sm[:, i, 256:512].rearrange("p (k t) -> p k t", k=2)
                        qdb = sm[:, i, 512:768].rearrange("p (k t) -> p k t", k=2)
                        b.TT("dve", M, pg[:, 0:128], R["dtot"][:, hd, :], ALU.mult, r=[pkg, "retdtot"], w=[("sm", i, 0)])
                        b.TT("dve", qdf, qT[:, :, tk], R["ecs"][:, hd:hd + 1, :].to_broadcast([128, 2, 128]), ALU.mult, r=["qT", "retecs"], w=[("sm", i, 1)])
                        b.TT("dve", qdb, qT[:, :, tk], R["ecs"][:, 4 + hd:5 + hd, :].to_broadcast([128, 2, 128]), ALU.mult, r=["qT", "retecs"], w=[("sm", i, 2)])

                    def stage2(cc):
                        c = s * CS + cc
                        i = c % 3
                        M = sm[:, i, 0:128]
                        qdf = sm[:, i, 256:512].rearrange("p (k t) -> p k t", k=2)
                        qdb = sm[:, i, 512:768].rearrange("p (k t) -> p k t", k=2)
                        py, pky = b.PS()
                        pys[cc] = (py, pky)
                        for vt in range(4):
                            vs = slice(vt * 128, (vt + 1) * 128)
                            o = py[:, vs]
                            b.MM(o, vtok[:, c, vs], M, start=True, stop=False, r=[("vtok", c), ("sm", i, 0)], w=[pky], inc=False)
                            for kt in range(2):
                                b.MM(o, Sfin[:, cc, kt, vs], qdf[:, kt, :], start=False, stop=False, r=[("Sfin", cc), ("sm", i, 1)], w=[pky], inc=False)
                            for kt in range(2):
                                b.MM(o, Sbin[:, cc, kt, vs], qdb[:, kt, :], start=False, stop=(kt == 1), r=[("Sbin", cc), ("sm", i, 2)], w=[pky], inc=(kt == 1))
                        b.ACT(sqb[:, i, :], py[:, :], AF.Square, r=[pky], w=[("sqb", i)])

                    def stage3(cc):
                        c = s * CS + cc
                        tk = slice(c * 128, (c + 1) * 128)
                        i = c % 3
                        py, pky = pys.pop(cc)
                        pn, pkn = b.PS()
                        for vt in range(4):
                            b.MM(pn[:, 0:128], onesb[:, :], sqb[:, i, vt * 128:(vt + 1) * 128], start=(vt == 0), stop=(vt == 3), r=[("sqb", i), "onesb"], w=[pkn], inc=(vt == 3))
                        b.ACT(rs[:, i, :], pn[:, 0:128], AF.Ln, r=[pkn], w=[("rs1", i)], scale=1.0 / 512.0, bias=EPS)
                        b.ACT(rs[:, i, :], rs[:, i, :], AF.Exp, r=[("rs1", i)], w=[("rs1", i)], scale=-0.5)
                        y3 = yt[:, i, :].rearrange("p (v t) -> p v t", v=4)
                        b.TT("dve", y3, py[:, :].rearrange("p (v t) -> p v t", v=4), rs[:, i:i + 1, :].to_broadcast([128, 4, 128]), ALU.mult,
                             r=[pky, ("rs1", i)], w=[("yt", i)])
                        b.TT("dve", ygT[:, :, tk], y3, sgT[:, :, tk], ALU.mult, r=[("yt", i), "sgT"], w=["ygT"])
                    for step in range(CS + 2):
                        if step < CS:
                            stage1(step)
                        if 0 <= step - 1 < CS:
                            stage2(step - 1)
                        if 0 <= step - 2 < CS:
                            stage3(step - 2)
                CS = CSs
                for half in range(2):
                    slot, wk = b.wload(b.wview(Wo, hd * 4, 4, half * 512, 512))
                    for j in range(4):
                        m = half * 4 + j
                        for tb in range(2):
                            ps, pk = b.PS()
                            for kt in range(4):
                                b.MM(ps[:, :], slot[:, kt, j * 128:(j + 1) * 128], ygT[:, kt, tb * 512:(tb + 1) * 512], start=(kt == 0), stop=(kt == 3),
                                     r=wk + ["ygT"], w=[pk], inc=(kt == 3))
                            b.STT(xT[:, m, tb * 512:(tb + 1) * 512], ps[:, :], gate[:, m:m + 1], xT[:, m, tb * 512:(tb + 1) * 512],
                                  ALU.mult, ALU.add, r=[pk, "mod1"] + XK, w=XK)
            S.barrier()
    S.barrier()


def _fm(v, n):
    return np.ascontiguousarray(np.asarray(v, np.float32).reshape(n, 128).T)


def prep_core(inp, core):
    f = lambda k: np.ascontiguousarray(np.asarray(inp[k], np.float32))
    bs = core % 4
    m = {}
    xp = f("x_prompt")[core * 4:(core + 1) * 4].reshape(NT, D)
    xs = f("x_sample")[bs]
    m["xg"] = np.ascontiguousarray(np.stack([xp, xs], 0))
    cond = np.stack([f("c_ctx"), f("c")[bs]], 0)
    m["condT"] = np.ascontiguousarray(cond.reshape(2, 8, 128).transpose(2, 1, 0))
    m["ident"] = np.eye(128, dtype=np.float32)
    for nm, src, n in [("l0_norm1", "l0_norm1_w", 8), ("l0_norm2", "l0_norm2_w", 8), ("l1_norm1", "l1_norm1_w", 8),
                       ("l1_norm2", "l1_norm2_w", 8), ("final_norm", "final_norm_w", 8),
                       ("l0_mod_b", "l0_mod_b", 48), ("l1_mod_b", "l1_mod_b", 48)]:
        m[nm] = _fm(inp[src], n)
    for k in ["l0_mod_w", "l1_mod_w", "l0_w_in", "l0_w_out", "l1_w_in", "l1_w_out", "l0_ffn_w1", "l0_ffn_w3", "l0_ffn_w2",
              "l1_ffn_w1", "l1_ffn_w3", "l1_ffn_w2"]:
        m[k] = f(k)
    m["k_ctx"] = f("cache_l0_na_k")[bs].reshape(256, 512)
    m["v_ctx"] = f("cache_l0_na_v")[bs].reshape(256, 512)
    m["ssd_state"] = f("state_l0_ssd")[bs]
    rb = f("l0_na_bias")
    p = np.arange(128)
    e, ck = p // 64, p % 64
    dd = np.arange(16)
    cq = np.arange(64)
    dr = dd[None, :] - 8 + e[:, None]
    vr = (dr >= -7) & (dr <= 7)
    dc = np.clip(ck[:, None] - cq[None, :] + 15, 0, 30)
    c0 = np.clip(cq - 8, 0, 48)
    colin = (ck[:, None] >= c0[None, :]) & (ck[:, None] < c0[None, :] + 16)
    G = rb[:, np.clip(dr, -7, 7)[:, :, None] + 7, dc[:, None, :]]
    G = np.where(vr[None, :, :, None], G, np.float32(0.0))
    m["na_biasG"] = np.ascontiguousarray(G.transpose(1, 0, 2, 3).astype(np.float32))
    m["na_mask"] = np.ascontiguousarray((vr[:, :, None] & colin[:, None, :]).astype(np.float32).reshape(128, 1024))
    tt_, ii_ = np.arange(128)[:, None], np.arange(128)[None, :]
    m["tri"] = np.ascontiguousarray(np.stack([tt_ <= ii_, tt_ >= ii_, tt_ < ii_, tt_ > ii_, np.ones((128, 128), bool)], 1).astype(np.float32))
    m["l0_conv_w"] = np.ascontiguousarray(f("l0_conv_w").T.reshape(12, 128, 5).transpose(1, 0, 2))
    m["l0_conv_b"] = _fm(inp["l0_conv_b"], 12)
    m["l0_dtbias"] = f("l0_ssd_dt_bias").reshape(1, 32)
    m["l0_alog"] = f("l0_ssd_a_log").reshape(1, 32)
    m["l0_dskip"] = _fm(np.repeat(f("l0_ssd_d"), 64), 8)
    m["l0_ssdnw"] = _fm(inp["l0_ssd_norm_w"], 8)
    m["ret_decay_b"] = f("l1_ret_decay").reshape(1, 8)
    jj = np.arange(128)[:, None].astype(np.float32)
    ii = np.arange(128)[None, :].astype(np.float32)
    m["ret_idx"] = np.ascontiguousarray(np.stack([np.maximum(ii - jj, 0), np.maximum(jj - ii, 0), (jj <= ii).astype(np.float32),
                                                  (jj >= ii).astype(np.float32)], 1).astype(np.float32))
    m["ret_row"] = np.ascontiguousarray(np.stack([np.broadcast_to(ii + 1, (128, 128)), np.broadcast_to(128 - ii, (128, 128))], 1).astype(np.float32))
    m["ret_col"] = np.ascontiguousarray(np.concatenate([127 - jj, jj], 1).astype(np.float32))
    m["ret_state"] = f("state_l1_ret")[bs]
    m["l1_ret_norm"] = _fm(inp["l1_ret_norm_w"], 16)
    t = np.arange(NT)
    row = (t // 64).astype(np.float32)
    col = (t % 64).astype(np.float32)
    freqs = (10000.0 ** (-np.arange(0, 128, 2, dtype=np.float32) / 128)).astype(np.float32)
    ang = np.concatenate([row[:, None] * freqs, col[:, None] * freqs], -1)
    angd = np.repeat(ang, 2, axis=1)
    m["rope_cos"] = np.ascontiguousarray(np.cos(angd).T.reshape(2, 128, NT).transpose(1, 0, 2).astype(np.float32))
    m["rope_sin"] = np.ascontiguousarray(np.sin(angd).T.reshape(2, 128, NT).transpose(1, 0, 2).astype(np.float32))
    P = np.zeros((128, 128), np.float32)
    for i in range(64):
        P[2 * i + 1, 2 * i] = -1.0
        P[2 * i, 2 * i + 1] = 1.0
    m["permP"] = P
    return m


_CACHE = {}


def kernel(**inputs):
    if "b" not in _CACHE:
        _CACHE["b"] = build()
    b = _CACHE["b"]
    in_maps = []
    for core in range(8):
        m = prep_core(inputs, core)
        in_maps.append({k: v for k, v in m.items() if k in b.din})
    res = run_bass_kernel_spmd(b.nc, in_maps, core_ids=list(range(8)))
    R = res.results
    y_prompt = np.concatenate([np.asarray(R[c]["y"][0]).reshape(4, 256, D) for c in range(8)], 0).astype(np.float32)
    y_sample = np.stack([np.asarray(R[c]["y"][1]) for c in range(4)], 0).astype(np.float32)
    new_k = np.concatenate([np.asarray(R[c]["new_k"]).reshape(4, 256, 8, 64) for c in range(8)], 0).astype(np.float32)
    new_v = np.concatenate([np.asarray(R[c]["new_v"]).reshape(4, 256, 8, 64) for c in range(8)], 0).astype(np.float32)
    new_ssd = np.concatenate([np.asarray(R[c]["new_ssd"]) for c in range(8)], 0).astype(np.float32)
    new_ret = np.concatenate([np.asarray(R[c]["new_ret"]) for c in range(8)], 0).astype(np.float32)
    return (y_prompt, y_sample, new_k, new_v, new_ssd, new_ret)
```

```python
import numpy as np
import concourse.bass as bass
import concourse.mybir as mybir
from concourse.bass_utils import run_bass_kernel_spmd

F32 = mybir.dt.float32
BF16 = mybir.dt.bfloat16
AF = mybir.ActivationFunctionType
ALU = mybir.AluOpType

D = 1024
NT = 1024
KT = 8
FFN_H = 2816
EPS = 1e-6
L0_IN = 4128
L1_IN = 6144


class Sched:
    ENG = ("pe", "act", "dve", "pool", "sp")

    def __init__(self, nc):
        self.nc = nc
        self.q = {k: [] for k in self.ENG}
        self.esem = {k: nc.alloc_semaphore(name=f"s_{k}") for k in self.ENG}
        self.ecnt = {k: 0 for k in self.ENG}
        self.open = {k: False for k in self.ENG}
        self.known = {k: {} for k in self.ENG}
        self.pending = {k: [] for k in self.ENG}
        self.lastw = {}
        self.readers = {}
        self.dsem = {}
        self.dcnt = {}
        self.dpersist = set()
        self.out_sems = set()

    def _need(self, eng, reads, writes, use_pending=True):
        need = {}

        def add(sv, kind):
            sem, val, owner = sv
            if owner == eng and (eng == "pe" or kind == "war"):
                return
            nm = sem.name
            if nm not in need or need[nm][1] < val:
                need[nm] = (sem, val)

        for k in reads:
            if k in self.lastw:
                add(self.lastw[k], "raw")
        for k in writes:
            if k in self.lastw:
                add(self.lastw[k], "waw")
            for sv in self.readers.get(k, {}).values():
                add(sv, "war")
        if use_pending and self.pending[eng]:
            for (sem, val) in self.pending[eng]:
                nm = sem.name
                if nm not in need or need[nm][1] < val:
                    need[nm] = (sem, val)
            self.pending[eng] = []
        out = []
        kn = self.known[eng]
        for nm, (sem, val) in need.items():
            if kn.get(nm, 0) < val:
                kn[nm] = val
                out.append((sem, val))
        return out

    def op(self, eng, fn, reads=(), writes=(), inc=True, persistent=False):
        waits = self._need(eng, reads, writes, use_pending=not persistent)
        if eng == "pool" and self.ecnt[eng] > 0:
            nm = self.esem[eng].name
            if self.known[eng].get(nm, 0) < self.ecnt[eng]:
                self.known[eng][nm] = self.ecnt[eng]
                waits.append((self.esem[eng], self.ecnt[eng]))
        if inc:
            self.ecnt[eng] += 1
            val = self.ecnt[eng]
            self.open[eng] = False
        else:
            val = self.ecnt[eng] + 1
            self.open[eng] = True
        sem = self.esem[eng]
        self.q[eng].append((waits, fn, sem, 1 if inc else 0))
        sv = (sem, val, eng)
        for k in writes:
            self.lastw[k] = sv
            self.readers[k] = {}
        for k in reads:
            self.readers.setdefault(k, {})[eng] = sv

    def dma(self, qeng, out, in_, reads=(), writes=(), semkey=None, is_output=False, persistent=False, **kw):
        waits = self._need(qeng, reads, writes, use_pending=not persistent)
        sk = semkey if semkey is not None else (tuple(writes) + tuple(reads))
        if sk not in self.dsem:
            self.dsem[sk] = self.nc.alloc_semaphore(name=f"d{len(self.dsem)}")
            self.dcnt[sk] = 0
        if persistent:
            self.dpersist.add(sk)
        sem = self.dsem[sk]
        self.dcnt[sk] += 16
        val = self.dcnt[sk]
        if is_output:
            self.out_sems.add(sk)

        def fn(e, out=out, in_=in_, kw=kw):
            return e.dma_start(out=out, in_=in_, **kw)

        self.q[qeng].append((waits, fn, sem, 16))
        sv = (sem, val, "dma:" + str(sk))
        for k in writes:
            self.lastw[k] = sv
            self.readers[k] = {}
        for k in reads:
            self.readers.setdefault(k, {})["dma:" + str(sk)] = sv

    def barrier(self):
        tg = [(self.esem[e], self.ecnt[e]) for e in ("pe", "act", "dve") if self.ecnt[e] > 0]
        tg += [(self.dsem[k], self.dcnt[k]) for k in self.dsem if k not in self.dpersist]
        tg += [(self.esem["pool"], self.ecnt["pool"])] if self.ecnt["pool"] > 0 else []
        for e in ("pe", "act", "dve", "sp", "pool"):
            self.pending[e] = list(tg)

    def finish(self):
        nc = self.nc
        for e in self.ENG:
            assert not self.open[e], e
        fin = [(self.dsem[sk], self.dcnt[sk]) for sk in self.out_sems]
        with nc.Block() as block:
            def emit(name):
                def body(e):
                    for waits, fn, sem, inc in self.q[name]:
                        for (s, v) in waits:
                            e.wait_ge(s, v)
                        inst = fn(e)
                        if inc:
                            inst.then_inc(sem, inc)
                    if name == "sp":
                        for (s, v) in fin:
                            e.wait_ge(s, v)
                return body
            block.tensor(emit("pe"))
            block.scalar(emit("act"))
            block.vector(emit("dve"))
            block.gpsimd(emit("pool"))
            block.sync(emit("sp"))


class B:
    def __init__(self, dbg=None, stages=99, plan=None):
        self.plan = plan
        self.wrec = []
        self.wissued = 0
        self.dbg = dbg or []
        self.stages = stages
        nc = self.nc = bass.Bass("TRN2", target_bir_lowering=False)
        self.S = Sched(nc)
        self.din = {}
        self.dout = {}
        self.psn = 0
        self.ps = [nc.alloc_psum_tensor(f"psb{i}", [128, 512], F32) for i in range(8)]
        self.wn = 0
        self.wslot = [nc.alloc_sbuf_tensor(f"wslot{i}", [128, 2048], BF16) for i in range(self.NSLOT)]

    def inp(self, name, shape):
        t = self.nc.dram_tensor(name, list(shape), F32, kind="ExternalInput").ap()
        self.din[name] = t
        return t

    def outp(self, name, shape, dt=F32):
        t = self.nc.dram_tensor(name, list(shape), dt, kind="ExternalOutput").ap()
        self.dout[name] = t
        return t

    def sb(self, name, shape, dt=F32):
        return self.nc.alloc_sbuf_tensor("s_" + name, list(shape), dt)

    def tmps(self, *specs):
        import contextlib

        @contextlib.contextmanager
        def cm():
            with contextlib.ExitStack() as st:
                yield [st.enter_context(self.tmp(*sp)) for sp in specs]
        return cm()

    def tmp(self, name, shape, dt=F32):
        self.tn = getattr(self, "tn", 0) + 1
        return self.nc.sbuf_tensor(f"t_{name}_{self.tn}", list(shape), dt)

    def PS(self):
        b = self.psn % 8
        self.psn += 1
        return self.ps[b], ("ps", b)

    def MM(self, out, lhsT, rhs, start=True, stop=True, r=(), w=(), inc=True):
        self.S.op("pe", lambda e: e.matmul(out, lhsT=lhsT, rhs=rhs, start=start, stop=stop), reads=r, writes=w, inc=inc)

    def TR(self, out, in_, ident, r=(), w=(), inc=True):
        self.S.op("pe", lambda e: e.transpose(out=out, in_=in_, identity=ident), reads=r, writes=w, inc=inc)

    def ACT(self, out, in_, func, r=(), w=(), **kw):
        self.S.op("act", lambda e: e.activation(out=out, in_=in_, func=func, **kw), reads=r, writes=w)

    def TT(self, eng, out, in0, in1, op, r=(), w=()):
        self.S.op(eng, lambda e: e.tensor_tensor(out=out, in0=in0, in1=in1, op=op), reads=r, writes=w)

    def TS(self, eng, out, in0, s1, s2, op0, op1=None, r=(), w=()):
        if op1 is None:
            self.S.op(eng, lambda e: e.tensor_scalar(out=out, in0=in0, scalar1=s1, scalar2=None, op0=op0), reads=r, writes=w)
        else:
            self.S.op(eng, lambda e: e.tensor_scalar(out=out, in0=in0, scalar1=s1, scalar2=s2, op0=op0, op1=op1), reads=r, writes=w)

    def STT(self, out, in0, scalar, in1, op0, op1, r=(), w=()):
        self.S.op("dve", lambda e: e.scalar_tensor_tensor(out=out, in0=in0, scalar=scalar, in1=in1, op0=op0, op1=op1), reads=r, writes=w)

    def CP(self, eng, out, in_, r=(), w=()):
        if eng == "act":
            self.S.op("act", lambda e: e.copy(out=out, in_=in_), reads=r, writes=w)
        else:
            self.S.op(eng, lambda e: e.tensor_copy(out=out, in_=in_), reads=r, writes=w)

    def RECIP(self, out, in_, r=(), w=()):
        self.S.op("dve", lambda e: e.reciprocal(out=out, in_=in_), reads=r, writes=w)

    def MEMSET(self, eng, ap, val, w=()):
        self.S.op(eng, lambda e: e.memset(ap, val), writes=w)

    def LOAD(self, out, in_, w, **kw):
        self.S.dma("sp", out, in_, writes=w, **kw)

    def STORE(self, out, in_, r, key=None):
        self.S.dma("sp", out, in_, reads=r, is_output=True, semkey=key)

    def dump(self, name, ap, shape, r):
        if name in self.dbg:
            o = self.outp("dbg_" + name, shape, ap.dtype)
            self.S.dma("sp", o, ap, reads=r, is_output=True, semkey=("dbg", name))

    NSLOT, LA = 4, 2

    def _wissue(self, j):
        name, k0, kt, c0, nco = self.plan[j]
        w3 = self.din[name].rearrange("(kt p) c -> p kt c", p=128)[:, k0:k0 + kt, c0:c0 + nco]
        sl = j % self.NSLOT
        slot = self.wslot[sl][:, 0:kt * nco].rearrange("p (k c) -> p k c", k=kt)
        self.S.dma("pool", slot, w3, writes=[("wslot", sl)], persistent=True)

    def wload(self, desc):
        name, k0, kt, c0, nco = desc
        assert kt * nco <= 2048
        i = self.wn
        self.wn += 1
        self.wrec.append(desc)
        if self.plan is None:
            self.plan_tmp = getattr(self, "plan_tmp", [])
            self.plan_tmp.append(desc)
            plan_saved, self.plan = self.plan, self.plan_tmp
            self._wissue(i)
            self.plan = plan_saved
        else:
            assert self.plan[i] == desc, (i, desc, self.plan[i])
            while self.wissued <= min(i + self.LA, len(self.plan) - 1):
                self._wissue(self.wissued)
                self.wissued += 1
        sl = i % self.NSLOT
        slot = self.wslot[sl][:, 0:kt * nco].rearrange("p (k c) -> p k c", k=kt)
        return slot, [("wslot", sl)]

    @staticmethod
    def wview(W, k0, nk, c0, nco):
        return (W.name, k0, nk, c0, nco)

    def proj_fm(self, *a, **kw):
        for _ in self.proj_fm_it(*a, **kw):
            pass

    def proj_fm_it(self, W, c0, ntile, inT, in_keys, nk, evac, tblocks=(0, 1)):
        per = max(1, 2048 // (nk * 128))
        m = 0
        while m < ntile:
            n = min(per, ntile - m)
            slot, wk = self.wload(self.wview(W, 0, nk, c0 + m * 128, n * 128))
            for j in range(n):
                for tb in tblocks:
                    ps, pk = self.PS()
                    for kt in range(nk):
                        self.MM(ps[:, :], slot[:, kt, j * 128:(j + 1) * 128], inT[:, kt, tb * 512:(tb + 1) * 512],
                                start=(kt == 0), stop=(kt == nk - 1), r=wk + list(in_keys), w=[pk], inc=(kt == nk - 1))
                    evac(m + j, tb, ps[:, :], pk)
                    yield
            m += n

    def proj_tm(self, W, c0, nco, inT, in_keys, evac, nk=8):
        assert nk * nco <= 2048
        slot, wk = self.wload(self.wview(W, 0, nk, c0, nco))
        for tt in range(NT // 128):
            ps, pk = self.PS()
            for kt in range(nk):
                self.MM(ps[:, 0:nco], inT[:, kt, tt * 128:(tt + 1) * 128], slot[:, kt, :],
                        start=(kt == 0), stop=(kt == nk - 1), r=wk + list(in_keys), w=[pk], inc=(kt == nk - 1))
            evac(tt, ps[:, 0:nco], pk)


ALL_STAGES = ("l0", "ffn0", "l1", "ffn1")


def build(dbg=None, stages=ALL_STAGES, plan=None):
    if plan is None:
        plan = build(dbg=dbg, stages=stages, plan=[]).wrec
        return build(dbg=dbg, stages=stages, plan=plan)
    b = B(dbg, stages, plan if plan else None)
    nc, S = b.nc, b.S
    I = {}
    for nm, shp in [("xg", (2, NT, D)), ("condT", (128, 8, 2)), ("ident", (128, 128)),
                    ("l0_norm1", (128, 8)), ("l0_norm2", (128, 8)), ("l1_norm1", (128, 8)), ("l1_norm2", (128, 8)),
                    ("final_norm", (128, 8)), ("l0_mod_b", (128, 48)), ("l1_mod_b", (128, 48)),
                    ("l0_mod_w", (D, 6 * D)), ("l1_mod_w", (D, 6 * D)),
                    ("l0_w_in", (D, L0_IN)), ("l0_w_out", (1536, D)), ("l1_w_in", (D, L1_IN)), ("l1_w_out", (2048, D)),
                    ("l0_ffn_w1", (D, FFN_H)), ("l0_ffn_w3", (D, FFN_H)), ("l0_ffn_w2", (FFN_H, D)),
                    ("l1_ffn_w1", (D, FFN_H)), ("l1_ffn_w3", (D, FFN_H)), ("l1_ffn_w2", (FFN_H, D)),
                    ("k_ctx", (256, 512)), ("v_ctx", (256, 512)), ("na_biasG", (128, 8, 16, 64)), ("na_mask", (128, 1024)),
                    ("ssd_state", (2, 16, 128, 64)), ("tri", (128, 5, 128)), ("l0_conv_w", (128, 12, 5)), ("l0_conv_b", (128, 12)),
                    ("l0_dtbias", (1, 32)), ("l0_alog", (1, 32)), ("l0_dskip", (128, 8)), ("l0_ssdnw", (128, 8)),
                    ("ret_decay_b", (1, 8)), ("ret_idx", (128, 4, 128)), ("ret_row", (128, 2, 128)), ("ret_col", (128, 2)),
                    ("ret_state", (2, 4, 256, 512)), ("l1_ret_norm", (128, 16)), ("rope_cos", (128, 2, NT)),
                    ("rope_sin", (128, 2, NT)), ("permP", (128, 128))]:
        I[nm] = b.inp(nm, shp)
    y_out = b.outp("y", (2, NT, D))
    new_ret = b.outp("new_ret", (4, 2, 4, 256, 512))
    new_k = b.outp("new_k", (NT, 512))
    new_v = b.outp("new_v", (NT, 512))
    new_ssd = b.outp("new_ssd", (4, 2, 16, 128, 64))

    ident = b.sb("c_ident", [128, 128], F32)
    identb = b.sb("c_identb", [128, 128], BF16)
    onesb = b.sb("c_onesb", [128, 128], BF16)
    b.LOAD(ident[:, :], I["ident"][:, :], w=["ident"])
    b.CP("dve", identb[:, :], ident[:, :], r=["ident"], w=["identb"])
    b.MEMSET("dve", onesb[:, :], 1.0, w=["onesb"])
    vecs = {}
    for nm, n in [("l0_norm1", 8), ("l0_norm2", 8), ("l1_norm1", 8), ("l1_norm2", 8), ("final_norm", 8),
                  ("l0_mod_b", 48), ("l1_mod_b", 48)]:
        vecs[nm] = b.sb("v_" + nm, [128, n], F32)
        b.LOAD(vecs[nm][:, :], I[nm][:, :], w=["v_" + nm])

    condT = b.sb("condT", [128, 8, 2], F32)
    scb = b.sb("scb", [128, 8, 2], BF16)
    b.LOAD(condT[:, :, :], I["condT"][:, :, :], w=["condT"])
    b.ACT(scb[:, :, :], condT[:, :, :], AF.Silu, r=["condT"], w=["scb"])
    mod = [b.sb(f"mod{l}", [128, 48, 2], F32) for l in range(2)]
    modA = [[[b.sb(f"modA{l}{g}{j}", [128, 8], F32) for j in range(2)] for g in range(2)] for l in range(2)]
    def adaln_it(l):
        W = I[f"l{l}_mod_w"]

        def mk_modA(j):
            sc_i, nw = [(1, f"l{l}_norm1"), (4, f"l{l}_norm2")][j]
            for g in range(2):
                b.S.op("dve", lambda e, o=modA[l][g][j][:, :], i0=mod[l][:, sc_i * 8:(sc_i + 1) * 8, g], i1=vecs[nw][:, :]:
                       e.scalar_tensor_tensor(out=o, in0=i0, scalar=1.0, in1=i1, op0=ALU.add, op1=ALU.mult),
                       reads=[f"mod{l}", "v_" + nw], writes=[f"modA{l}{g}{j}"])
        for c in range(24):
            slot, wk = b.wload(b.wview(W, 0, 8, c * 256, 256))
            for j in range(2):
                ft = c * 2 + j
                ps, pk = b.PS()
                for kt in range(8):
                    b.MM(ps[:, 0:2], slot[:, kt, j * 128:(j + 1) * 128], scb[:, kt, :], start=(kt == 0), stop=(kt == 7),
                         r=wk + ["scb"], w=[pk], inc=(kt == 7))
                b.TS("dve", mod[l][:, ft, :], ps[:, 0:2], vecs[f"l{l}_mod_b"][:, ft:ft + 1], None, ALU.add,
                     r=[pk, f"v_l{l}_mod_b"], w=[f"mod{l}"])
            if c == 7:
                mk_modA(0)
            if c == 23:
                mk_modA(1)
            yield

    def modv(l, g, idx):
        return mod[l][:, idx * 8:(idx + 1) * 8, g]

    xT = b.sb("xT", [128, 8, NT], F32)

    def load_x(g):
        with b.tmp("xtok", [128, 2, D], F32) as xtok:
            for tt in range(8):
                bi = tt % 2
                b.LOAD(xtok[:, bi, :], I["xg"][g, tt * 128:(tt + 1) * 128, :], w=[("xtok", bi)])
                for half in range(2):
                    ps, pk = b.PS()
                    for q in range(4):
                        kt = half * 4 + q
                        b.TR(ps[:, q * 128:(q + 1) * 128], xtok[:, bi, kt * 128:(kt + 1) * 128], ident[:, :],
                             r=[("xtok", bi), "ident"], w=[pk], inc=(q == 3))
                    eng = "dve" if half == 0 else "act"
                    b.CP(eng, xT[:, half * 4:(half + 1) * 4, tt * 128:(tt + 1) * 128],
                         ps[:, :].rearrange("p (q t) -> p q t", q=4), r=[pk], w=[("xT", tt)])
            S.barrier()

    XK = [("xT", tt) for tt in range(8)]

    def rstd_block(src, src_keys, tb, nkt, rs, rs_key, inv_n):
        ps, pk = b.PS()
        with b.tmp("sqt", [128, 2, 512], BF16) as sq:
            for kt in range(nkt):
                bi = kt % 2
                b.ACT(sq[:, bi, :], src[:, kt, tb * 512:(tb + 1) * 512], AF.Square, r=src_keys, w=[("sq", bi)])
                b.MM(ps[:, :], onesb[:, :], sq[:, bi, :], start=(kt == 0), stop=(kt == nkt - 1),
                     r=[("sq", bi), "onesb"], w=[pk], inc=True)
        b.ACT(rs, ps[:, :], AF.Sqrt, r=[pk], w=[rs_key], scale=inv_n, bias=EPS)
        b.RECIP(rs, rs, r=[rs_key], w=[rs_key])

    def norm_mod(l, g, j, hT, hkey):
        A = modA[l][g][j]
        Bv = modv(l, g, 0 if j == 0 else 3)
        with b.tmp("rs", [128, 512], F32) as rs, b.tmp("ntmp", [128, 2, 512], F32) as tmp:
            for tb in range(2):
                rstd_block(xT, XK, tb, 8, rs[:, :], ("rs", tb), 1.0 / D)
                for kt in range(8):
                    bi = kt % 2
                    b.TT("dve", tmp[:, bi, :], xT[:, kt, tb * 512:(tb + 1) * 512], rs[:, :], ALU.mult,
                         r=XK + [("rs", tb)], w=[("ntmp", bi)])
                    b.ACT(hT[:, kt, tb * 512:(tb + 1) * 512], tmp[:, bi, :], AF.Identity,
                          r=[("ntmp", bi), f"modA{l}{g}{j}", f"mod{l}"], w=[hkey], scale=A[:, kt:kt + 1], bias=Bv[:, kt:kt + 1])
        S.barrier()

    def resid_evac(l, g, gi):
        gate = modv(l, g, gi)

        def ev(m, tb, ps, pk):
            b.STT(xT[:, m, tb * 512:(tb + 1) * 512], ps, gate[:, m:m + 1], xT[:, m, tb * 512:(tb + 1) * 512],
                  ALU.mult, ALU.add, r=[pk, f"mod{l}"] + XK, w=XK)
        return ev

    def ffn(l, g, other=None):
        def tick():
            if other is not None:
                try:
                    next(other)
                except StopIteration:
                    pass
        with b.tmp("h2T", [128, 8, NT], BF16) as h2T, b.tmp("gT", [128, 22, NT], BF16) as gT, \
                b.tmp("s1", [128, 2, 512], F32) as s1:
            norm_mod(l, g, 1, h2T, "h2T")
            W1, W3, W2 = I[f"l{l}_ffn_w1"], I[f"l{l}_ffn_w3"], I[f"l{l}_ffn_w2"]
            cnt = [0]
            for c in range(11):
                s_1, k1 = b.wload(b.wview(W1, 0, 8, c * 256, 256))
                s_3, k3 = b.wload(b.wview(W3, 0, 8, c * 256, 256))
                for j in range(2):
                    m = c * 2 + j
                    for tb in range(2):
                        p1, pk1 = b.PS()
                        p3, pk3 = b.PS()
                        for kt in range(8):
                            b.MM(p1[:, :], s_1[:, kt, j * 128:(j + 1) * 128], h2T[:, kt, tb * 512:(tb + 1) * 512],
                                 start=(kt == 0), stop=(kt == 7), r=k1 + ["h2T"], w=[pk1], inc=(kt == 7))
                        for kt in range(8):
                            b.MM(p3[:, :], s_3[:, kt, j * 128:(j + 1) * 128], h2T[:, kt, tb * 512:(tb + 1) * 512],
                                 start=(kt == 0), stop=(kt == 7), r=k3 + ["h2T"], w=[pk3], inc=(kt == 7))
                        bi = cnt[0] % 2
                        cnt[0] += 1
                        b.ACT(s1[:, bi, :], p1[:, :], AF.Silu, r=[pk1], w=[("s1", bi)])
                        b.TT("dve", gT[:, m, tb * 512:(tb + 1) * 512], s1[:, bi, :], p3[:, :], ALU.mult,
                             r=[("s1", bi), pk3], w=[("gT", m)])
                tick()
                tick()
            ev = resid_evac(l, g, 5)
            GK = [("gT", m) for m in range(22)]
            for mo in range(8):
                sa, ka = b.wload(b.wview(W2, 0, 11, mo * 128, 128))
                sb_, kb = b.wload(b.wview(W2, 11, 11, mo * 128, 128))
                for tb in range(2):
                    ps, pk = b.PS()
                    for kt in range(22):
                        sl, kk = (sa, ka) if kt < 11 else (sb_, kb)
                        b.MM(ps[:, :], sl[:, kt % 11, :], gT[:, kt, tb * 512:(tb + 1) * 512], start=(kt == 0), stop=(kt == 21),
                             r=kk + GK, w=[pk], inc=(kt == 21))
                    ev(mo, tb, ps[:, :], pk)
                tick()
            if other is not None:
                for _ in other:
                    pass
            S.barrier()

    def final_out(g, gnext=None):
        fw = vecs["final_norm"]
        with b.tmp("rs", [128, 512], F32) as rs, b.tmp("ytmp", [128, 2, 512], F32) as tmp, \
                b.tmp("ytok", [128, 2, D], F32) as ytok, b.tmp("xtokn", [128, 2, D if gnext is not None else 2], F32) as xtok:
            for tb in range(2):
                rstd_block(xT, XK, tb, 8, rs[:, :], ("rs", tb), 1.0 / D)
                for kt in range(8):
                    b.STT(xT[:, kt, tb * 512:(tb + 1) * 512], xT[:, kt, tb * 512:(tb + 1) * 512], fw[:, kt:kt + 1], rs[:, :],
                          ALU.mult, ALU.mult, r=XK + ["v_final_norm", ("rs", tb)], w=XK)
            for tt in range(8):
                bi = tt % 2
                for half in range(2):
                    ps, pk = b.PS()
                    for q in range(4):
                        kt = half * 4 + q
                        b.TR(ps[:, q * 128:(q + 1) * 128], xT[:, kt, tt * 128:(tt + 1) * 128], ident[:, :],
                             r=[("xT", tt), "ident"], w=[pk], inc=(q == 3))
                    eng = "dve" if half == 0 else "act"
                    b.CP(eng, ytok[:, bi, half * 512:(half + 1) * 512], ps[:, :], r=[pk], w=[("ytok", bi, half)])
                b.STORE(y_out[g, tt * 128:(tt + 1) * 128, :], ytok[:, bi, :], r=[("ytok", bi, 0), ("ytok", bi, 1)], key=("ytok", bi))
                if gnext is not None:
                    b.LOAD(xtok[:, bi, :], I["xg"][gnext, tt * 128:(tt + 1) * 128, :], w=[("xtokn", bi)])
                    for half in range(2):
                        ps, pk = b.PS()
                        for q in range(4):
                            kt = half * 4 + q
                            b.TR(ps[:, q * 128:(q + 1) * 128], xtok[:, bi, kt * 128:(kt + 1) * 128], ident[:, :],
                                 r=[("xtokn", bi), "ident"], w=[pk], inc=(q == 3))
                        eng = "act" if half == 0 else "dve"
                        b.CP(eng, xT[:, half * 4:(half + 1) * 4, tt * 128:(tt + 1) * 128],
                             ps[:, :].rearrange("p (q t) -> p q t", q=4), r=[pk], w=[("xT", tt)])
            S.barrier()

    l0c = {}
    if "l0" in stages:
        l0c["tri"] = b.sb("tri", [128, 5, 128], F32)
        l0c["trib"] = b.sb("trib", [128, 5, 128], BF16)
        l0c["conv_w"] = b.sb("convw", [128, 12, 5], F32)
        l0c["conv_b"] = b.sb("convb", [128, 12], F32)
        l0c["dtbias"] = b.sb("dtbias", [128, 32], F32)
        l0c["negA"] = b.sb("negA", [128, 32], F32)
        l0c["dskip"] = b.sb("dskip", [128, 8], F32)
        l0c["ssdnw"] = b.sb("ssdnw", [128, 8], F32)
        b.LOAD(l0c["tri"][:, :, :], I["tri"][:, :, :], w=["tri"])
        b.CP("dve", l0c["trib"][:, :, :], l0c["tri"][:, :, :], r=["tri"], w=["tri"])
        l0c["trir"] = b.sb("trir", [128, 5, 128], mybir.dt.float32r)
        b.CP("dve", l0c["trir"][:, :, :], l0c["tri"][:, :, :], r=["tri"], w=["tri"])
        b.LOAD(l0c["conv_w"][:, :, :], I["l0_conv_w"][:, :, :], w=["l0c"], semkey="l0c1")
        b.LOAD(l0c["conv_b"][:, :], I["l0_conv_b"][:, :], w=["l0c"], semkey="l0c2")
        b.LOAD(l0c["dtbias"][:, :], I["l0_dtbias"][0:1, :].partition_broadcast(128), w=["l0c"], semkey="l0c3")
        b.LOAD(l0c["negA"][:, :], I["l0_alog"][0:1, :].partition_broadcast(128), w=["l0c"], semkey="l0c4")
        b.LOAD(l0c["dskip"][:, :], I["l0_dskip"][:, :], w=["l0c"], semkey="l0c5")
        b.LOAD(l0c["ssdnw"][:, :], I["l0_ssdnw"][:, :], w=["l0c"], semkey="l0c6")
        b.ACT(l0c["negA"][:, :], l0c["negA"][:, :], AF.Exp, r=["l0c"], w=["l0c"])
        b.TS("dve", l0c["negA"][:, :], l0c["negA"][:, :], -1.0, None, ALU.mult, r=["l0c"], w=["l0c"])

    ret = {}
    if "l1" in stages:
        lgb = b.sb("lgb", [128, 8], F32)
        ridx = b.sb("ridx", [128, 4, 128], F32)
        rrow = b.sb("rrow", [128, 2, 128], F32)
        rcol = b.sb("rcol", [128, 2], F32)
        ret["normw"] = b.sb("retnw", [128, 16], F32)
        ret["permP"] = b.sb("permP", [128, 128], F32)
        ret["dtot"] = b.sb("dtot", [128, 4, 128], BF16)
        ret["ecs"] = b.sb("ecs", [128, 8, 128], BF16)
        ret["tail"] = b.sb("rtail", [128, 8], F32)
        ret["etot"] = b.sb("retot", [128, 8], F32)
        b.LOAD(lgb[:, :], I["ret_decay_b"][0:1, :].partition_broadcast(128), w=["lgb"])
        b.LOAD(ridx[:, :, :], I["ret_idx"][:, :, :], w=["ridx"])
        b.LOAD(rrow[:, :, :], I["ret_row"][:, :, :], w=["rrow"])
        b.LOAD(rcol[:, :], I["ret_col"][:, :], w=["rcol"])
        b.LOAD(ret["normw"][:, :], I["l1_ret_norm"][:, :], w=["retnw"])
        b.LOAD(ret["permP"][:, :], I["permP"][:, :], w=["permP"])
        ret["permPr"] = b.sb("permPr", [128, 128], mybir.dt.float32r)
        b.CP("dve", ret["permPr"][:, :], ret["permP"][:, :], r=["permP"], w=["permP"])
        b.ACT(lgb[:, :], lgb[:, :], AF.Exp, r=["lgb"], w=["lgb"], scale=-1.0)
        b.ACT(lgb[:, :], lgb[:, :], AF.Ln, r=["lgb"], w=["lgb"], bias=1.0)
        b.TS("dve", lgb[:, :], lgb[:, :], -1.0, None, ALU.mult, r=["lgb"], w=["lgb"])
        with b.tmp("rtmp", [128, 2, 128], F32) as rtmp:
            for hd in range(4):
                for d in range(2):
                    b.ACT(rtmp[:, d, :], ridx[:, d, :], AF.Exp, r=["ridx", "lgb"], w=[("rtmp", d)], scale=lgb[:, d * 4 + hd:d * 4 + hd + 1])
                    b.TT("dve", rtmp[:, d, :], rtmp[:, d, :], ridx[:, 2 + d, :], ALU.mult, r=[("rtmp", d), "ridx"], w=[("rtmp", d)])
                    b.ACT(ret["ecs"][:, d * 4 + hd, :], rrow[:, d, :], AF.Exp, r=["rrow", "lgb"], w=["retecs"], scale=lgb[:, d * 4 + hd:d * 4 + hd + 1])
                b.TT("dve", ret["dtot"][:, hd, :], rtmp[:, 0, :], rtmp[:, 1, :], ALU.add, r=[("rtmp", 0), ("rtmp", 1)], w=["retdtot"])
            for d in range(2):
                b.ACT(ret["tail"][:, d * 4:(d + 1) * 4], lgb[:, d * 4:(d + 1) * 4], AF.Exp, r=["lgb", "rcol"], w=["rettail"], scale=rcol[:, d:d + 1])
            b.ACT(ret["etot"][:, :], lgb[:, :], AF.Exp, r=["lgb"], w=["retetot"], scale=128.0)
            S.barrier()

    b.env = dict(ret=ret, new_ret=new_ret, l0c=l0c, new_k=new_k, new_v=new_v, new_ssd=new_ssd, I=I, vecs=vecs, ident=ident, identb=identb, onesb=onesb, mod=mod, modv=modv, xT=xT, XK=XK,
                 rstd_block=rstd_block, norm_mod=norm_mod, resid_evac=resid_evac)

    groups = [0] if 'g0' in stages else [1] if 'g1' in stages else [0, 1]
    import itertools
    load_x(groups[0])
    ada0 = adaln_it(0)
    for _ in range(8):
        next(ada0)
    ada1 = itertools.chain(ada0, adaln_it(1))
    for gi, g in enumerate(groups):
        if "l0" in stages:
            layer0(b, g, other=ada1 if gi == 0 else None)
        if gi == 0:
            for _ in ada1:
                pass
        if "ffn0" in stages:
            ffn(0, g)
        if "l1" in stages:
            layer1(b, g)
        if "ffn1" in stages:
            ffn(1, g)
        final_out(g, groups[gi + 1] if gi + 1 < len(groups) else None)
    S.finish()
    return b


def layer0(b, g, other=None):
    nc, S, E = b.nc, b.S, b.env

    def tick():
        if other is not None:
            try:
                next(other)
            except StopIteration:
                pass
    I, xT, XK, onesb, identb, ident = E["I"], E["xT"], E["XK"], E["onesb"], E["identb"], E["ident"]
    is_s = (g == 1)
    W = I["l0_w_in"]
    Wo = I["l0_w_out"]
    C0 = E["l0c"]
    nseq, CS, L = (1, 8, 1024) if is_s else (4, 2, 256)
    gate = E["modv"](0, g, 2)
    cnt = [0]

    def nxt():
        cnt[0] += 1
        return cnt[0] % 2
    with b.tmp("mixT", [128, 12, NT], BF16) as mixT:
        if "nona" in b.stages or "nossd" in b.stages:
            b.MEMSET("dve", mixT[:, :, :], 0.0, w=["mixT"])
        with b.tmp("szT", [128, 8, NT], BF16) as szT, b.tmp("xcT", [128, 12, NT], BF16) as xcT, \
                b.tmp("dta", [128, 2, 8, 32], F32) as dta, b.tmp("teall", [128, 8, 64], F32) as teall, \
                b.tmp("ahl", [128, 8, 2, 32], BF16) as ahl:
            dt_tok, a_tok = dta[:, 0, :, :], dta[:, 1, :, :]
            with b.tmp("qT", [128, 4, NT], BF16) as qT, b.tmp("kT", [128, 4, NT], BF16) as kT, \
                    b.tmp("vtok", [128, 8, 512], BF16) as vtok, b.tmp("ostg", [128, 2, 256], F32) as ostg, \
                    b.tmp("Eb", [128, 2, 512], BF16) as Eb, b.tmp("rden", [128, 2, 256], F32) as rden:
                with b.tmp("hT", [128, 8, NT], BF16) as hT, b.tmp("raw", [128, 2, 1024 + 4 * nseq], BF16) as raw, b.tmp("DG", [128, 2, 5, 128], BF16) as DG, \
                        b.tmp("dtt", [128, 4, 32], F32) as dtt:
                    E["norm_mod"](0, g, 0, hT, "hT")

                    def cp_evac(dst, dkey):
                        def ev(m, tb, ps, pk):
                            b.CP("act" if (m + tb) % 2 else "dve", dst[:, m, tb * 512:(tb + 1) * 512], ps, r=[pk], w=[dkey])
                        return ev
                    if "noqk" not in b.stages:
                        b.proj_fm(W, 0, 4, hT, ["hT"], 8, cp_evac(qT, "qT"))
                        b.proj_fm(W, 512, 4, hT, ["hT"], 8, cp_evac(kT, "kT"))
                    for half in range(2 if "nov" not in b.stages else 0):
                        def v_evac(tt, ps, pk, half=half):
                            if is_s:
                                b.CP("act", vtok[:, tt, half * 256:(half + 1) * 256], ps, r=[pk], w=[("vtok", tt)])
                            else:
                                i = nxt()
                                b.CP("dve", ostg[:, i, :], ps, r=[pk], w=[("ostg", i)])
                                b.CP("act", vtok[:, tt, half * 256:(half + 1) * 256], ostg[:, i, :], r=[("ostg", i)], w=[("vtok", tt)])
                                b.STORE(E["new_v"][tt * 128:(tt + 1) * 128, half * 256:(half + 1) * 256], ostg[:, i, :], r=[("ostg", i)], key=("ostg", i))
                        b.proj_tm(W, 1024 + half * 256, 256, hT, ["hT"], v_evac)
                    if not is_s:
                        for half in range(2 if "nok" not in b.stages else 0):
                            def k_evac(tt, ps, pk, half=half):
                                i = nxt()
                                b.CP("dve", ostg[:, i, :], ps, r=[pk], w=[("ostg", i)])
                                b.STORE(E["new_k"][tt * 128:(tt + 1) * 128, half * 256:(half + 1) * 256], ostg[:, i, :], r=[("ostg", i)], key=("ostg", i))
                            b.proj_tm(W, 512 + half * 256, 256, hT, ["hT"], k_evac)

                    def z_evac(m, tb, ps, pk):
                        b.ACT(szT[:, m, tb * 512:(tb + 1) * 512], ps, AF.Silu, r=[pk], w=["szT"])
                    if "noz" not in b.stages:
                        b.proj_fm(W, 1536, 8, hT, ["hT"], 8, z_evac)
                    for i in range(2):
                        b.MEMSET("dve", raw[:, i, :], 0.0, w=[("raw", i)])
                    spb = nseq // 2 if nseq > 1 else 1

                    def x_evac(m, tb, ps, pk):
                        i = m % 2
                        r3 = raw[:, i, :].rearrange("p (s l) -> p s l", s=nseq)
                        if nseq == 1:
                            b.CP("act", raw[:, i, 2 + tb * 512:2 + (tb + 1) * 512], ps, r=[pk], w=[("raw", i)])
                        else:
                            b.CP("act", r3[:, tb * 2:(tb + 1) * 2, 2:2 + L], ps.rearrange("p (s l) -> p s l", s=2), r=[pk], w=[("raw", i)])
                        if tb == 1:
                            b.TT("dve", DG[:, i, :, :], identb[:, :].unsqueeze(1).to_broadcast([128, 5, 128]),
                                 C0["conv_w"][:, m, :].unsqueeze(2).to_broadcast([128, 5, 128]), ALU.mult, r=["identb", "l0c"], w=[("DG", i)])
                            for t2 in range(2):
                                pc, pkc = b.PS()
                                if nseq == 1:
                                    for k in range(5):
                                        b.MM(pc[:, :], DG[:, i, k, :], raw[:, i, t2 * 512 + k:t2 * 512 + k + 512], start=(k == 0), stop=(k == 4),
                                             r=[("raw", i), ("DG", i)], w=[pkc], inc=(k == 4))
                                else:
                                    for sq in range(2):
                                        for k in range(5):
                                            b.MM(pc[:, sq * 256:(sq + 1) * 256], DG[:, i, k, :], r3[:, t2 * 2 + sq, k:k + L], start=(k == 0), stop=(k == 4),
                                                 r=[("raw", i), ("DG", i)], w=[pkc], inc=(sq == 1 and k == 4))
                                b.ACT(xcT[:, m, t2 * 512:(t2 + 1) * 512], pc[:, :], AF.Silu, r=[pkc, "l0c"], w=["xcT"], bias=C0["conv_b"][:, m:m + 1])
                    if "nox" not in b.stages:
                        b.proj_fm(W, 2560, 12, hT, ["hT"], 8, x_evac)

                    def dt_evac(tt, ps, pk):
                        b.TT("dve", dt_tok[:, tt, :], ps, C0["dtbias"][:, :], ALU.add, r=[pk, "l0c"], w=["dt"])
                        u_, s_, s2, p_ = dtt[:, 0, :], dtt[:, 1, :], dtt[:, 2, :], dtt[:, 3, :]
                        K_ = ["dtt"]
                        b.ACT(u_, dt_tok[:, tt, :], AF.Exp, r=["dt"], w=K_)
                        b.TS("dve", s_, u_, 2.0, None, ALU.add, r=K_, w=K_)
                        b.RECIP(s_, s_, r=K_, w=K_)
                        b.TT("dve", s_, s_, u_, ALU.mult, r=K_, w=K_)
                        b.TT("dve", s2, s_, s_, ALU.mult, r=K_, w=K_)
                        b.TS("dve", p_, s2, 1.0 / 11.0, 1.0 / 9.0, ALU.mult, ALU.add, r=K_, w=K_)
                        for cf in (1.0 / 7.0, 1.0 / 5.0, 1.0 / 3.0, 1.0):
                            b.TT("dve", p_, p_, s2, ALU.mult, r=K_, w=K_)
                            b.TS("dve", p_, p_, cf, None, ALU.add, r=K_, w=K_)
                        b.TT("dve", p_, p_, s_, ALU.mult, r=K_, w=K_)
                        b.TS("dve", dt_tok[:, tt, :], p_, 2.0, None, ALU.mult, r=K_, w=["dt"])
                        b.TT("dve", a_tok[:, tt, :], dt_tok[:, tt, :], C0["negA"][:, :], ALU.mult, r=["dt", "l0c"], w=["dt"])
                        b.CP("dve", ahl[:, tt, 0, :], a_tok[:, tt, :], r=["dt"], w=["dt"])
                        b.TT("dve", ahl[:, tt, 1, :], a_tok[:, tt, :], ahl[:, tt, 0, :], ALU.subtract, r=["dt"], w=["dt"])
                    if "nodt" not in b.stages:
                        b.proj_tm(W, 4096, 32, hT, ["hT"], dt_evac)
                    b.dump(f"h{g}", hT[:, :, :], (128, 8, NT), ["hT"])
                    b.dump(f"xc{g}", xcT[:, :, :], (128, 12, NT), ["xcT"])
                    b.dump(f"sz{g}", szT[:, :, :], (128, 8, NT), ["szT"])
                    b.dump(f"dt{g}", dta[:, :, :, :], (128, 2, 8, 32), ["dt"])
                    S.barrier()
                if not is_s:
                    def ctx_s1(s, h, i):
                        hp, ht = (h % 2) * 64, h // 2
                        tq = slice(s * 256, (s + 1) * 256)
                        ps, pk = b.PS()
                        for c in range(2):
                            tkk = slice(s * 256 + c * 128, s * 256 + (c + 1) * 128)
                            b.MM(ps[:, c * 256:(c + 1) * 256], kT[hp:hp + 64, ht, tkk], qT[hp:hp + 64, ht, tq], r=["kT", "qT"], w=[pk], inc=(c == 1))
                        b.ACT(Eb[:, i, :], ps[:, :], AF.Exp, r=[pk], w=[("Eb", i)], scale=0.125)

                    def ctx_s2(s, h, i):
                        hp, ht = (h % 2) * 64, h // 2
                        tq = slice(s * 256, (s + 1) * 256)
                        po, pko = b.PS()
                        for c in range(2):
                            b.MM(po[hp:hp + 64, 0:256], vtok[:, s * 2 + c, h * 64:(h + 1) * 64], Eb[:, i, c * 256:(c + 1) * 256], start=(c == 0), stop=(c == 1),
                                 r=[("vtok", s * 2 + c), ("Eb", i)], w=[pko], inc=False)
                        for c in range(2):
                            b.MM(po[hp:hp + 64, 256:512], onesb[:, 0:64], Eb[:, i, c * 256:(c + 1) * 256], start=(c == 0), stop=(c == 1),
                                 r=["onesb", ("Eb", i)], w=[pko], inc=(c == 1))
                        b.ACT(rden[hp:hp + 64, i, :], po[hp:hp + 64, 256:512], AF.Ln, r=[pko], w=[("rden", i)])
                        b.ACT(rden[hp:hp + 64, i, :], rden[hp:hp + 64, i, :], AF.Exp, r=[("rden", i)], w=[("rden", i)], scale=-1.0)
                        b.TT("dve", mixT[hp:hp + 64, ht, tq], po[hp:hp + 64, 0:256], rden[hp:hp + 64, i, :], ALU.mult, r=[pko, ("rden", i)], w=["mixT"])
                    its = [(s, h) for s in range(4 if "nona" not in b.stages else 0) for h in range(8)]
                    for n in range(len(its) + 1):
                        if n < len(its):
                            ctx_s1(its[n][0], its[n][1], n % 2)
                        if n >= 1:
                            ctx_s2(its[n - 1][0], its[n - 1][1], (n - 1) % 2)
                        tick()
                else:
                    with b.tmp("kcT", [128, 4, 256], BF16) as kcT, b.tmp("vc", [128, 2, 512], BF16) as vc, \
                            b.tmp("expB", [128, 8, 16, 64], BF16) as expB:
                      with b.tmp("ctmp", [128, 2, 1024], F32) as ctmp, b.tmp("namask", [128, 1024], F32) as nmask:
                        for c in range(2):
                            b.LOAD(ctmp[:, 0, 0:512], I["k_ctx"][c * 128:(c + 1) * 128, :], w=[("ctmp", 0)])
                            b.LOAD(ctmp[:, 1, 0:512], I["v_ctx"][c * 128:(c + 1) * 128, :], w=[("ctmp", 1)])
                            b.CP("dve", vc[:, c, :], ctmp[:, 1, 0:512], r=[("ctmp", 1)], w=["vc"])
                            ps, pk = b.PS()
                            for m in range(4):
                                b.TR(ps[:, m * 128:(m + 1) * 128], ctmp[:, 0, m * 128:(m + 1) * 128], ident[:, :], r=[("ctmp", 0), "ident"], w=[pk], inc=(m == 3))
                            b.CP("act", kcT[:, :, c * 128:(c + 1) * 128], ps[:, :].rearrange("p (m t) -> p m t", m=4), r=[pk], w=["kcT"])
                        b.LOAD(nmask[:, :], I["na_mask"][:, :], w=["na_mask"])
                        for h in range(8):
                            i = h % 2
                            b.LOAD(ctmp[:, i, :], I["na_biasG"][:, h].rearrange("p d c -> p (d c)"), w=[("ctmp", i)])
                            b.ACT(ctmp[:, i, :], ctmp[:, i, :], AF.Exp, r=[("ctmp", i)], w=[("ctmp", i)])
                            b.TT("dve", expB[:, h, :, :].rearrange("p d c -> p (d c)"), ctmp[:, i, :], nmask[:, :], ALU.mult,
                                 r=[("ctmp", i), "na_mask"], w=["expB"])
                        S.barrier()
                      if True:
                        def lat_info(r_):
                            r0 = min(max(r_ - 4, 0), 8)
                            kcs = list(range(r0 // 2, (r0 + 7) // 2 + 1))
                            return r0, kcs, len(kcs), 2 * kcs[0] - r_ + 8

                        def lat_s1(r_, h, i):
                            r0, kcs, nl, d0 = lat_info(r_)
                            hp, ht = (h % 2) * 64, h // 2
                            tq = slice(r_ * 64, (r_ + 1) * 64)
                            ps, pk = b.PS()
                            n = nl + 2
                            for mi, kc in enumerate(kcs):
                                b.MM(ps[:, mi * 64:(mi + 1) * 64], kT[hp:hp + 64, ht, kc * 128:(kc + 1) * 128], qT[hp:hp + 64, ht, tq], r=["kT", "qT"], w=[pk], inc=False)
                            for cc in range(2):
                                b.MM(ps[:, (nl + cc) * 64:(nl + cc + 1) * 64], kcT[hp:hp + 64, ht, cc * 128:(cc + 1) * 128], qT[hp:hp + 64, ht, tq], r=["kcT", "qT"], w=[pk], inc=(cc == 1))
                            b.ACT(Eb[:, i, 0:n * 64], ps[:, 0:n * 64], AF.Exp, r=[pk], w=[("Eb", i)], scale=0.125)
                            b.TT("dve", Eb[:, i, 0:nl * 64].rearrange("p (m c) -> p m c", m=nl), Eb[:, i, 0:nl * 64].rearrange("p (m c) -> p m c", m=nl),
                                 expB[:, h, d0:d0 + 2 * nl - 1:2, :], ALU.mult, r=[("Eb", i), "expB"], w=[("Eb", i)])

                        def lat_s2(r_, h, i):
                            r0, kcs, nl, d0 = lat_info(r_)
                            hp, ht = (h % 2) * 64, h // 2
                            tq = slice(r_ * 64, (r_ + 1) * 64)
                            n = nl + 2
                            po, pko = b.PS()
                            for which in range(2):
                                for mi in range(n):
                                    if mi < nl:
                                        kc = kcs[mi]
                                        e0 = r0 <= 2 * kc <= r0 + 7
                                        e1 = r0 <= 2 * kc + 1 <= r0 + 7
                                        lo, hi = (0 if e0 else 64), (128 if e1 else 64)
                                        lhs = vtok[lo:hi, kc, h * 64:(h + 1) * 64] if which == 0 else onesb[lo:hi, 0:64]
                                        rk = ("vtok", kc)
                                    else:
                                        lo, hi = 0, 128
                                        lhs = vc[:, mi - nl, h * 64:(h + 1) * 64] if which == 0 else onesb[:, 0:64]
                                        rk = "vc"
                                    b.MM(po[hp:hp + 64, which * 64:(which + 1) * 64], lhs, Eb[lo:hi, i, mi * 64:(mi + 1) * 64], start=(mi == 0), stop=(mi == n - 1),
                                         r=[rk, "onesb", ("Eb", i)], w=[pko], inc=(which == 1 and mi == n - 1))
                            b.ACT(rden[hp:hp + 64, i, 0:64], po[hp:hp + 64, 64:128], AF.Ln, r=[pko], w=[("rden", i)])
                            b.ACT(rden[hp:hp + 64, i, 0:64], rden[hp:hp + 64, i, 0:64], AF.Exp, r=[("rden", i)], w=[("rden", i)], scale=-1.0)
                            b.TT("dve", mixT[hp:hp + 64, ht, tq], po[hp:hp + 64, 0:64], rden[hp:hp + 64, i, 0:64], ALU.mult, r=[pko, ("rden", i)], w=["mixT"])
                        its = [(r_, h) for r_ in range(16 if "nona" not in b.stages else 0) for h in range(8)]
                        for n_ in range(len(its) + 1):
                            if n_ < len(its):
                                lat_s1(its[n_][0], its[n_][1], n_ % 2)
                            if n_ >= 1:
                                lat_s2(its[n_ - 1][0], its[n_ - 1][1], (n_ - 1) % 2)
                            if n_ % 4 == 0:
                                tick()
                        S.barrier()
                S.barrier()
            b.dump(f"att{g}", mixT[:, 0:4, :], (128, 4, NT), ["mixT"])
            with b.tmps(("xtok", [128, 2, 1024], BF16), ("Btok", [128, 2, 256], BF16), ("Sst", [128, 2, 1024], F32),
                        ("Sfb", [128, 1024], BF16), ("Sbin", [128, CS, 1024], BF16), ("R1", [128, 2, 2, 512], mybir.dt.float32r),
                        ("DE", [128, 2, 2, 2, 512], BF16), ("MC", [128, 2, 2, 2, 512], BF16), ("vd", [128, 2, 2, 256], BF16),
                        ("vt", [128, 1, 1024], BF16), ("Gm", [128, 2, 2, 2, 128], BF16), ("yz", [128, 2, 512], F32),
                        ("sq", [128, 2, 512], BF16), ("rs", [128, 2, 128], F32), ("wdt", [128, 2, 32], F32),
                        ("stmp", [128, 1, 512], F32)) as (xtok2, Btok2, Sst, Sfb, Sbin, R1, DE, MC, vd, vtl, Gm, yz, sqb, rs, wdt, stmp):
                tri, trib = C0["tri"], C0["trib"]
                tokc = [0]

                def tok_tiles(c):
                    tokc[0] += 1
                    i = tokc[0] % 2
                    tk = slice(c * 128, (c + 1) * 128)
                    ps, pk = b.PS()
                    psb = ps[:, :].bitcast(BF16)
                    for m in range(8):
                        b.TR(psb[:, m * 128:(m + 1) * 128], xcT[:, m, tk], identb[:, :], r=["xcT", "identb"], w=[pk], inc=(m == 7))
                    b.CP("act" if c % 2 else "dve", xtok2[:, i, :], psb[:, :], r=[pk], w=[("xtok", i)])
                    ps, pk = b.PS()
                    psb = ps[:, :].bitcast(BF16)
                    for m in range(2):
                        b.TR(psb[:, m * 128:(m + 1) * 128], xcT[:, 8 + m, tk], identb[:, :], r=["xcT", "identb"], w=[pk], inc=(m == 1))
                    b.CP("dve" if c % 2 else "act", Btok2[:, i, :], psb[:, 0:256], r=[pk], w=[("Btok", i)])
                    return i
                for c in range(8 if "notails" not in b.stages else 0):
                    pt, pkt = b.PS()
                    for hl in range(2):
                        b.MM(pt[:, 0:16], trib[:, 3, :], ahl[:, c, hl, 0:16], start=(hl == 0), stop=(hl == 1), r=["dt", "tri"], w=[pkt], inc=False)
                    for hl in range(2):
                        b.MM(pt[:, 16:32], trib[:, 2, :], ahl[:, c, hl, 16:32], start=(hl == 0), stop=(hl == 1), r=["dt", "tri"], w=[pkt], inc=False)
                    for hl in range(2):
                        b.MM(pt[:, 32:64], trib[:, 4, :], ahl[:, c, hl, 0:32], start=(hl == 0), stop=(hl == 1), r=["dt", "tri"], w=[pkt], inc=(hl == 1))
                    b.ACT(teall[:, c, :], pt[:, 0:64], AF.Exp, r=[pkt], w=[("te", c)])

                def upd(d, c, bi):
                    i = nxt()
                    vi = 0
                    b.TT("dve", wdt[:, i, 0:16], dt_tok[:, c, d * 16:(d + 1) * 16], teall[:, c, d * 16:(d + 1) * 16], ALU.mult, r=["dt", ("te", c)], w=[("wdt", i)])
                    b.TT("dve", vtl[:, vi, :].rearrange("p (h v) -> p h v", h=16), xtok2[:, bi, :].rearrange("p (h v) -> p h v", h=16),
                         wdt[:, i, 0:16].unsqueeze(2).to_broadcast([128, 16, 64]), ALU.mult, r=[("xtok", bi), ("wdt", i)], w=[("vtl", vi)])
                    for grp in range(2):
                        gs = slice(grp * 512, (grp + 1) * 512)
                        ps, pk = b.PS()
                        b.MM(ps[:, :], Btok2[:, bi, grp * 128:(grp + 1) * 128], vtl[:, vi, gs], r=[("Btok", bi), ("vtl", vi)], w=[pk])
                        j = 0
                        b.TT("dve", stmp[:, j, :].rearrange("p (h v) -> p h v", h=8), Sst[:, d, gs].rearrange("p (h v) -> p h v", h=8),
                             teall[:, c, 32 + d * 16 + grp * 8:32 + d * 16 + grp * 8 + 8].unsqueeze(2).to_broadcast([128, 8, 64]), ALU.mult,
                             r=[("S", d), ("te", c)], w=[("stmp", j)])
                        b.TT("dve", Sst[:, d, gs], stmp[:, j, :], ps[:, :], ALU.add, r=[("stmp", j), pk], w=[("S", d)])
                for s in range(nseq if "nossd" not in b.stages else 0):
                    if is_s:
                        for d in range(2):
                            b.LOAD(Sst[:, d, :].rearrange("p (h v) -> p h v", h=16), I["ssd_state"][d].rearrange("h n v -> n h v"), w=[("S", d)])
                    else:
                        b.MEMSET("dve", Sst[:, 0, :], 0.0, w=[("S", 0)])
                        b.MEMSET("dve", Sst[:, 1, :], 0.0, w=[("S", 1)])
                    for cc in reversed(range(CS)):
                        c = s * CS + cc
                        b.CP("act", Sbin[:, cc, :], Sst[:, 1, :], r=[("S", 1)], w=[("Sbin", cc)])
                        bi = tok_tiles(c)
                        upd(1, c, bi)
                    if not is_s:
                        b.STORE(E["new_ssd"][s, 1].rearrange("h n v -> n h v"), Sst[:, 1, :].rearrange("p (h v) -> p h v", h=16), r=[("S", 1)], key=("So", 1))
                    info = {}

                    def stageA(cc, qd, par):
                        c = s * CS + cc
                        tk = slice(c * 128, (c + 1) * 128)
                        grp, cp = qd // 2, cc % 2
                        if qd == 0:
                            info[cc] = tok_tiles(c)
                            for g2 in range(2):
                                pg, pkg = b.PS()
                                b.MM(pg[:, 0:128], xcT[:, 8 + g2, tk], xcT[:, 10 + g2, tk], r=["xcT"], w=[pkg])
                                for d in range(2):
                                    b.TT("dve", Gm[:, cp, g2, d, :], pg[:, 0:128], trib[:, d, :], ALU.mult, r=[pkg, "tri"], w=[("Gm", cp, g2, d)])
                        bi = info[cc]
                        pds = {}
                        for d in range(2):
                            for hh in range(4):
                                dh = d * 16 + qd * 4 + hh
                                b.ACT(R1[:, par, d, hh * 128:(hh + 1) * 128], tri[:, d, :], AF.Identity, r=["dt", "tri"], w=[("R1", par, d)],
                                      scale=a_tok[:, c, dh:dh + 1])
                            b.TT("dve", vd[:, par, d, :].rearrange("p (h v) -> p h v", h=4), xtok2[:, bi, qd * 256:(qd + 1) * 256].rearrange("p (h v) -> p h v", h=4),
                                 dt_tok[:, c, d * 16 + qd * 4:d * 16 + qd * 4 + 4].unsqueeze(2).to_broadcast([128, 4, 64]), ALU.mult,
                                 r=[("xtok", bi), "dt"], w=[("vd", par, d)])
                        for d in range(2):
                            pd, pkd = b.PS()
                            b.MM(pd[:, :], C0["trir"][:, 3 - d, :], R1[:, par, d, :], r=[("R1", par, d), "tri"], w=[pkd])
                            pc, pkc = b.PS()
                            b.MM(pc[:, :], C0["trir"][:, 4, :], R1[:, par, d, :], r=[("R1", par, d), "tri"], w=[pkc])
                            pds[d] = (pd, pkd, pc, pkc)
                        for d in range(2):
                            pd, pkd, pc, pkc = pds[d]
                            b.ACT(DE[:, par, d, 0, :], pd[:, :], AF.Exp, r=[pkd], w=[("DE", par, d, 0)])
                            b.ACT(DE[:, par, d, 1, :], pc[:, :], AF.Exp, r=[pkc], w=[("DE", par, d, 1)])

                    def stageB(cc, qd, par):
                        c = s * CS + cc
                        tk = slice(c * 128, (c + 1) * 128)
                        grp, cp, q2 = qd // 2, cc % 2, qd % 2
                        if qd == 0:
                            b.CP("act", Sfb[:, :], Sst[:, 0, :], r=[("S", 0)], w=["Sfb"])
                        for d in range(2):
                            b.TT("dve", MC[:, par, d, 0, :].rearrange("p (h t) -> p h t", h=4), DE[:, par, d, 0, :].rearrange("p (h t) -> p h t", h=4),
                                 Gm[:, cp, grp, d:d + 1, :].to_broadcast([128, 4, 128]), ALU.mult, r=[("DE", par, d, 0), ("Gm", cp, grp, d)], w=[("MC", par, d, 0)])
                            b.TT("dve", MC[:, par, d, 1, :].rearrange("p (h t) -> p h t", h=4), DE[:, par, d, 1, :].rearrange("p (h t) -> p h t", h=4),
                                 xcT[:, 10 + grp:11 + grp, tk].to_broadcast([128, 4, 128]), ALU.mult, r=[("DE", par, d, 1), "xcT"], w=[("MC", par, d, 1)])
                        py, pky = b.PS()
                        for hh in range(4):
                            h = qd * 4 + hh
                            hp = (h % 2) * 64
                            o = py[hp:hp + 64, (hh // 2) * 128:(hh // 2 + 1) * 128]
                            hs = slice(hh * 128, (hh + 1) * 128)
                            for d in range(2):
                                st = (Sfb[:, h * 64:(h + 1) * 64], "Sfb") if d == 0 else (Sbin[:, cc, h * 64:(h + 1) * 64], ("Sbin", cc))
                                b.MM(o, vd[:, par, d, hh * 64:(hh + 1) * 64], MC[:, par, d, 0, hs], start=(d == 0), stop=False,
                                     r=[("vd", par, d), ("MC", par, d, 0)], w=[pky], inc=False)
                                b.MM(o, st[0], MC[:, par, d, 1, hs], start=False, stop=(d == 1), r=[st[1], ("MC", par, d, 1)], w=[pky], inc=(d == 1))
                        for mm in range(2):
                            m = qd * 2 + mm
                            b.STT(yz[:, grp, (q2 * 2 + mm) * 128:(q2 * 2 + mm + 1) * 128], xcT[:, m, tk], C0["dskip"][:, m:m + 1], py[:, mm * 128:(mm + 1) * 128],
                                  ALU.mult, ALU.add, r=["xcT", "l0c", pky], w=[("yz", grp, q2)])
                        b.TT("dve", yz[:, grp, q2 * 256:(q2 + 1) * 256].rearrange("p (m t) -> p m t", m=2), yz[:, grp, q2 * 256:(q2 + 1) * 256].rearrange("p (m t) -> p m t", m=2),
                             szT[:, qd * 2:qd * 2 + 2, tk], ALU.mult, r=[("yz", grp, q2), "szT"], w=[("yz", grp, q2)])
                        if q2 == 1:
                            b.ACT(sqb[:, grp, :], yz[:, grp, :], AF.Square, r=[("yz", grp, 0), ("yz", grp, 1)], w=[("sqb", grp)])
                            pn, pkn = b.PS()
                            for t in range(4):
                                b.MM(pn[:, 0:128], onesb[:, :], sqb[:, grp, t * 128:(t + 1) * 128], start=(t == 0), stop=(t == 3), r=[("sqb", grp), "onesb"], w=[pkn], inc=(t == 3))
                            b.ACT(rs[:, grp, :], pn[:, 0:128], AF.Ln, r=[pkn], w=[("rs0", grp)], scale=1.0 / 512.0, bias=EPS)
                            b.ACT(rs[:, grp, :], rs[:, grp, :], AF.Exp, r=[("rs0", grp)], w=[("rs0", grp)], scale=-0.5)
                            for t in range(4):
                                m = grp * 4 + t
                                b.STT(mixT[:, 4 + m, tk], yz[:, grp, t * 128:(t + 1) * 128], C0["ssdnw"][:, m:m + 1], rs[:, grp, :], ALU.mult, ALU.mult,
                                      r=[("yz", grp, 0), ("yz", grp, 1), "l0c", ("rs0", grp)], w=["mixT"])
                        if qd == 3:
                            upd(0, c, info[cc])
                    units = [(cc, qd) for cc in range(CS) for qd in range(4)]
                    for n_ in range(len(units) + 1):
                        if n_ < len(units):
                            stageA(units[n_][0], units[n_][1], n_ % 2)
                        if n_ >= 1:
                            stageB(units[n_ - 1][0], units[n_ - 1][1], (n_ - 1) % 2)
                        tick()
                    if not is_s:
                        b.STORE(E["new_ssd"][s, 0].rearrange("h n v -> n h v"), Sst[:, 0, :].rearrange("p (h v) -> p h v", h=16), r=[("S", 0)], key=("So", 0))
                S.barrier()
            S.barrier()
        b.dump(f"mix{g}", mixT[:, :, :], (128, 12, NT), ["mixT"])
        for m in range(8):
            slot, wk = b.wload(b.wview(Wo, 0, 12, m * 128, 128))
            for tb in range(2):
                ps, pk = b.PS()
                for kt in range(12):
                    b.MM(ps[:, :], slot[:, kt, :], mixT[:, kt, tb * 512:(tb + 1) * 512], start=(kt == 0), stop=(kt == 11), r=wk + ["mixT"], w=[pk], inc=(kt == 11))
                b.STT(xT[:, m, tb * 512:(tb + 1) * 512], ps[:, :], gate[:, m:m + 1], xT[:, m, tb * 512:(tb + 1) * 512], ALU.mult, ALU.add,
                      r=[pk, "mod0"] + XK, w=XK)
    S.barrier()


def layer1(b, g):
    nc, S, E = b.nc, b.S, b.env
    I, xT, XK, onesb, identb = E["I"], E["xT"], E["XK"], E["onesb"], E["identb"]
    is_s = (g == 1)
    W = I["l1_w_in"]
    Wo = I["l1_w_out"]
    R = E["ret"]
    nseq, CS = (1, 8) if is_s else (4, 2)
    gate = E["modv"](1, g, 2)
    with b.tmp("hT", [128, 8, NT], BF16) as hT, b.tmp("ropec", [128, 2, NT if is_s else 2], F32) as ropec, \
            b.tmp("ropes", [128, 2, NT if is_s else 2], F32) as ropes:
        if is_s:
            R = dict(R)
            R["cos"], R["sin"] = ropec, ropes
            b.LOAD(ropec[:, :, :], I["rope_cos"][:, :, :], w=["rope"], semkey="ropec")
            b.LOAD(ropes[:, :, :], I["rope_sin"][:, :, :], w=["rope"], semkey="ropes")
        E["norm_mod"](1, g, 0, hT, "hT")
        with b.tmps(("qT", [128, 2, NT], BF16), ("kT", [128, 2, NT], BF16), ("vtok", [128, 8, 512], BF16),
                    ("sgT", [128, 4, NT], BF16), ("ktok", [128, 8, 256], BF16), ("ygT", [128, 4, NT], BF16),
                    ("Sst", [128, 2, 2, 512], F32), ("Sfin", [128, 8, 2, 512], BF16), ("Sbin", [128, 8, 2, 512], BF16),
                    ("rt", [128, 6, 512 if is_s else 2], F32), ("sm", [128, 3, 768], BF16), ("sq", [128, 3, 512], BF16),
                    ("rs", [128, 3, 128], F32), ("yt", [128, 3, 512], F32),
                    ("ktl", [128, 2, 256], BF16), ("rawr", [128, 2, 512 if is_s else 2], mybir.dt.float32r)) as (qT, kT, vtok, sgT, ktok, ygT, Sst, Sfin, Sbin, rt, sm, sqb, rs, yt, ktl, rawr):
            for hd in range(4):
                cnt = [0]

                def qk_evac(dst, dkey, scl):
                    def ev(m, tb, ps, pk):
                        sl = slice(tb * 512, (tb + 1) * 512)
                        if not is_s:
                            b.ACT(dst[:, m, sl], ps, AF.Copy, r=[pk], w=[dkey], scale=scl)
                            return
                        i = cnt[0] % 2
                        cnt[0] += 1
                        raw, t1, t2 = rawr[:, i, :], rt[:, i * 3 + 1, :], rt[:, i * 3 + 2, :]
                        b.ACT(raw, ps, AF.Copy, r=[pk], w=[("rt", i, 0)], scale=scl)
                        p2, pk2 = b.PS()
                        b.MM(p2[:, :], R["permPr"][:, :], raw, r=[("rt", i, 0), "permP"], w=[pk2])
                        b.TT("dve", t1, raw, R["cos"][:, m, sl], ALU.mult, r=[("rt", i, 0), "rope"], w=[("rt", i, 1)])
                        b.TT("dve", t2, p2[:, :], R["sin"][:, m, sl], ALU.mult, r=[pk2, "rope"], w=[("rt", i, 2)])
                        b.TT("dve", dst[:, m, sl], t1, t2, ALU.add, r=[("rt", i, 1), ("rt", i, 2)], w=[dkey])
                    return ev
                b.proj_fm(W, hd * 256, 2, hT, ["hT"], 8, qk_evac(qT, "qT", 1.0))
                b.proj_fm(W, 1024 + hd * 256, 2, hT, ["hT"], 8, qk_evac(kT, "kT", 1.0 / 16.0))
                for half in range(2):
                    def v_evac(tt, ps, pk, half=half):
                        b.CP("act" if tt % 2 else "dve", vtok[:, tt, half * 256:(half + 1) * 256], ps, r=[pk], w=[("vtok", tt)])
                    b.proj_tm(W, 2048 + hd * 512 + half * 256, 256, hT, ["hT"], v_evac)
                def g_evac(m, tb, ps, pk):
                    i = cnt[0] % 2
                    cnt[0] += 1
                    b.ACT(yt[:, i, :], ps, AF.Silu, r=[pk], w=[("yt", i)])
                    b.ACT(sgT[:, m, tb * 512:(tb + 1) * 512], yt[:, i, :], AF.Copy, r=[("yt", i), "retnw"], w=["sgT"],
                          scale=R["normw"][:, hd * 4 + m:hd * 4 + m + 1])
                g_it = b.proj_fm_it(W, 4096 + hd * 512, 4, hT, ["hT"], 8, g_evac)
                for c in range(8):
                    ps, pk = b.PS()
                    psb = ps[:, :].bitcast(BF16)
                    for kt in range(2):
                        b.TR(psb[:, kt * 128:(kt + 1) * 128], kT[:, kt, c * 128:(c + 1) * 128], identb[:, :], r=["kT", "identb"], w=[pk], inc=(kt == 1))
                    b.CP("act" if c % 2 else "dve", ktok[:, c, :], psb[:, 0:256], r=[pk], w=[("ktok", c)])
                Sf, Sb = Sst[:, 0, :, :], Sst[:, 1, :, :]
                EtF, EtB = R["etot"][:, hd:hd + 1], R["etot"][:, 4 + hd:5 + hd]

                def upd(d, c, tailcol, et):
                    Sd = Sst[:, d, :, :]
                    i = cnt[0] % 2
                    cnt[0] += 1
                    b.ACT(ktl[:, i, :], ktok[:, c, :], AF.Copy, r=[("ktok", c), "rettail"], w=[("ktl", i)], scale=tailcol)
                    for kt in range(2):
                        ps, pk = b.PS()
                        b.MM(ps[:, :], ktl[:, i, kt * 128:(kt + 1) * 128], vtok[:, c, :], r=[("ktl", i), ("vtok", c)], w=[pk])
                        b.STT(Sd[:, kt, :], Sd[:, kt, :], et, ps[:, :], ALU.mult, ALU.add, r=[("S", d), pk, "retetot"], w=[("S", d)])

                def state_steps():
                    for s in range(nseq):
                        if is_s:
                            for d in range(2):
                                b.LOAD(Sst[:, d, :, :], I["ret_state"][d, hd].rearrange("(kt p) v -> p kt v", p=128), w=[("S", d)])
                        else:
                            b.MEMSET("dve", Sst[:, 0, :, :], 0.0, w=[("S", 0)])
                            b.MEMSET("dve", Sst[:, 1, :, :], 0.0, w=[("S", 1)])
                        yield
                        for k in range(CS):
                            cb, cf = s * CS + CS - 1 - k, s * CS + k
                            b.CP("act", Sbin[:, cb, :, :], Sb, r=[("S", 1)], w=[("Sbin", cb)])
                            upd(1, cb, R["tail"][:, 4 + hd:5 + hd], EtB)
                            b.CP("act", Sfin[:, cf, :, :], Sf, r=[("S", 0)], w=[("Sfin", cf)])
                            upd(0, cf, R["tail"][:, hd:hd + 1], EtF)
                            yield
                        if not is_s:
                            b.STORE(E["new_ret"][s, 1, hd].rearrange("(kt p) v -> p kt v", p=128), Sb, r=[("S", 1)], key=("So", 1))
                            b.STORE(E["new_ret"][s, 0, hd].rearrange("(kt p) v -> p kt v", p=128), Sf, r=[("S", 0)], key=("So", 0))
                its_ = [g_it, state_steps()]
                while its_:
                    for it_ in list(its_):
                        try:
                            next(it_)
                        except StopIteration:
                            its_.remove(it_)
                if True:
                    s = 0
                    CSs = CS
                    CS = 8
                    pys = {}

                    def stage1(cc):
                        c = s * CS + cc
                        tk = slice(c * 128, (c + 1) * 128)
                        i = c % 3
                        pg, pkg = b.PS()
                        for kt in range(2):
                            b.MM(pg[:, 0:128], kT[:, kt, tk], qT[:, kt, tk], start=(kt == 0), stop=(kt == 1), r=["kT", "qT"], w=[pkg], inc=(kt == 1))
                        M = sm[:, i, 0:128]
                        qdf = sm[:, i, 256:512].rearrange("p (k t) -> p k t", k=2)
                        qdb = sm[:, i, 512:768].rearrange("p (k t) -> p k t", k=2)
                        b.TT("dve", M, pg[:, 0:128], R["dtot"][:, hd, :], ALU.mult, r=[pkg, "retdtot"], w=[("sm", i, 0)])
                        b.TT("dve", qdf, qT[:, :, tk], R["ecs"][:, hd:hd + 1, :].to_broadcast([128, 2, 128]), ALU.mult, r=["qT", "retecs"], w=[("sm", i, 1)])
                        b.TT("dve", qdb, qT[:, :, tk], R["ecs"][:, 4 + hd:5 + hd, :].to_broadcast([128, 2, 128]), ALU.mult, r=["qT", "retecs"], w=[("sm", i, 2)])

                    def stage2(cc):
                        c = s * CS + cc
                        i = c % 3
                        M = sm[:, i, 0:128]
                        qdf = sm[:, i, 256:512].rearrange("p (k t) -> p k t", k=2)
                        qdb = sm[:, i, 512:768].rearrange("p (k t) -> p k t", k=2)
                        py, pky = b.PS()
                        pys[cc] = (py, pky)
                        for vt in range(4):
                            vs = slice(vt * 128, (vt + 1) * 128)
                            o = py[:, vs]
                            b.MM(o, vtok[:, c, vs], M, start=True, stop=False, r=[("vtok", c), ("sm", i, 0)], w=[pky], inc=False)
                            for kt in range(2):
                                b.MM(o, Sfin[:, cc, kt, vs], qdf[:, kt, :], start=False, stop=False, r=[("Sfin", cc), ("sm", i, 1)], w=[pky], inc=False)
                            for kt in range(2):
                                b.MM(o, Sbin[:, cc, kt, vs], qdb[:, kt, :], start=False, stop=(kt == 1), r=[("Sbin", cc), ("sm", i, 2)], w=[pky], inc=(kt == 1))
                        b.ACT(sqb[:, i, :], py[:, :], AF.Square, r=[pky], w=[("sqb", i)])

                    def stage3(cc):
                        c = s * CS + cc
                        tk = slice(c * 128, (c + 1) * 128)
                        i = c % 3
                        py, pky = pys.pop(cc)
                        pn, pkn = b.PS()
                        for vt in range(4):
                            b.MM(pn[:, 0:128], onesb[:, :], sqb[:, i, vt * 128:(vt + 1) * 128], start=(vt == 0), stop=(vt == 3), r=[("sqb", i), "onesb"], w=[pkn], inc=(vt == 3))
                        b.ACT(rs[:, i, :], pn[:, 0:128], AF.Ln, r=[pkn], w=[("rs1", i)], scale=1.0 / 512.0, bias=EPS)
                        b.ACT(rs[:, i, :], rs[:, i, :], AF.Exp, r=[("rs1", i)], w=[("rs1", i)], scale=-0.5)
                        y3 = yt[:, i, :].rearrange("p (v t) -> p v t", v=4)
                        b.TT("dve", y3, py[:, :].rearrange("p (v t) -> p v t", v=4), rs[:, i:i + 1, :].to_broadcast([128, 4, 128]), ALU.mult,
                             r=[pky, ("rs1", i)], w=[("yt", i)])
                        b.TT("dve", ygT[:, :, tk], y3, sgT[:, :, tk], ALU.mult, r=[("yt", i), "sgT"], w=["ygT"])
                    for step in range(CS + 2):
                        if step < CS:
                            stage1(step)
                        if 0 <= step - 1 < CS:
                            stage2(step - 1)
                        if 0 <= step - 2 < CS:
                            stage3(step - 2)
                CS = CSs
                for half in range(2):
                    slot, wk = b.wload(b.wview(Wo, hd * 4, 4, half * 512, 512))
                    for j in range(4):
                        m = half * 4 + j
                        for tb in range(2):
                            ps, pk = b.PS()
                            for kt in range(4):
                                b.MM(ps[:, :], slot[:, kt, j * 128:(j + 1) * 128], ygT[:, kt, tb * 512:(tb + 1) * 512], start=(kt == 0), stop=(kt == 3),
                                     r=wk + ["ygT"], w=[pk], inc=(kt == 3))
                            b.STT(xT[:, m, tb * 512:(tb + 1) * 512], ps[:, :], gate[:, m:m + 1], xT[:, m, tb * 512:(tb + 1) * 512],
                                  ALU.mult, ALU.add, r=[pk, "mod1"] + XK, w=XK)
            S.barrier()
    S.barrier()


def _fm(v, n):
    return np.ascontiguousarray(np.asarray(v, np.float32).reshape(n, 128).T)


def prep_core(inp, core):
    f = lambda k: np.ascontiguousarray(np.asarray(inp[k], np.float32))
    bs = core % 4
    m = {}
    xp = f("x_prompt")[core * 4:(core + 1) * 4].reshape(NT, D)
    xs = f("x_sample")[bs]
    m["xg"] = np.ascontiguousarray(np.stack([xp, xs], 0))
    cond = np.stack([f("c_ctx"), f("c")[bs]], 0)
    m["condT"] = np.ascontiguousarray(cond.reshape(2, 8, 128).transpose(2, 1, 0))
    m["ident"] = np.eye(128, dtype=np.float32)
    for nm, src, n in [("l0_norm1", "l0_norm1_w", 8), ("l0_norm2", "l0_norm2_w", 8), ("l1_norm1", "l1_norm1_w", 8),
                       ("l1_norm2", "l1_norm2_w", 8), ("final_norm", "final_norm_w", 8),
                       ("l0_mod_b", "l0_mod_b", 48), ("l1_mod_b", "l1_mod_b", 48)]:
        m[nm] = _fm(inp[src], n)
    for k in ["l0_mod_w", "l1_mod_w", "l0_w_in", "l0_w_out", "l1_w_in", "l1_w_out", "l0_ffn_w1", "l0_ffn_w3", "l0_ffn_w2",
              "l1_ffn_w1", "l1_ffn_w3", "l1_ffn_w2"]:
        m[k] = f(k)
    m["k_ctx"] = f("cache_l0_na_k")[bs].reshape(256, 512)
    m["v_ctx"] = f("cache_l0_na_v")[bs].reshape(256, 512)
    m["ssd_state"] = f("state_l0_ssd")[bs]
    rb = f("l0_na_bias")
    p = np.arange(128)
    e, ck = p // 64, p % 64
    dd = np.arange(16)
    cq = np.arange(64)
    dr = dd[None, :] - 8 + e[:, None]
    vr = (dr >= -7) & (dr <= 7)
    dc = np.clip(ck[:, None] - cq[None, :] + 15, 0, 30)
    c0 = np.clip(cq - 8, 0, 48)
    colin = (ck[:, None] >= c0[None, :]) & (ck[:, None] < c0[None, :] + 16)
    G = rb[:, np.clip(dr, -7, 7)[:, :, None] + 7, dc[:, None, :]]
    G = np.where(vr[None, :, :, None], G, np.float32(0.0))
    m["na_biasG"] = np.ascontiguousarray(G.transpose(1, 0, 2, 3).astype(np.float32))
    m["na_mask"] = np.ascontiguousarray((vr[:, :, None] & colin[:, None, :]).astype(np.float32).reshape(128, 1024))
    tt_, ii_ = np.arange(128)[:, None], np.arange(128)[None, :]
    m["tri"] = np.ascontiguousarray(np.stack([tt_ <= ii_, tt_ >= ii_, tt_ < ii_, tt_ > ii_, np.ones((128, 128), bool)], 1).astype(np.float32))
    m["l0_conv_w"] = np.ascontiguousarray(f("l0_conv_w").T.reshape(12, 128, 5).transpose(1, 0, 2))
    m["l0_conv_b"] = _fm(inp["l0_conv_b"], 12)
    m["l0_dtbias"] = f("l0_ssd_dt_bias").reshape(1, 32)
    m["l0_alog"] = f("l0_ssd_a_log").reshape(1, 32)
    m["l0_dskip"] = _fm(np.repeat(f("l0_ssd_d"), 64), 8)
    m["l0_ssdnw"] = _fm(inp["l0_ssd_norm_w"], 8)
    m["ret_decay_b"] = f("l1_ret_decay").reshape(1, 8)
    jj = np.arange(128)[:, None].astype(np.float32)
    ii = np.arange(128)[None, :].astype(np.float32)
    m["ret_idx"] = np.ascontiguousarray(np.stack([np.maximum(ii - jj, 0), np.maximum(jj - ii, 0), (jj <= ii).astype(np.float32),
                                                  (jj >= ii).astype(np.float32)], 1).astype(np.float32))
    m["ret_row"] = np.ascontiguousarray(np.stack([np.broadcast_to(ii + 1, (128, 128)), np.broadcast_to(128 - ii, (128, 128))], 1).astype(np.float32))
    m["ret_col"] = np.ascontiguousarray(np.concatenate([127 - jj, jj], 1).astype(np.float32))
    m["ret_state"] = f("state_l1_ret")[bs]
    m["l1_ret_norm"] = _fm(inp["l1_ret_norm_w"], 16)
    t = np.arange(NT)
    row = (t // 64).astype(np.float32)
    col = (t % 64).astype(np.float32)
    freqs = (10000.0 ** (-np.arange(0, 128, 2, dtype=np.float32) / 128)).astype(np.float32)
    ang = np.concatenate([row[:, None] * freqs, col[:, None] * freqs], -1)
    angd = np.repeat(ang, 2, axis=1)
    m["rope_cos"] = np.ascontiguousarray(np.cos(angd).T.reshape(2, 128, NT).transpose(1, 0, 2).astype(np.float32))
    m["rope_sin"] = np.ascontiguousarray(np.sin(angd).T.reshape(2, 128, NT).transpose(1, 0, 2).astype(np.float32))
    P = np.zeros((128, 128), np.float32)
    for i in range(64):
        P[2 * i + 1, 2 * i] = -1.0
        P[2 * i, 2 * i + 1] = 1.0
    m["permP"] = P
    return m


_CACHE = {}


def kernel(**inputs):
    if "b" not in _CACHE:
        _CACHE["b"] = build()
    b = _CACHE["b"]
    in_maps = []
    for core in range(8):
        m = prep_core(inputs, core)
        in_maps.append({k: v for k, v in m.items() if k in b.din})
    res = run_bass_kernel_spmd(b.nc, in_maps, core_ids=list(range(8)))
    R = res.results
    y_prompt = np.concatenate([np.asarray(R[c]["y"][0]).reshape(4, 256, D) for c in range(8)], 0).astype(np.float32)
    y_sample = np.stack([np.asarray(R[c]["y"][1]) for c in range(4)], 0).astype(np.float32)
    new_k = np.concatenate([np.asarray(R[c]["new_k"]).reshape(4, 256, 8, 64) for c in range(8)], 0).astype(np.float32)
    new_v = np.concatenate([np.asarray(R[c]["new_v"]).reshape(4, 256, 8, 64) for c in range(8)], 0).astype(np.float32)
    new_ssd = np.concatenate([np.asarray(R[c]["new_ssd"]) for c in range(8)], 0).astype(np.float32)
    new_ret = np.concatenate([np.asarray(R[c]["new_ret"]) for c in range(8)], 0).astype(np.float32)
    return (y_prompt, y_sample, new_k, new_v, new_ssd, new_ret)
```

```python
import numpy as np
import concourse.bass as bass
import concourse.mybir as mybir
from concourse.bass_utils import run_bass_kernel_spmd

F32 = mybir.dt.float32
BF16 = mybir.dt.bfloat16
AF = mybir.ActivationFunctionType
ALU = mybir.AluOpType

D = 1024
NT = 1024
KT = 8
FFN_H = 2816
EPS = 1e-6
L0_IN = 4128
L1_IN = 6144


class Sched:
    ENG = ("pe", "act", "dve", "pool", "sp")

    def __init__(self, nc):
        self.nc = nc
        self.q = {k: [] for k in self.ENG}
        self.esem = {k: nc.alloc_semaphore(name=f"s_{k}") for k in self.ENG}
        self.ecnt = {k: 0 for k in self.ENG}
        self.open = {k: False for k in self.ENG}
        self.known = {k: {} for k in self.ENG}
        self.pending = {k: [] for k in self.ENG}
        self.lastw = {}
        self.readers = {}
        self.dsem = {}
        self.dcnt = {}
        self.dpersist = set()
        self.out_sems = set()

    def _need(self, eng, reads, writes, use_pending=True):
        need = {}

        def add(sv, kind):
            sem, val, owner = sv
            if owner == eng and (eng == "pe" or kind == "war"):
                return
            nm = sem.name
            if nm not in need or need[nm][1] < val:
                need[nm] = (sem, val)

        for k in reads:
            if k in self.lastw:
                add(self.lastw[k], "raw")
        for k in writes:
            if k in self.lastw:
                add(self.lastw[k], "waw")
            for sv in self.readers.get(k, {}).values():
                add(sv, "war")
        if use_pending and self.pending[eng]:
            for (sem, val) in self.pending[eng]:
                nm = sem.name
                if nm not in need or need[nm][1] < val:
                    need[nm] = (sem, val)
            self.pending[eng] = []
        out = []
        kn = self.known[eng]
        for nm, (sem, val) in need.items():
            if kn.get(nm, 0) < val:
                kn[nm] = val
                out.append((sem, val))
        return out

    def op(self, eng, fn, reads=(), writes=(), inc=True, persistent=False):
        waits = self._need(eng, reads, writes, use_pending=not persistent)
        if eng == "pool" and self.ecnt[eng] > 0:
            nm = self.esem[eng].name
            if self.known[eng].get(nm, 0) < self.ecnt[eng]:
                self.known[eng][nm] = self.ecnt[eng]
                waits.append((self.esem[eng], self.ecnt[eng]))
        if inc:
            self.ecnt[eng] += 1
            val = self.ecnt[eng]
            self.open[eng] = False
        else:
            val = self.ecnt[eng] + 1
            self.open[eng] = True
        sem = self.esem[eng]
        self.q[eng].append((waits, fn, sem, 1 if inc else 0))
        sv = (sem, val, eng)
        for k in writes:
            self.lastw[k] = sv
            self.readers[k] = {}
        for k in reads:
            self.readers.setdefault(k, {})[eng] = sv

    def dma(self, qeng, out, in_, reads=(), writes=(), semkey=None, is_output=False, persistent=False, **kw):
        waits = self._need(qeng, reads, writes, use_pending=not persistent)
        sk = semkey if semkey is not None else (tuple(writes) + tuple(reads))
        if sk not in self.dsem:
            self.dsem[sk] = self.nc.alloc_semaphore(name=f"d{len(self.dsem)}")
            self.dcnt[sk] = 0
        if persistent:
            self.dpersist.add(sk)
        sem = self.dsem[sk]
        self.dcnt[sk] += 16
        val = self.dcnt[sk]
        if is_output:
            self.out_sems.add(sk)

        def fn(e, out=out, in_=in_, kw=kw):
            return e.dma_start(out=out, in_=in_, **kw)

        self.q[qeng].append((waits, fn, sem, 16))
        sv = (sem, val, "dma:" + str(sk))
        for k in writes:
            self.lastw[k] = sv
            self.readers[k] = {}
        for k in reads:
            self.readers.setdefault(k, {})["dma:" + str(sk)] = sv

    def barrier(self):
        tg = [(self.esem[e], self.ecnt[e]) for e in ("pe", "act", "dve") if self.ecnt[e] > 0]
        tg += [(self.dsem[k], self.dcnt[k]) for k in self.dsem if k not in self.dpersist]
        tg += [(self.esem["pool"], self.ecnt["pool"])] if self.ecnt["pool"] > 0 else []
        for e in ("pe", "act", "dve", "sp", "pool"):
            self.pending[e] = list(tg)

    def finish(self):
        nc = self.nc
        for e in self.ENG:
            assert not self.open[e], e
        fin = [(self.dsem[sk], self.dcnt[sk]) for sk in self.out_sems]
        with nc.Block() as block:
            def emit(name):
                def body(e):
                    for waits, fn, sem, inc in self.q[name]:
                        for (s, v) in waits:
                            e.wait_ge(s, v)
                        inst = fn(e)
                        if inc:
                            inst.then_inc(sem, inc)
                    if name == "sp":
                        for (s, v) in fin:
                            e.wait_ge(s, v)
                return body
            block.tensor(emit("pe"))
            block.scalar(emit("act"))
            block.vector(emit("dve"))
            block.gpsimd(emit("pool"))
            block.sync(emit("sp"))


class B:
    def __init__(self, dbg=None, stages=99, plan=None):
        self.plan = plan
        self.wrec = []
        self.wissued = 0
        self.dbg = dbg or []
        self.stages = stages
        nc = self.nc = bass.Bass("TRN2", target_bir_lowering=False)
        self.S = Sched(nc)
        self.din = {}
        self.dout = {}
        self.psn = 0
        self.ps = [nc.alloc_psum_tensor(f"psb{i}", [128, 512], F32) for i in range(8)]
        self.wn = 0
        self.wslot = [nc.alloc_sbuf_tensor(f"wslot{i}", [128, 2048], BF16) for i in range(self.NSLOT)]

    def inp(self, name, shape):
        t = self.nc.dram_tensor(name, list(shape), F32, kind="ExternalInput").ap()
        self.din[name] = t
        return t

    def outp(self, name, shape, dt=F32):
        t = self.nc.dram_tensor(name, list(shape), dt, kind="ExternalOutput").ap()
        self.dout[name] = t
        return t

    def sb(self, name, shape, dt=F32):
        return self.nc.alloc_sbuf_tensor("s_" + name, list(shape), dt)

    def tmps(self, *specs):
        import contextlib

        @contextlib.contextmanager
        def cm():
            with contextlib.ExitStack() as st:
                yield [st.enter_context(self.tmp(*sp)) for sp in specs]
        return cm()

    def tmp(self, name, shape, dt=F32):
        self.tn = getattr(self, "tn", 0) + 1
        return self.nc.sbuf_tensor(f"t_{name}_{self.tn}", list(shape), dt)

    def PS(self):
        b = self.psn % 8
        self.psn += 1
        return self.ps[b], ("ps", b)

    def MM(self, out, lhsT, rhs, start=True, stop=True, r=(), w=(), inc=True):
        self.S.op("pe", lambda e: e.matmul(out, lhsT=lhsT, rhs=rhs, start=start, stop=stop), reads=r, writes=w, inc=inc)

    def TR(self, out, in_, ident, r=(), w=(), inc=True):
        self.S.op("pe", lambda e: e.transpose(out=out, in_=in_, identity=ident), reads=r, writes=w, inc=inc)

    def ACT(self, out, in_, func, r=(), w=(), **kw):
        self.S.op("act", lambda e: e.activation(out=out, in_=in_, func=func, **kw), reads=r, writes=w)

    def TT(self, eng, out, in0, in1, op, r=(), w=()):
        self.S.op(eng, lambda e: e.tensor_tensor(out=out, in0=in0, in1=in1, op=op), reads=r, writes=w)

    def TS(self, eng, out, in0, s1, s2, op0, op1=None, r=(), w=()):
        if op1 is None:
            self.S.op(eng, lambda e: e.tensor_scalar(out=out, in0=in0, scalar1=s1, scalar2=None, op0=op0), reads=r, writes=w)
        else:
            self.S.op(eng, lambda e: e.tensor_scalar(out=out, in0=in0, scalar1=s1, scalar2=s2, op0=op0, op1=op1), reads=r, writes=w)

    def STT(self, out, in0, scalar, in1, op0, op1, r=(), w=()):
        self.S.op("dve", lambda e: e.scalar_tensor_tensor(out=out, in0=in0, scalar=scalar, in1=in1, op0=op0, op1=op1), reads=r, writes=w)

    def CP(self, eng, out, in_, r=(), w=()):
        if eng == "act":
            self.S.op("act", lambda e: e.copy(out=out, in_=in_), reads=r, writes=w)
        else:
            self.S.op(eng, lambda e: e.tensor_copy(out=out, in_=in_), reads=r, writes=w)

    def RECIP(self, out, in_, r=(), w=()):
        self.S.op("dve", lambda e: e.reciprocal(out=out, in_=in_), reads=r, writes=w)

    def MEMSET(self, eng, ap, val, w=()):
        self.S.op(eng, lambda e: e.memset(ap, val), writes=w)

    def LOAD(self, out, in_, w, **kw):
        self.S.dma("sp", out, in_, writes=w, **kw)

    def STORE(self, out, in_, r, key=None):
        self.S.dma("sp", out, in_, reads=r, is_output=True, semkey=key)

    def dump(self, name, ap, shape, r):
        if name in self.dbg:
            o = self.outp("dbg_" + name, shape, ap.dtype)
            self.S.dma("sp", o, ap, reads=r, is_output=True, semkey=("dbg", name))

    NSLOT, LA = 4, 2

    def _wissue(self, j):
        name, k0, kt, c0, nco = self.plan[j]
        w3 = self.din[name].rearrange("(kt p) c -> p kt c", p=128)[:, k0:k0 + kt, c0:c0 + nco]
        sl = j % self.NSLOT
        slot = self.wslot[sl][:, 0:kt * nco].rearrange("p (k c) -> p k c", k=kt)
        self.S.dma("pool", slot, w3, writes=[("wslot", sl)], persistent=True)

    def wload(self, desc):
        name, k0, kt, c0, nco = desc
        assert kt * nco <= 2048
        i = self.wn
        self.wn += 1
        self.wrec.append(desc)
        if self.plan is None:
            self.plan_tmp = getattr(self, "plan_tmp", [])
            self.plan_tmp.append(desc)
            plan_saved, self.plan = self.plan, self.plan_tmp
            self._wissue(i)
            self.plan = plan_saved
        else:
            assert self.plan[i] == desc, (i, desc, self.plan[i])
            while self.wissued <= min(i + self.LA, len(self.plan) - 1):
                self._wissue(self.wissued)
                self.wissued += 1
        sl = i % self.NSLOT
        slot = self.wslot[sl][:, 0:kt * nco].rearrange("p (k c) -> p k c", k=kt)
        return slot, [("wslot", sl)]

    @staticmethod
    def wview(W, k0, nk, c0, nco):
        return (W.name, k0, nk, c0, nco)

    def proj_fm(self, *a, **kw):
        for _ in self.proj_fm_it(*a, **kw):
            pass

    def proj_fm_it(self, W, c0, ntile, inT, in_keys, nk, evac, tblocks=(0, 1)):
        per = max(1, 2048 // (nk * 128))
        m = 0
        while m < ntile:
            n = min(per, ntile - m)
            slot, wk = self.wload(self.wview(W, 0, nk, c0 + m * 128, n * 128))
            for j in range(n):
                for tb in tblocks:
                    ps, pk = self.PS()
                    for kt in range(nk):
                        self.MM(ps[:, :], slot[:, kt, j * 128:(j + 1) * 128], inT[:, kt, tb * 512:(tb + 1) * 512],
                                start=(kt == 0), stop=(kt == nk - 1), r=wk + list(in_keys), w=[pk], inc=(kt == nk - 1))
                    evac(m + j, tb, ps[:, :], pk)
                    yield
            m += n

    def proj_tm(self, W, c0, nco, inT, in_keys, evac, nk=8):
        assert nk * nco <= 2048
        slot, wk = self.wload(self.wview(W, 0, nk, c0, nco))
        for tt in range(NT // 128):
            ps, pk = self.PS()
            for kt in range(nk):
                self.MM(ps[:, 0:nco], inT[:, kt, tt * 128:(tt + 1) * 128], slot[:, kt, :],
                        start=(kt == 0), stop=(kt == nk - 1), r=wk + list(in_keys), w=[pk], inc=(kt == nk - 1))
            evac(tt, ps[:, 0:nco], pk)


ALL_STAGES = ("l0", "ffn0", "l1", "ffn1")


def build(dbg=None, stages=ALL_STAGES, plan=None):
    if plan is None:
        plan = build(dbg=dbg, stages=stages, plan=[]).wrec
        return build(dbg=dbg, stages=stages, plan=plan)
    b = B(dbg, stages, plan if plan else None)
    nc, S = b.nc, b.S
    I = {}
    for nm, shp in [("xg", (2, NT, D)), ("condT", (128, 8, 2)), ("ident", (128, 128)),
                    ("l0_norm1", (128, 8)), ("l0_norm2", (128, 8)), ("l1_norm1", (128, 8)), ("l1_norm2", (128, 8)),
                    ("final_norm", (128, 8)), ("l0_mod_b", (128, 48)), ("l1_mod_b", (128, 48)),
                    ("l0_mod_w", (D, 6 * D)), ("l1_mod_w", (D, 6 * D)),
                    ("l0_w_in", (D, L0_IN)), ("l0_w_out", (1536, D)), ("l1_w_in", (D, L1_IN)), ("l1_w_out", (2048, D)),
                    ("l0_ffn_w1", (D, FFN_H)), ("l0_ffn_w3", (D, FFN_H)), ("l0_ffn_w2", (FFN_H, D)),
                    ("l1_ffn_w1", (D, FFN_H)), ("l1_ffn_w3", (D, FFN_H)), ("l1_ffn_w2", (FFN_H, D)),
                    ("k_ctx", (256, 512)), ("v_ctx", (256, 512)), ("na_biasG", (128, 8, 16, 64)), ("na_mask", (128, 1024)),
                    ("ssd_state", (2, 16, 128, 64)), ("tri", (128, 5, 128)), ("l0_conv_w", (128, 12, 5)), ("l0_conv_b", (128, 12)),
                    ("l0_dtbias", (1, 32)), ("l0_alog", (1, 32)), ("l0_dskip", (128, 8)), ("l0_ssdnw", (128, 8)),
                    ("ret_decay_b", (1, 8)), ("ret_idx", (128, 4, 128)), ("ret_row", (128, 2, 128)), ("ret_col", (128, 2)),
                    ("ret_state", (2, 4, 256, 512)), ("l1_ret_norm", (128, 16)), ("rope_cos", (128, 2, NT)),
                    ("rope_sin", (128, 2, NT)), ("permP", (128, 128))]:
        I[nm] = b.inp(nm, shp)
    y_out = b.outp("y", (2, NT, D))
    new_ret = b.outp("new_ret", (4, 2, 4, 256, 512))
    new_k = b.outp("new_k", (NT, 512))
    new_v = b.outp("new_v", (NT, 512))
    new_ssd = b.outp("new_ssd", (4, 2, 16, 128, 64))

    ident = b.sb("c_ident", [128, 128], F32)
    identb = b.sb("c_identb", [128, 128], BF16)
    onesb = b.sb("c_onesb", [128, 128], BF16)
    b.LOAD(ident[:, :], I["ident"][:, :], w=["ident"])
    b.CP("dve", identb[:, :], ident[:, :], r=["ident"], w=["identb"])
    b.MEMSET("dve", onesb[:, :], 1.0, w=["onesb"])
    vecs = {}
    for nm, n in [("l0_norm1", 8), ("l0_norm2", 8), ("l1_norm1", 8), ("l1_norm2", 8), ("final_norm", 8),
                  ("l0_mod_b", 48), ("l1_mod_b", 48)]:
        vecs[nm] = b.sb("v_" + nm, [128, n], F32)
        b.LOAD(vecs[nm][:, :], I[nm][:, :], w=["v_" + nm])

    condT = b.sb("condT", [128, 8, 2], F32)
    scb = b.sb("scb", [128, 8, 2], BF16)
    b.LOAD(condT[:, :, :], I["condT"][:, :, :], w=["condT"])
    b.ACT(scb[:, :, :], condT[:, :, :], AF.Silu, r=["condT"], w=["scb"])
    mod = [b.sb(f"mod{l}", [128, 48, 2], F32) for l in range(2)]
    modA = [[[b.sb(f"modA{l}{g}{j}", [128, 8], F32) for j in range(2)] for g in range(2)] for l in range(2)]
    def adaln_it(l):
        W = I[f"l{l}_mod_w"]

        def mk_modA(j):
            sc_i, nw = [(1, f"l{l}_norm1"), (4, f"l{l}_norm2")][j]
            for g in range(2):
                b.S.op("dve", lambda e, o=modA[l][g][j][:, :], i0=mod[l][:, sc_i * 8:(sc_i + 1) * 8, g], i1=vecs[nw][:, :]:
                       e.scalar_tensor_tensor(out=o, in0=i0, scalar=1.0, in1=i1, op0=ALU.add, op1=ALU.mult),
                       reads=[f"mod{l}", "v_" + nw], writes=[f"modA{l}{g}{j}"])
        for c in range(24):
            slot, wk = b.wload(b.wview(W, 0, 8, c * 256, 256))
            for j in range(2):
                ft = c * 2 + j
                ps, pk = b.PS()
                for kt in range(8):
                    b.MM(ps[:, 0:2], slot[:, kt, j * 128:(j + 1) * 128], scb[:, kt, :], start=(kt == 0), stop=(kt == 7),
                         r=wk + ["scb"], w=[pk], inc=(kt == 7))
                b.TS("dve", mod[l][:, ft, :], ps[:, 0:2], vecs[f"l{l}_mod_b"][:, ft:ft + 1], None, ALU.add,
                     r=[pk, f"v_l{l}_mod_b"], w=[f"mod{l}"])
            if c == 7:
                mk_modA(0)
            if c == 23:
                mk_modA(1)
            yield

    def modv(l, g, idx):
        return mod[l][:, idx * 8:(idx + 1) * 8, g]

    xT = b.sb("xT", [128, 8, NT], F32)

    def load_x(g):
        with b.tmp("xtok", [128, 2, D], F32) as xtok:
            for tt in range(8):
                bi = tt % 2
                b.LOAD(xtok[:, bi, :], I["xg"][g, tt * 128:(tt + 1) * 128, :], w=[("xtok", bi)])
                for half in range(2):
                    ps, pk = b.PS()
                    for q in range(4):
                        kt = half * 4 + q
                        b.TR(ps[:, q * 128:(q + 1) * 128], xtok[:, bi, kt * 128:(kt + 1) * 128], ident[:, :],
                             r=[("xtok", bi), "ident"], w=[pk], inc=(q == 3))
                    eng = "dve" if half == 0 else "act"
                    b.CP(eng, xT[:, half * 4:(half + 1) * 4, tt * 128:(tt + 1) * 128],
                         ps[:, :].rearrange("p (q t) -> p q t", q=4), r=[pk], w=[("xT", tt)])
            S.barrier()

    XK = [("xT", tt) for tt in range(8)]

    def rstd_block(src, src_keys, tb, nkt, rs, rs_key, inv_n):
        ps, pk = b.PS()
        with b.tmp("sqt", [128, 2, 512], BF16) as sq:
            for kt in range(nkt):
                bi = kt % 2
                b.ACT(sq[:, bi, :], src[:, kt, tb * 512:(tb + 1) * 512], AF.Square, r=src_keys, w=[("sq", bi)])
                b.MM(ps[:, :], onesb[:, :], sq[:, bi, :], start=(kt == 0), stop=(kt == nkt - 1),
                     r=[("sq", bi), "onesb"], w=[pk], inc=True)
        b.ACT(rs, ps[:, :], AF.Sqrt, r=[pk], w=[rs_key], scale=inv_n, bias=EPS)
        b.RECIP(rs, rs, r=[rs_key], w=[rs_key])

    def norm_mod(l, g, j, hT, hkey):
        A = modA[l][g][j]
        Bv = modv(l, g, 0 if j == 0 else 3)
        with b.tmp("rs", [128, 512], F32) as rs, b.tmp("ntmp", [128, 2, 512], F32) as tmp:
            for tb in range(2):
                rstd_block(xT, XK, tb, 8, rs[:, :], ("rs", tb), 1.0 / D)
                for kt in range(8):
                    bi = kt % 2
                    b.TT("dve", tmp[:, bi, :], xT[:, kt, tb * 512:(tb + 1) * 512], rs[:, :], ALU.mult,
                         r=XK + [("rs", tb)], w=[("ntmp", bi)])
                    b.ACT(hT[:, kt, tb * 512:(tb + 1) * 512], tmp[:, bi, :], AF.Identity,
                          r=[("ntmp", bi), f"modA{l}{g}{j}", f"mod{l}"], w=[hkey], scale=A[:, kt:kt + 1], bias=Bv[:, kt:kt + 1])
        S.barrier()

    def resid_evac(l, g, gi):
        gate = modv(l, g, gi)

        def ev(m, tb, ps, pk):
            b.STT(xT[:, m, tb * 512:(tb + 1) * 512], ps, gate[:, m:m + 1], xT[:, m, tb * 512:(tb + 1) * 512],
                  ALU.mult, ALU.add, r=[pk, f"mod{l}"] + XK, w=XK)
        return ev

    def ffn(l, g, other=None):
        def tick():
            if other is not None:
                try:
                    next(other)
                except StopIteration:
                    pass
        with b.tmp("h2T", [128, 8, NT], BF16) as h2T, b.tmp("gT", [128, 22, NT], BF16) as gT, \
                b.tmp("s1", [128, 2, 512], F32) as s1:
            norm_mod(l, g, 1, h2T, "h2T")
            W1, W3, W2 = I[f"l{l}_ffn_w1"], I[f"l{l}_ffn_w3"], I[f"l{l}_ffn_w2"]
            cnt = [0]
            for c in range(11):
                s_1, k1 = b.wload(b.wview(W1, 0, 8, c * 256, 256))
                s_3, k3 = b.wload(b.wview(W3, 0, 8, c * 256, 256))
                for j in range(2):
                    m = c * 2 + j
                    for tb in range(2):
                        p1, pk1 = b.PS()
                        p3, pk3 = b.PS()
                        for kt in range(8):
                            b.MM(p1[:, :], s_1[:, kt, j * 128:(j + 1) * 128], h2T[:, kt, tb * 512:(tb + 1) * 512],
                                 start=(kt == 0), stop=(kt == 7), r=k1 + ["h2T"], w=[pk1], inc=(kt == 7))
                        for kt in range(8):
                            b.MM(p3[:, :], s_3[:, kt, j * 128:(j + 1) * 128], h2T[:, kt, tb * 512:(tb + 1) * 512],
                                 start=(kt == 0), stop=(kt == 7), r=k3 + ["h2T"], w=[pk3], inc=(kt == 7))
                        bi = cnt[0] % 2
                        cnt[0] += 1
                        b.ACT(s1[:, bi, :], p1[:, :], AF.Silu, r=[pk1], w=[("s1", bi)])
                        b.TT("dve", gT[:, m, tb * 512:(tb + 1) * 512], s1[:, bi, :], p3[:, :], ALU.mult,
                             r=[("s1", bi), pk3], w=[("gT", m)])
                tick()
                tick()
            ev = resid_evac(l, g, 5)
            GK = [("gT", m) for m in range(22)]
            for mo in range(8):
                sa, ka = b.wload(b.wview(W2, 0, 11, mo * 128, 128))
                sb_, kb = b.wload(b.wview(W2, 11, 11, mo * 128, 128))
                for tb in range(2):
                    ps, pk = b.PS()
                    for kt in range(22):
                        sl, kk = (sa, ka) if kt < 11 else (sb_, kb)
                        b.MM(ps[:, :], sl[:, kt % 11, :], gT[:, kt, tb * 512:(tb + 1) * 512], start=(kt == 0), stop=(kt == 21),
                             r=kk + GK, w=[pk], inc=(kt == 21))
                    ev(mo, tb, ps[:, :], pk)
                tick()
            if other is not None:
                for _ in other:
                    pass
            S.barrier()

    def final_out(g, gnext=None):
        fw = vecs["final_norm"]
        with b.tmp("rs", [128, 512], F32) as rs, b.tmp("ytmp", [128, 2, 512], F32) as tmp, \
                b.tmp("ytok", [128, 2, D], F32) as ytok, b.tmp("xtokn", [128, 2, D if gnext is not None else 2], F32) as xtok:
            for tb in range(2):
                rstd_block(xT, XK, tb, 8, rs[:, :], ("rs", tb), 1.0 / D)
                for kt in range(8):
                    b.STT(xT[:, kt, tb * 512:(tb + 1) * 512], xT[:, kt, tb * 512:(tb + 1) * 512], fw[:, kt:kt + 1], rs[:, :],
                          ALU.mult, ALU.mult, r=XK + ["v_final_norm", ("rs", tb)], w=XK)
            for tt in range(8):
                bi = tt % 2
                for half in range(2):
                    ps, pk = b.PS()
                    for q in range(4):
                        kt = half * 4 + q
                        b.TR(ps[:, q * 128:(q + 1) * 128], xT[:, kt, tt * 128:(tt + 1) * 128], ident[:, :],
                             r=[("xT", tt), "ident"], w=[pk], inc=(q == 3))
                    eng = "dve" if half == 0 else "act"
                    b.CP(eng, ytok[:, bi, half * 512:(half + 1) * 512], ps[:, :], r=[pk], w=[("ytok", bi, half)])
                b.STORE(y_out[g, tt * 128:(tt + 1) * 128, :], ytok[:, bi, :], r=[("ytok", bi, 0), ("ytok", bi, 1)], key=("ytok", bi))
                if gnext is not None:
                    if tt == 0:
                        b.LOAD(xtok[:, 0, :], I["xg"][gnext, 0:128, :], w=[("xtokn", 0)])
                    if tt + 1 < 8:
                        b.LOAD(xtok[:, (tt + 1) % 2, :], I["xg"][gnext, (tt + 1) * 128:(tt + 2) * 128, :], w=[("xtokn", (tt + 1) % 2)])
                    for half in range(2):
                        ps, pk = b.PS()
                        for q in range(4):
                            kt = half * 4 + q
                            b.TR(ps[:, q * 128:(q + 1) * 128], xtok[:, bi, kt * 128:(kt + 1) * 128], ident[:, :],
                                 r=[("xtokn", bi), "ident"], w=[pk], inc=(q == 3))
                        eng = "act" if half == 0 else "dve"
                        b.CP(eng, xT[:, half * 4:(half + 1) * 4, tt * 128:(tt + 1) * 128],
                             ps[:, :].rearrange("p (q t) -> p q t", q=4), r=[pk], w=[("xT", tt)])
            S.barrier()

    l0c = {}
    if "l0" in stages:
        l0c["tri"] = b.sb("tri", [128, 5, 128], F32)
        l0c["trib"] = b.sb("trib", [128, 5, 128], BF16)
        l0c["conv_w"] = b.sb("convw", [128, 12, 5], F32)
        l0c["conv_b"] = b.sb("convb", [128, 12], F32)
        l0c["dtbias"] = b.sb("dtbias", [128, 32], F32)
        l0c["negA"] = b.sb("negA", [128, 32], F32)
        l0c["dskip"] = b.sb("dskip", [128, 8], F32)
        l0c["ssdnw"] = b.sb("ssdnw", [128, 8], F32)
        b.LOAD(l0c["tri"][:, :, :], I["tri"][:, :, :], w=["tri"])
        b.CP("dve", l0c["trib"][:, :, :], l0c["tri"][:, :, :], r=["tri"], w=["tri"])
        l0c["trir"] = b.sb("trir", [128, 5, 128], mybir.dt.float32r)
        b.CP("dve", l0c["trir"][:, :, :], l0c["tri"][:, :, :], r=["tri"], w=["tri"])
        b.LOAD(l0c["conv_w"][:, :, :], I["l0_conv_w"][:, :, :], w=["l0c"], semkey="l0c1")
        b.LOAD(l0c["conv_b"][:, :], I["l0_conv_b"][:, :], w=["l0c"], semkey="l0c2")
        b.LOAD(l0c["dtbias"][:, :], I["l0_dtbias"][0:1, :].partition_broadcast(128), w=["l0c"], semkey="l0c3")
        b.LOAD(l0c["negA"][:, :], I["l0_alog"][0:1, :].partition_broadcast(128), w=["l0c"], semkey="l0c4")
        b.LOAD(l0c["dskip"][:, :], I["l0_dskip"][:, :], w=["l0c"], semkey="l0c5")
        b.LOAD(l0c["ssdnw"][:, :], I["l0_ssdnw"][:, :], w=["l0c"], semkey="l0c6")
        b.ACT(l0c["negA"][:, :], l0c["negA"][:, :], AF.Exp, r=["l0c"], w=["l0c"])
        b.TS("dve", l0c["negA"][:, :], l0c["negA"][:, :], -1.0, None, ALU.mult, r=["l0c"], w=["l0c"])

    ret = {}
    if "l1" in stages:
        lgb = b.sb("lgb", [128, 8], F32)
        ridx = b.sb("ridx", [128, 4, 128], F32)
        rrow = b.sb("rrow", [128, 2, 128], F32)
        rcol = b.sb("rcol", [128, 2], F32)
        ret["normw"] = b.sb("retnw", [128, 16], F32)
        ret["permP"] = b.sb("permP", [128, 128], F32)
        ret["dtot"] = b.sb("dtot", [128, 4, 128], BF16)
        ret["ecs"] = b.sb("ecs", [128, 8, 128], BF16)
        ret["tail"] = b.sb("rtail", [128, 8], F32)
        ret["etot"] = b.sb("retot", [128, 8], F32)
        b.LOAD(lgb[:, :], I["ret_decay_b"][0:1, :].partition_broadcast(128), w=["lgb"])
        b.LOAD(ridx[:, :, :], I["ret_idx"][:, :, :], w=["ridx"])
        b.LOAD(rrow[:, :, :], I["ret_row"][:, :, :], w=["rrow"])
        b.LOAD(rcol[:, :], I["ret_col"][:, :], w=["rcol"])
        b.LOAD(ret["normw"][:, :], I["l1_ret_norm"][:, :], w=["retnw"])
        b.LOAD(ret["permP"][:, :], I["permP"][:, :], w=["permP"])
        ret["permPr"] = b.sb("permPr", [128, 128], mybir.dt.float32r)
        b.CP("dve", ret["permPr"][:, :], ret["permP"][:, :], r=["permP"], w=["permP"])
        b.ACT(lgb[:, :], lgb[:, :], AF.Exp, r=["lgb"], w=["lgb"], scale=-1.0)
        b.ACT(lgb[:, :], lgb[:, :], AF.Ln, r=["lgb"], w=["lgb"], bias=1.0)
        b.TS("dve", lgb[:, :], lgb[:, :], -1.0, None, ALU.mult, r=["lgb"], w=["lgb"])
        with b.tmp("rtmp", [128, 2, 128], F32) as rtmp:
            for hd in range(4):
                for d in range(2):
                    b.ACT(rtmp[:, d, :], ridx[:, d, :], AF.Exp, r=["ridx", "lgb"], w=[("rtmp", d)], scale=lgb[:, d * 4 + hd:d * 4 + hd + 1])
                    b.TT("dve", rtmp[:, d, :], rtmp[:, d, :], ridx[:, 2 + d, :], ALU.mult, r=[("rtmp", d), "ridx"], w=[("rtmp", d)])
                    b.ACT(ret["ecs"][:, d * 4 + hd, :], rrow[:, d, :], AF.Exp, r=["rrow", "lgb"], w=["retecs"], scale=lgb[:, d * 4 + hd:d * 4 + hd + 1])
                b.TT("dve", ret["dtot"][:, hd, :], rtmp[:, 0, :], rtmp[:, 1, :], ALU.add, r=[("rtmp", 0), ("rtmp", 1)], w=["retdtot"])
            for d in range(2):
                b.ACT(ret["tail"][:, d * 4:(d + 1) * 4], lgb[:, d * 4:(d + 1) * 4], AF.Exp, r=["lgb", "rcol"], w=["rettail"], scale=rcol[:, d:d + 1])
            b.ACT(ret["etot"][:, :], lgb[:, :], AF.Exp, r=["lgb"], w=["retetot"], scale=128.0)
            S.barrier()

    b.env = dict(ret=ret, new_ret=new_ret, l0c=l0c, new_k=new_k, new_v=new_v, new_ssd=new_ssd, I=I, vecs=vecs, ident=ident, identb=identb, onesb=onesb, mod=mod, modv=modv, xT=xT, XK=XK,
                 rstd_block=rstd_block, norm_mod=norm_mod, resid_evac=resid_evac)

    groups = [0] if 'g0' in stages else [1] if 'g1' in stages else [0, 1]
    import itertools
    load_x(groups[0])
    ada0 = adaln_it(0)
    for _ in range(8):
        next(ada0)
    ada1 = itertools.chain(ada0, adaln_it(1))
    for gi, g in enumerate(groups):
        if "l0" in stages:
            layer0(b, g, other=ada1 if gi == 0 else None)
        if gi == 0:
            for _ in ada1:
                pass
        if "ffn0" in stages:
            ffn(0, g)
        if "l1" in stages:
            layer1(b, g)
        if "ffn1" in stages:
            ffn(1, g)
        final_out(g, groups[gi + 1] if gi + 1 < len(groups) else None)
    S.finish()
    return b


def layer0(b, g, other=None):
    nc, S, E = b.nc, b.S, b.env

    def tick():
        if other is not None:
            try:
                next(other)
            except StopIteration:
                pass
    I, xT, XK, onesb, identb, ident = E["I"], E["xT"], E["XK"], E["onesb"], E["identb"], E["ident"]
    is_s = (g == 1)
    W = I["l0_w_in"]
    Wo = I["l0_w_out"]
    C0 = E["l0c"]
    nseq, CS, L = (1, 8, 1024) if is_s else (4, 2, 256)
    gate = E["modv"](0, g, 2)
    cnt = [0]

    def nxt():
        cnt[0] += 1
        return cnt[0] % 2
    with b.tmp("mixT", [128, 12, NT], BF16) as mixT:
        if "nona" in b.stages or "nossd" in b.stages:
            b.MEMSET("dve", mixT[:, :, :], 0.0, w=["mixT"])
        with b.tmp("szT", [128, 8, NT], BF16) as szT, b.tmp("xcT", [128, 12, NT], BF16) as xcT, \
                b.tmp("dta", [128, 2, 8, 32], F32) as dta, b.tmp("teall", [128, 8, 64], F32) as teall, \
                b.tmp("ahl", [128, 8, 2, 32], BF16) as ahl:
            dt_tok, a_tok = dta[:, 0, :, :], dta[:, 1, :, :]
            with b.tmp("qT", [128, 4, NT], BF16) as qT, b.tmp("kT", [128, 4, NT], BF16) as kT, \
                    b.tmp("vtok", [128, 8, 512], BF16) as vtok, b.tmp("ostg", [128, 2, 256], F32) as ostg, \
                    b.tmp("Eb", [128, 2, 512], BF16) as Eb, b.tmp("rden", [128, 2, 256], F32) as rden:
                with b.tmp("hT", [128, 8, NT], BF16) as hT, b.tmp("raw", [128, 2, 1024 + 4 * nseq], BF16) as raw, b.tmp("DG", [128, 2, 5, 128], BF16) as DG, \
                        b.tmp("dtt", [128, 4, 32], F32) as dtt:
                    E["norm_mod"](0, g, 0, hT, "hT")

                    def cp_evac(dst, dkey):
                        def ev(m, tb, ps, pk):
                            b.CP("act" if (m + tb) % 2 else "dve", dst[:, m, tb * 512:(tb + 1) * 512], ps, r=[pk], w=[dkey])
                        return ev
                    if "noqk" not in b.stages:
                        b.proj_fm(W, 0, 4, hT, ["hT"], 8, cp_evac(qT, "qT"))
                        b.proj_fm(W, 512, 4, hT, ["hT"], 8, cp_evac(kT, "kT"))
                    for half in range(2 if "nov" not in b.stages else 0):
                        def v_evac(tt, ps, pk, half=half):
                            if is_s:
                                b.CP("act", vtok[:, tt, half * 256:(half + 1) * 256], ps, r=[pk], w=[("vtok", tt)])
                            else:
                                i = nxt()
                                b.CP("dve", ostg[:, i, :], ps, r=[pk], w=[("ostg", i)])
                                b.CP("act", vtok[:, tt, half * 256:(half + 1) * 256], ostg[:, i, :], r=[("ostg", i)], w=[("vtok", tt)])
                                b.STORE(E["new_v"][tt * 128:(tt + 1) * 128, half * 256:(half + 1) * 256], ostg[:, i, :], r=[("ostg", i)], key=("ostg", i))
                        b.proj_tm(W, 1024 + half * 256, 256, hT, ["hT"], v_evac)
                    if not is_s:
                        for half in range(2 if "nok" not in b.stages else 0):
                            def k_evac(tt, ps, pk, half=half):
                                i = nxt()
                                b.CP("dve", ostg[:, i, :], ps, r=[pk], w=[("ostg", i)])
                                b.STORE(E["new_k"][tt * 128:(tt + 1) * 128, half * 256:(half + 1) * 256], ostg[:, i, :], r=[("ostg", i)], key=("ostg", i))
                            b.proj_tm(W, 512 + half * 256, 256, hT, ["hT"], k_evac)

                    def z_evac(m, tb, ps, pk):
                        b.ACT(szT[:, m, tb * 512:(tb + 1) * 512], ps, AF.Silu, r=[pk], w=["szT"])
                    if "noz" not in b.stages:
                        b.proj_fm(W, 1536, 8, hT, ["hT"], 8, z_evac)
                    for i in range(2):
                        b.MEMSET("dve", raw[:, i, :], 0.0, w=[("raw", i)])
                    spb = nseq // 2 if nseq > 1 else 1

                    def x_evac(m, tb, ps, pk):
                        i = m % 2
                        r3 = raw[:, i, :].rearrange("p (s l) -> p s l", s=nseq)
                        if nseq == 1:
                            b.CP("act", raw[:, i, 2 + tb * 512:2 + (tb + 1) * 512], ps, r=[pk], w=[("raw", i)])
                        else:
                            b.CP("act", r3[:, tb * 2:(tb + 1) * 2, 2:2 + L], ps.rearrange("p (s l) -> p s l", s=2), r=[pk], w=[("raw", i)])
                        if tb == 1:
                            b.TT("dve", DG[:, i, :, :], identb[:, :].unsqueeze(1).to_broadcast([128, 5, 128]),
                                 C0["conv_w"][:, m, :].unsqueeze(2).to_broadcast([128, 5, 128]), ALU.mult, r=["identb", "l0c"], w=[("DG", i)])
                            for t2 in range(2):
                                pc, pkc = b.PS()
                                if nseq == 1:
                                    for k in range(5):
                                        b.MM(pc[:, :], DG[:, i, k, :], raw[:, i, t2 * 512 + k:t2 * 512 + k + 512], start=(k == 0), stop=(k == 4),
                                             r=[("raw", i), ("DG", i)], w=[pkc], inc=(k == 4))
                                else:
                                    for sq in range(2):
                                        for k in range(5):
                                            b.MM(pc[:, sq * 256:(sq + 1) * 256], DG[:, i, k, :], r3[:, t2 * 2 + sq, k:k + L], start=(k == 0), stop=(k == 4),
                                                 r=[("raw", i), ("DG", i)], w=[pkc], inc=(sq == 1 and k == 4))
                                b.ACT(xcT[:, m, t2 * 512:(t2 + 1) * 512], pc[:, :], AF.Silu, r=[pkc, "l0c"], w=["xcT"], bias=C0["conv_b"][:, m:m + 1])
                    if "nox" not in b.stages:
                        b.proj_fm(W, 2560, 12, hT, ["hT"], 8, x_evac)

                    def dt_evac(tt, ps, pk):
                        b.TT("dve", dt_tok[:, tt, :], ps, C0["dtbias"][:, :], ALU.add, r=[pk, "l0c"], w=["dt"])
                        u_, s_, s2, p_ = dtt[:, 0, :], dtt[:, 1, :], dtt[:, 2, :], dtt[:, 3, :]
                        K_ = ["dtt"]
                        b.ACT(u_, dt_tok[:, tt, :], AF.Exp, r=["dt"], w=K_)
                        b.TS("dve", s_, u_, 2.0, None, ALU.add, r=K_, w=K_)
                        b.RECIP(s_, s_, r=K_, w=K_)
                        b.TT("dve", s_, s_, u_, ALU.mult, r=K_, w=K_)
                        b.TT("dve", s2, s_, s_, ALU.mult, r=K_, w=K_)
                        b.TS("dve", p_, s2, 1.0 / 11.0, 1.0 / 9.0, ALU.mult, ALU.add, r=K_, w=K_)
                        for cf in (1.0 / 7.0, 1.0 / 5.0, 1.0 / 3.0, 1.0):
                            b.TT("dve", p_, p_, s2, ALU.mult, r=K_, w=K_)
                            b.TS("dve", p_, p_, cf, None, ALU.add, r=K_, w=K_)
                        b.TT("dve", p_, p_, s_, ALU.mult, r=K_, w=K_)
                        b.TS("dve", dt_tok[:, tt, :], p_, 2.0, None, ALU.mult, r=K_, w=["dt"])
                        b.TT("dve", a_tok[:, tt, :], dt_tok[:, tt, :], C0["negA"][:, :], ALU.mult, r=["dt", "l0c"], w=["dt"])
                        b.CP("dve", ahl[:, tt, 0, :], a_tok[:, tt, :], r=["dt"], w=["dt"])
                        b.TT("dve", ahl[:, tt, 1, :], a_tok[:, tt, :], ahl[:, tt, 0, :], ALU.subtract, r=["dt"], w=["dt"])
                    if "nodt" not in b.stages:
                        b.proj_tm(W, 4096, 32, hT, ["hT"], dt_evac)
                    b.dump(f"h{g}", hT[:, :, :], (128, 8, NT), ["hT"])
                    b.dump(f"xc{g}", xcT[:, :, :], (128, 12, NT), ["xcT"])
                    b.dump(f"sz{g}", szT[:, :, :], (128, 8, NT), ["szT"])
                    b.dump(f"dt{g}", dta[:, :, :, :], (128, 2, 8, 32), ["dt"])
                    S.barrier()
                if not is_s:
                    def ctx_s1(s, h, i):
                        hp, ht = (h % 2) * 64, h // 2
                        tq = slice(s * 256, (s + 1) * 256)
                        ps, pk = b.PS()
                        for c in range(2):
                            tkk = slice(s * 256 + c * 128, s * 256 + (c + 1) * 128)
                            b.MM(ps[:, c * 256:(c + 1) * 256], kT[hp:hp + 64, ht, tkk], qT[hp:hp + 64, ht, tq], r=["kT", "qT"], w=[pk], inc=(c == 1))
                        b.ACT(Eb[:, i, :], ps[:, :], AF.Exp, r=[pk], w=[("Eb", i)], scale=0.125)

                    def ctx_s2(s, h, i):
                        hp, ht = (h % 2) * 64, h // 2
                        tq = slice(s * 256, (s + 1) * 256)
                        po, pko = b.PS()
                        for c in range(2):
                            b.MM(po[hp:hp + 64, 0:256], vtok[:, s * 2 + c, h * 64:(h + 1) * 64], Eb[:, i, c * 256:(c + 1) * 256], start=(c == 0), stop=(c == 1),
                                 r=[("vtok", s * 2 + c), ("Eb", i)], w=[pko], inc=False)
                        for c in range(2):
                            b.MM(po[hp:hp + 64, 256:512], onesb[:, 0:64], Eb[:, i, c * 256:(c + 1) * 256], start=(c == 0), stop=(c == 1),
                                 r=["onesb", ("Eb", i)], w=[pko], inc=(c == 1))
                        b.ACT(rden[hp:hp + 64, i, :], po[hp:hp + 64, 256:512], AF.Ln, r=[pko], w=[("rden", i)])
                        b.ACT(rden[hp:hp + 64, i, :], rden[hp:hp + 64, i, :], AF.Exp, r=[("rden", i)], w=[("rden", i)], scale=-1.0)
                        b.TT("dve", mixT[hp:hp + 64, ht, tq], po[hp:hp + 64, 0:256], rden[hp:hp + 64, i, :], ALU.mult, r=[pko, ("rden", i)], w=["mixT"])
                    its = [(s, h) for s in range(4 if "nona" not in b.stages else 0) for h in range(8)]
                    for n in range(len(its) + 1):
                        if n < len(its):
                            ctx_s1(its[n][0], its[n][1], n % 2)
                        if n >= 1:
                            ctx_s2(its[n - 1][0], its[n - 1][1], (n - 1) % 2)
                        tick()
                else:
                    with b.tmp("kcT", [128, 4, 256], BF16) as kcT, b.tmp("vc", [128, 2, 512], BF16) as vc, \
                            b.tmp("expB", [128, 8, 16, 64], BF16) as expB:
                      with b.tmp("ctmp", [128, 2, 1024], F32) as ctmp, b.tmp("namask", [128, 1024], F32) as nmask:
                        for c in range(2):
                            b.LOAD(ctmp[:, 0, 0:512], I["k_ctx"][c * 128:(c + 1) * 128, :], w=[("ctmp", 0)])
                            b.LOAD(ctmp[:, 1, 0:512], I["v_ctx"][c * 128:(c + 1) * 128, :], w=[("ctmp", 1)])
                            b.CP("dve", vc[:, c, :], ctmp[:, 1, 0:512], r=[("ctmp", 1)], w=["vc"])
                            ps, pk = b.PS()
                            for m in range(4):
                                b.TR(ps[:, m * 128:(m + 1) * 128], ctmp[:, 0, m * 128:(m + 1) * 128], ident[:, :], r=[("ctmp", 0), "ident"], w=[pk], inc=(m == 3))
                            b.CP("act", kcT[:, :, c * 128:(c + 1) * 128], ps[:, :].rearrange("p (m t) -> p m t", m=4), r=[pk], w=["kcT"])
                        b.LOAD(nmask[:, :], I["na_mask"][:, :], w=["na_mask"])
                        for h in range(8):
                            i = h % 2
                            b.LOAD(ctmp[:, i, :], I["na_biasG"][:, h].rearrange("p d c -> p (d c)"), w=[("ctmp", i)])
                            b.ACT(ctmp[:, i, :], ctmp[:, i, :], AF.Exp, r=[("ctmp", i)], w=[("ctmp", i)])
                            b.TT("dve", expB[:, h, :, :].rearrange("p d c -> p (d c)"), ctmp[:, i, :], nmask[:, :], ALU.mult,
                                 r=[("ctmp", i), "na_mask"], w=["expB"])
                        S.barrier()
                      if True:
                        def lat_info(r_):
                            r0 = min(max(r_ - 4, 0), 8)
                            kcs = list(range(r0 // 2, (r0 + 7) // 2 + 1))
                            return r0, kcs, len(kcs), 2 * kcs[0] - r_ + 8

                        def lat_s1(r_, h, i):
                            r0, kcs, nl, d0 = lat_info(r_)
                            hp, ht = (h % 2) * 64, h // 2
                            tq = slice(r_ * 64, (r_ + 1) * 64)
                            ps, pk = b.PS()
                            n = nl + 2
                            for mi, kc in enumerate(kcs):
                                b.MM(ps[:, mi * 64:(mi + 1) * 64], kT[hp:hp + 64, ht, kc * 128:(kc + 1) * 128], qT[hp:hp + 64, ht, tq], r=["kT", "qT"], w=[pk], inc=False)
                            for cc in range(2):
                                b.MM(ps[:, (nl + cc) * 64:(nl + cc + 1) * 64], kcT[hp:hp + 64, ht, cc * 128:(cc + 1) * 128], qT[hp:hp + 64, ht, tq], r=["kcT", "qT"], w=[pk], inc=(cc == 1))
                            b.ACT(Eb[:, i, 0:n * 64], ps[:, 0:n * 64], AF.Exp, r=[pk], w=[("Eb", i)], scale=0.125)
                            b.TT("dve", Eb[:, i, 0:nl * 64].rearrange("p (m c) -> p m c", m=nl), Eb[:, i, 0:nl * 64].rearrange("p (m c) -> p m c", m=nl),
                                 expB[:, h, d0:d0 + 2 * nl - 1:2, :], ALU.mult, r=[("Eb", i), "expB"], w=[("Eb", i)])

                        def lat_s2(r_, h, i):
                            r0, kcs, nl, d0 = lat_info(r_)
                            hp, ht = (h % 2) * 64, h // 2
                            tq = slice(r_ * 64, (r_ + 1) * 64)
                            n = nl + 2
                            po, pko = b.PS()
                            for which in range(2):
                                for mi in range(n):
                                    if mi < nl:
                                        kc = kcs[mi]
                                        e0 = r0 <= 2 * kc <= r0 + 7
                                        e1 = r0 <= 2 * kc + 1 <= r0 + 7
                                        lo, hi = (0 if e0 else 64), (128 if e1 else 64)
                                        lhs = vtok[lo:hi, kc, h * 64:(h + 1) * 64] if which == 0 else onesb[lo:hi, 0:64]
                                        rk = ("vtok", kc)
                                    else:
                                        lo, hi = 0, 128
                                        lhs = vc[:, mi - nl, h * 64:(h + 1) * 64] if which == 0 else onesb[:, 0:64]
                                        rk = "vc"
                                    b.MM(po[hp:hp + 64, which * 64:(which + 1) * 64], lhs, Eb[lo:hi, i, mi * 64:(mi + 1) * 64], start=(mi == 0), stop=(mi == n - 1),
                                         r=[rk, "onesb", ("Eb", i)], w=[pko], inc=(which == 1 and mi == n - 1))
                            b.ACT(rden[hp:hp + 64, i, 0:64], po[hp:hp + 64, 64:128], AF.Ln, r=[pko], w=[("rden", i)])
                            b.ACT(rden[hp:hp + 64, i, 0:64], rden[hp:hp + 64, i, 0:64], AF.Exp, r=[("rden", i)], w=[("rden", i)], scale=-1.0)
                            b.TT("dve", mixT[hp:hp + 64, ht, tq], po[hp:hp + 64, 0:64], rden[hp:hp + 64, i, 0:64], ALU.mult, r=[pko, ("rden", i)], w=["mixT"])
                        its = [(r_, h) for r_ in range(16 if "nona" not in b.stages else 0) for h in range(8)]
                        for n_ in range(len(its) + 1):
                            if n_ < len(its):
                                lat_s1(its[n_][0], its[n_][1], n_ % 2)
                            if n_ >= 1:
                                lat_s2(its[n_ - 1][0], its[n_ - 1][1], (n_ - 1) % 2)
                            if n_ % 4 == 0:
                                tick()
                        S.barrier()
                S.barrier()
            b.dump(f"att{g}", mixT[:, 0:4, :], (128, 4, NT), ["mixT"])
            with b.tmps(("xtok", [128, 2, 1024], BF16), ("Btok", [128, 2, 256], BF16), ("Sst", [128, 2, 1024], F32),
                        ("Sfb", [128, 1024], BF16), ("Sbin", [128, CS, 1024], BF16), ("R1", [128, 2, 2, 512], mybir.dt.float32r),
                        ("DE", [128, 2, 2, 2, 512], BF16), ("MC", [128, 2, 2, 2, 512], BF16), ("vd", [128, 2, 2, 256], BF16),
                        ("vt", [128, 1, 1024], BF16), ("Gm", [128, 2, 2, 2, 128], BF16), ("yz", [128, 2, 512], F32),
                        ("sq", [128, 2, 512], BF16), ("rs", [128, 2, 128], F32), ("wdt", [128, 2, 32], F32),
                        ("stmp", [128, 1, 512], F32)) as (xtok2, Btok2, Sst, Sfb, Sbin, R1, DE, MC, vd, vtl, Gm, yz, sqb, rs, wdt, stmp):
                tri, trib = C0["tri"], C0["trib"]
                tokc = [0]

                def tok_tiles(c):
                    tokc[0] += 1
                    i = tokc[0] % 2
                    tk = slice(c * 128, (c + 1) * 128)
                    ps, pk = b.PS()
                    psb = ps[:, :].bitcast(BF16)
                    for m in range(8):
                        b.TR(psb[:, m * 128:(m + 1) * 128], xcT[:, m, tk], identb[:, :], r=["xcT", "identb"], w=[pk], inc=(m == 7))
                    b.CP("act" if c % 2 else "dve", xtok2[:, i, :], psb[:, :], r=[pk], w=[("xtok", i)])
                    ps, pk = b.PS()
                    psb = ps[:, :].bitcast(BF16)
                    for m in range(2):
                        b.TR(psb[:, m * 128:(m + 1) * 128], xcT[:, 8 + m, tk], identb[:, :], r=["xcT", "identb"], w=[pk], inc=(m == 1))
                    b.CP("dve" if c % 2 else "act", Btok2[:, i, :], psb[:, 0:256], r=[pk], w=[("Btok", i)])
                    return i
                for c in range(8 if "notails" not in b.stages else 0):
                    pt, pkt = b.PS()
                    for hl in range(2):
                        b.MM(pt[:, 0:16], trib[:, 3, :], ahl[:, c, hl, 0:16], start=(hl == 0), stop=(hl == 1), r=["dt", "tri"], w=[pkt], inc=False)
                    for hl in range(2):
                        b.MM(pt[:, 16:32], trib[:, 2, :], ahl[:, c, hl, 16:32], start=(hl == 0), stop=(hl == 1), r=["dt", "tri"], w=[pkt], inc=False)
                    for hl in range(2):
                        b.MM(pt[:, 32:64], trib[:, 4, :], ahl[:, c, hl, 0:32], start=(hl == 0), stop=(hl == 1), r=["dt", "tri"], w=[pkt], inc=(hl == 1))
                    b.ACT(teall[:, c, :], pt[:, 0:64], AF.Exp, r=[pkt], w=[("te", c)])

                def upd(d, c, bi):
                    i = nxt()
                    vi = 0
                    b.TT("dve", wdt[:, i, 0:16], dt_tok[:, c, d * 16:(d + 1) * 16], teall[:, c, d * 16:(d + 1) * 16], ALU.mult, r=["dt", ("te", c)], w=[("wdt", i)])
                    b.TT("dve", vtl[:, vi, :].rearrange("p (h v) -> p h v", h=16), xtok2[:, bi, :].rearrange("p (h v) -> p h v", h=16),
                         wdt[:, i, 0:16].unsqueeze(2).to_broadcast([128, 16, 64]), ALU.mult, r=[("xtok", bi), ("wdt", i)], w=[("vtl", vi)])
                    for grp in range(2):
                        gs = slice(grp * 512, (grp + 1) * 512)
                        ps, pk = b.PS()
                        b.MM(ps[:, :], Btok2[:, bi, grp * 128:(grp + 1) * 128], vtl[:, vi, gs], r=[("Btok", bi), ("vtl", vi)], w=[pk])
                        j = 0
                        b.TT("dve", stmp[:, j, :].rearrange("p (h v) -> p h v", h=8), Sst[:, d, gs].rearrange("p (h v) -> p h v", h=8),
                             teall[:, c, 32 + d * 16 + grp * 8:32 + d * 16 + grp * 8 + 8].unsqueeze(2).to_broadcast([128, 8, 64]), ALU.mult,
                             r=[("S", d), ("te", c)], w=[("stmp", j)])
                        b.TT("dve", Sst[:, d, gs], stmp[:, j, :], ps[:, :], ALU.add, r=[("stmp", j), pk], w=[("S", d)])
                for s in range(nseq if "nossd" not in b.stages else 0):
                    if is_s:
                        for d in range(2):
                            b.LOAD(Sst[:, d, :].rearrange("p (h v) -> p h v", h=16), I["ssd_state"][d].rearrange("h n v -> n h v"), w=[("S", d)])
                    else:
                        b.MEMSET("dve", Sst[:, 0, :], 0.0, w=[("S", 0)])
                        b.MEMSET("dve", Sst[:, 1, :], 0.0, w=[("S", 1)])
                    for cc in reversed(range(CS)):
                        c = s * CS + cc
                        b.CP("act", Sbin[:, cc, :], Sst[:, 1, :], r=[("S", 1)], w=[("Sbin", cc)])
                        bi = tok_tiles(c)
                        upd(1, c, bi)
                    if not is_s:
                        b.STORE(E["new_ssd"][s, 1].rearrange("h n v -> n h v"), Sst[:, 1, :].rearrange("p (h v) -> p h v", h=16), r=[("S", 1)], key=("So", 1))
                    info = {}

                    def stageA(cc, qd, par):
                        c = s * CS + cc
                        tk = slice(c * 128, (c + 1) * 128)
                        grp, cp = qd // 2, cc % 2
                        if qd == 0:
                            info[cc] = tok_tiles(c)
                            for g2 in range(2):
                                pg, pkg = b.PS()
                                b.MM(pg[:, 0:128], xcT[:, 8 + g2, tk], xcT[:, 10 + g2, tk], r=["xcT"], w=[pkg])
                                for d in range(2):
                                    b.TT("dve", Gm[:, cp, g2, d, :], pg[:, 0:128], trib[:, d, :], ALU.mult, r=[pkg, "tri"], w=[("Gm", cp, g2, d)])
                        bi = info[cc]
                        pds = {}
                        for d in range(2):
                            for hh in range(4):
                                dh = d * 16 + qd * 4 + hh
                                b.ACT(R1[:, par, d, hh * 128:(hh + 1) * 128], tri[:, d, :], AF.Identity, r=["dt", "tri"], w=[("R1", par, d)],
                                      scale=a_tok[:, c, dh:dh + 1])
                            b.TT("dve", vd[:, par, d, :].rearrange("p (h v) -> p h v", h=4), xtok2[:, bi, qd * 256:(qd + 1) * 256].rearrange("p (h v) -> p h v", h=4),
                                 dt_tok[:, c, d * 16 + qd * 4:d * 16 + qd * 4 + 4].unsqueeze(2).to_broadcast([128, 4, 64]), ALU.mult,
                                 r=[("xtok", bi), "dt"], w=[("vd", par, d)])
                        for d in range(2):
                            pd, pkd = b.PS()
                            b.MM(pd[:, :], C0["trir"][:, 3 - d, :], R1[:, par, d, :], r=[("R1", par, d), "tri"], w=[pkd])
                            pc, pkc = b.PS()
                            b.MM(pc[:, :], C0["trir"][:, 4, :], R1[:, par, d, :], r=[("R1", par, d), "tri"], w=[pkc])
                            pds[d] = (pd, pkd, pc, pkc)
                        for d in range(2):
                            pd, pkd, pc, pkc = pds[d]
                            b.ACT(DE[:, par, d, 0, :], pd[:, :], AF.Exp, r=[pkd], w=[("DE", par, d, 0)])
                            b.ACT(DE[:, par, d, 1, :], pc[:, :], AF.Exp, r=[pkc], w=[("DE", par, d, 1)])

                    def stageB(cc, qd, par):
                        c = s * CS + cc
                        tk = slice(c * 128, (c + 1) * 128)
                        grp, cp, q2 = qd // 2, cc % 2, qd % 2
                        if qd == 0:
                            b.CP("act", Sfb[:, :], Sst[:, 0, :], r=[("S", 0)], w=["Sfb"])
                        for d in range(2):
                            b.TT("dve", MC[:, par, d, 0, :].rearrange("p (h t) -> p h t", h=4), DE[:, par, d, 0, :].rearrange("p (h t) -> p h t", h=4),
                                 Gm[:, cp, grp, d:d + 1, :].to_broadcast([128, 4, 128]), ALU.mult, r=[("DE", par, d, 0), ("Gm", cp, grp, d)], w=[("MC", par, d, 0)])
                            b.TT("dve", MC[:, par, d, 1, :].rearrange("p (h t) -> p h t", h=4), DE[:, par, d, 1, :].rearrange("p (h t) -> p h t", h=4),
                                 xcT[:, 10 + grp:11 + grp, tk].to_broadcast([128, 4, 128]), ALU.mult, r=[("DE", par, d, 1), "xcT"], w=[("MC", par, d, 1)])
                        py, pky = b.PS()
                        for hh in range(4):
                            h = qd * 4 + hh
                            hp = (h % 2) * 64
                            o = py[hp:hp + 64, (hh // 2) * 128:(hh // 2 + 1) * 128]
                            hs = slice(hh * 128, (hh + 1) * 128)
                            for d in range(2):
                                st = (Sfb[:, h * 64:(h + 1) * 64], "Sfb") if d == 0 else (Sbin[:, cc, h * 64:(h + 1) * 64], ("Sbin", cc))
                                b.MM(o, vd[:, par, d, hh * 64:(hh + 1) * 64], MC[:, par, d, 0, hs], start=(d == 0), stop=False,
                                     r=[("vd", par, d), ("MC", par, d, 0)], w=[pky], inc=False)
                                b.MM(o, st[0], MC[:, par, d, 1, hs], start=False, stop=(d == 1), r=[st[1], ("MC", par, d, 1)], w=[pky], inc=(d == 1))
                        for mm in range(2):
                            m = qd * 2 + mm
                            b.STT(yz[:, grp, (q2 * 2 + mm) * 128:(q2 * 2 + mm + 1) * 128], xcT[:, m, tk], C0["dskip"][:, m:m + 1], py[:, mm * 128:(mm + 1) * 128],
                                  ALU.mult, ALU.add, r=["xcT", "l0c", pky], w=[("yz", grp, q2)])
                        b.TT("dve", yz[:, grp, q2 * 256:(q2 + 1) * 256].rearrange("p (m t) -> p m t", m=2), yz[:, grp, q2 * 256:(q2 + 1) * 256].rearrange("p (m t) -> p m t", m=2),
                             szT[:, qd * 2:qd * 2 + 2, tk], ALU.mult, r=[("yz", grp, q2), "szT"], w=[("yz", grp, q2)])
                        if q2 == 1:
                            b.ACT(sqb[:, grp, :], yz[:, grp, :], AF.Square, r=[("yz", grp, 0), ("yz", grp, 1)], w=[("sqb", grp)])
                            pn, pkn = b.PS()
                            for t in range(4):
                                b.MM(pn[:, 0:128], onesb[:, :], sqb[:, grp, t * 128:(t + 1) * 128], start=(t == 0), stop=(t == 3), r=[("sqb", grp), "onesb"], w=[pkn], inc=(t == 3))
                            b.ACT(rs[:, grp, :], pn[:, 0:128], AF.Ln, r=[pkn], w=[("rs0", grp)], scale=1.0 / 512.0, bias=EPS)
                            b.ACT(rs[:, grp, :], rs[:, grp, :], AF.Exp, r=[("rs0", grp)], w=[("rs0", grp)], scale=-0.5)
                            for t in range(4):
                                m = grp * 4 + t
                                b.STT(mixT[:, 4 + m, tk], yz[:, grp, t * 128:(t + 1) * 128], C0["ssdnw"][:, m:m + 1], rs[:, grp, :], ALU.mult, ALU.mult,
                                      r=[("yz", grp, 0), ("yz", grp, 1), "l0c", ("rs0", grp)], w=["mixT"])
                        if qd == 3:
                            upd(0, c, info[cc])
                    units = [(cc, qd) for cc in range(CS) for qd in range(4)]
                    for n_ in range(len(units) + 1):
                        if n_ < len(units):
                            stageA(units[n_][0], units[n_][1], n_ % 2)
                        if n_ >= 1:
                            stageB(units[n_ - 1][0], units[n_ - 1][1], (n_ - 1) % 2)
                        tick()
                    if not is_s:
                        b.STORE(E["new_ssd"][s, 0].rearrange("h n v -> n h v"), Sst[:, 0, :].rearrange("p (h v) -> p h v", h=16), r=[("S", 0)], key=("So", 0))
                S.barrier()
            S.barrier()
        b.dump(f"mix{g}", mixT[:, :, :], (128, 12, NT), ["mixT"])
        for m in range(8):
            slot, wk = b.wload(b.wview(Wo, 0, 12, m * 128, 128))
            for tb in range(2):
                ps, pk = b.PS()
                for kt in range(12):
                    b.MM(ps[:, :], slot[:, kt, :], mixT[:, kt, tb * 512:(tb + 1) * 512], start=(kt == 0), stop=(kt == 11), r=wk + ["mixT"], w=[pk], inc=(kt == 11))
                b.STT(xT[:, m, tb * 512:(tb + 1) * 512], ps[:, :], gate[:, m:m + 1], xT[:, m, tb * 512:(tb + 1) * 512], ALU.mult, ALU.add,
                      r=[pk, "mod0"] + XK, w=XK)
    S.barrier()


def layer1(b, g):
    nc, S, E = b.nc, b.S, b.env
    I, xT, XK, onesb, identb = E["I"], E["xT"], E["XK"], E["onesb"], E["identb"]
    is_s = (g == 1)
    W = I["l1_w_in"]
    Wo = I["l1_w_out"]
    R = E["ret"]
    nseq, CS = (1, 8) if is_s else (4, 2)
    gate = E["modv"](1, g, 2)
    with b.tmp("hT", [128, 8, NT], BF16) as hT, b.tmp("ropec", [128, 2, NT if is_s else 2], F32) as ropec, \
            b.tmp("ropes", [128, 2, NT if is_s else 2], F32) as ropes:
        if is_s:
            R = dict(R)
            R["cos"], R["sin"] = ropec, ropes
            b.LOAD(ropec[:, :, :], I["rope_cos"][:, :, :], w=["rope"], semkey="ropec")
            b.LOAD(ropes[:, :, :], I["rope_sin"][:, :, :], w=["rope"], semkey="ropes")
        E["norm_mod"](1, g, 0, hT, "hT")
        for hd in range(4):
            with b.tmps(("qT", [128, 2, NT], BF16), ("kT", [128, 2, NT], BF16), ("vtok", [128, 8, 512], BF16),
                        ("sgT", [128, 4, NT], BF16), ("ktok", [128, 8, 256], BF16), ("ygT", [128, 4, NT], BF16),
                        ("Sst", [128, 2, 2, 512], F32), ("Sfin", [128, 8, 2, 512], BF16), ("Sbin", [128, 8, 2, 512], BF16),
                        ("rt", [128, 6, 512 if is_s else 2], F32), ("sm", [128, 3, 768], BF16), ("sq", [128, 3, 512], BF16),
                        ("rs", [128, 3, 128], F32), ("yt", [128, 3, 512], F32),
                        ("ktl", [128, 2, 256], BF16), ("rawr", [128, 2, 512 if is_s else 2], mybir.dt.float32r)) as (qT, kT, vtok, sgT, ktok, ygT, Sst, Sfin, Sbin, rt, sm, sqb, rs, yt, ktl, rawr):
                cnt = [0]

                def qk_evac(dst, dkey, scl):
                    def ev(m, tb, ps, pk):
                        sl = slice(tb * 512, (tb + 1) * 512)
                        if not is_s:
                            b.ACT(dst[:, m, sl], ps, AF.Copy, r=[pk], w=[dkey], scale=scl)
                            return
                        i = cnt[0] % 2
                        cnt[0] += 1
                        raw, t1, t2 = rawr[:, i, :], rt[:, i * 3 + 1, :], rt[:, i * 3 + 2, :]
                        b.ACT(raw, ps, AF.Copy, r=[pk], w=[("rt", i, 0)], scale=scl)
                        p2, pk2 = b.PS()
                        b.MM(p2[:, :], R["permPr"][:, :], raw, r=[("rt", i, 0), "permP"], w=[pk2])
                        b.TT("dve", t1, raw, R["cos"][:, m, sl], ALU.mult, r=[("rt", i, 0), "rope"], w=[("rt", i, 1)])
                        b.TT("dve", t2, p2[:, :], R["sin"][:, m, sl], ALU.mult, r=[pk2, "rope"], w=[("rt", i, 2)])
                        b.TT("dve", dst[:, m, sl], t1, t2, ALU.add, r=[("rt", i, 1), ("rt", i, 2)], w=[dkey])
                    return ev
                b.proj_fm(W, hd * 256, 2, hT, ["hT"], 8, qk_evac(qT, "qT", 1.0))
                b.proj_fm(W, 1024 + hd * 256, 2, hT, ["hT"], 8, qk_evac(kT, "kT", 1.0 / 16.0))
                for half in range(2):
                    def v_evac(tt, ps, pk, half=half):
                        b.CP("act" if tt % 2 else "dve", vtok[:, tt, half * 256:(half + 1) * 256], ps, r=[pk], w=[("vtok", tt)])
                    b.proj_tm(W, 2048 + hd * 512 + half * 256, 256, hT, ["hT"], v_evac)
                def g_evac(m, tb, ps, pk):
                    i = cnt[0] % 2
                    cnt[0] += 1
                    b.ACT(yt[:, i, :], ps, AF.Silu, r=[pk], w=[("yt", i)])
                    b.ACT(sgT[:, m, tb * 512:(tb + 1) * 512], yt[:, i, :], AF.Copy, r=[("yt", i), "retnw"], w=["sgT"],
                          scale=R["normw"][:, hd * 4 + m:hd * 4 + m + 1])
                g_it = b.proj_fm_it(W, 4096 + hd * 512, 4, hT, ["hT"], 8, g_evac)
                for c in range(8):
                    ps, pk = b.PS()
                    psb = ps[:, :].bitcast(BF16)
                    for kt in range(2):
                        b.TR(psb[:, kt * 128:(kt + 1) * 128], kT[:, kt, c * 128:(c + 1) * 128], identb[:, :], r=["kT", "identb"], w=[pk], inc=(kt == 1))
                    b.CP("act" if c % 2 else "dve", ktok[:, c, :], psb[:, 0:256], r=[pk], w=[("ktok", c)])
                Sf, Sb = Sst[:, 0, :, :], Sst[:, 1, :, :]
                EtF, EtB = R["etot"][:, hd:hd + 1], R["etot"][:, 4 + hd:5 + hd]

                def upd(d, c, tailcol, et):
                    Sd = Sst[:, d, :, :]
                    i = cnt[0] % 2
                    cnt[0] += 1
                    b.ACT(ktl[:, i, :], ktok[:, c, :], AF.Copy, r=[("ktok", c), "rettail"], w=[("ktl", i)], scale=tailcol)
                    for kt in range(2):
                        ps, pk = b.PS()
                        b.MM(ps[:, :], ktl[:, i, kt * 128:(kt + 1) * 128], vtok[:, c, :], r=[("ktl", i), ("vtok", c)], w=[pk])
                        b.STT(Sd[:, kt, :], Sd[:, kt, :], et, ps[:, :], ALU.mult, ALU.add, r=[("S", d), pk, "retetot"], w=[("S", d)])

                def state_steps():
                    for s in range(nseq):
                        if is_s:
                            for d in range(2):
                                b.LOAD(Sst[:, d, :, :], I["ret_state"][d, hd].rearrange("(kt p) v -> p kt v", p=128), w=[("S", d)])
                        else:
                            b.MEMSET("dve", Sst[:, 0, :, :], 0.0, w=[("S", 0)])
                            b.MEMSET("dve", Sst[:, 1, :, :], 0.0, w=[("S", 1)])
                        yield
                        for k in range(CS):
                            cb, cf = s * CS + CS - 1 - k, s * CS + k
                            b.CP("act", Sbin[:, cb, :, :], Sb, r=[("S", 1)], w=[("Sbin", cb)])
                            upd(1, cb, R["tail"][:, 4 + hd:5 + hd], EtB)
                            b.CP("act", Sfin[:, cf, :, :], Sf, r=[("S", 0)], w=[("Sfin", cf)])
                            upd(0, cf, R["tail"][:, hd:hd + 1], EtF)
                            yield
                        if not is_s:
                            b.STORE(E["new_ret"][s, 1, hd].rearrange("(kt p) v -> p kt v", p=128), Sb, r=[("S", 1)], key=("So", 1))
                            b.STORE(E["new_ret"][s, 0, hd].rearrange("(kt p) v -> p kt v", p=128), Sf, r=[("S", 0)], key=("So", 0))
                its_ = [g_it, state_steps()]
                while its_:
                    for it_ in list(its_):
                        try:
                            next(it_)
                        except StopIteration:
                            its_.remove(it_)
                if True:
                    s = 0
                    CSs = CS
                    CS = 8
                    pys = {}

                    def stage1(cc):
                        c = s * CS + cc
                        tk = slice(c * 128, (c + 1) * 128)
                        i = c % 3
                        pg, pkg = b.PS()
                        for kt in range(2):
                            b.MM(pg[:, 0:128], kT[:, kt, tk], qT[:, kt, tk], start=(kt == 0), stop=(kt == 1), r=["kT", "qT"], w=[pkg], inc=(kt == 1))
                        M = sm[:, i, 0:128]
                        qdf = sm[:, i, 256:512].rearrange("p (k t) -> p k t", k=2)
                        qdb = sm[:, i, 512:768].rearrange("p (k t) -> p k t", k=2)
                        b.TT("dve", M, pg[:, 0:128], R["dtot"][:, hd, :], ALU.mult, r=[pkg, "retdtot"], w=[("sm", i, 0)])
                        b.TT("dve", qdf, qT[:, :, tk], R["ecs"][:, hd:hd + 1, :].to_broadcast([128, 2, 128]), ALU.mult, r=["qT", "retecs"], w=[("sm", i, 1)])
                        b.TT("dve", qdb, qT[:, :, tk], R["ecs"][:, 4 + hd:5 + hd, :].to_broadcast([128, 2, 128]), ALU.mult, r=["qT", "retecs"], w=[("sm", i, 2)])

                    def stage2(cc):
                        c = s * CS + cc
                        i = c % 3
                        M = sm[:, i, 0:128]
                        qdf = sm[:, i, 256:512].rearrange("p (k t) -> p k t", k=2)
                        qdb = sm[:, i, 512:768].rearrange("p (k t) -> p k t", k=2)
                        py, pky = b.PS()
                        pys[cc] = (py, pky)
                        for vt in range(4):
                            vs = slice(vt * 128, (vt + 1) * 128)
                            o = py[:, vs]
                            b.MM(o, vtok[:, c, vs], M, start=True, stop=False, r=[("vtok", c), ("sm", i, 0)], w=[pky], inc=False)
                            for kt in range(2):
                                b.MM(o, Sfin[:, cc, kt, vs], qdf[:, kt, :], start=False, stop=False, r=[("Sfin", cc), ("sm", i, 1)], w=[pky], inc=False)
                            for kt in range(2):
                                b.MM(o, Sbin[:, cc, kt, vs], qdb[:, kt, :], start=False, stop=(kt == 1), r=[("Sbin", cc), ("sm", i, 2)], w=[pky], inc=(kt == 1))
                        b.ACT(sqb[:, i, :], py[:, :], AF.Square, r=[pky], w=[("sqb", i)])

                    def stage3(cc):
                        c = s * CS + cc
                        tk = slice(c * 128, (c + 1) * 128)
                        i = c % 3
                        py, pky = pys.pop(cc)
                        pn, pkn = b.PS()
                        for vt in range(4):
                            b.MM(pn[:, 0:128], onesb[:, :], sqb[:, i, vt * 128:(vt + 1) * 128], start=(vt == 0), stop=(vt == 3), r=[("sqb", i), "onesb"], w=[pkn], inc=(vt == 3))
                        b.ACT(rs[:, i, :], pn[:, 0:128], AF.Ln, r=[pkn], w=[("rs1", i)], scale=1.0 / 512.0, bias=EPS)
                        b.ACT(rs[:, i, :], rs[:, i, :], AF.Exp, r=[("rs1", i)], w=[("rs1", i)], scale=-0.5)
                        y3 = yt[:, i, :].rearrange("p (v t) -> p v t", v=4)
                        b.TT("dve", y3, py[:, :].rearrange("p (v t) -> p v t", v=4), rs[:, i:i + 1, :].to_broadcast([128, 4, 128]), ALU.mult,
                             r=[pky, ("rs1", i)], w=[("yt", i)])
                        b.TT("dve", ygT[:, :, tk], y3, sgT[:, :, tk], ALU.mult, r=[("yt", i), "sgT"], w=["ygT"])
                    for step in range(CS + 2):
                        if step < CS:
                            stage1(step)
                        if 0 <= step - 1 < CS:
                            stage2(step - 1)
                        if 0 <= step - 2 < CS:
                            stage3(step - 2)
                CS = CSs
                for half in range(2):
                    slot, wk = b.wload(b.wview(Wo, hd * 4, 4, half * 512, 512))
                    for j in range(4):
                        m = half * 4 + j
                        for tb in range(2):
                            ps, pk = b.PS()
                            for kt in range(4):
                                b.MM(ps[:, :], slot[:, kt, j * 128:(j + 1) * 128], ygT[:, kt, tb * 512:(tb + 1) * 512], start=(kt == 0), stop=(kt == 3),
                                     r=wk + ["ygT"], w=[pk], inc=(kt == 3))
                            b.STT(xT[:, m, tb * 512:(tb + 1) * 512], ps[:, :], gate[:, m:m + 1], xT[:, m, tb * 512:(tb + 1) * 512],
                                  ALU.mult, ALU.add, r=[pk, "mod1"] + XK, w=XK)
            S.barrier()
    S.barrier()


def _fm(v, n):
    return np.ascontiguousarray(np.asarray(v, np.float32).reshape(n, 128).T)


def prep_core(inp, core):
    f = lambda k: np.ascontiguousarray(np.asarray(inp[k], np.float32))
    bs = core % 4
    m = {}
    xp = f("x_prompt")[core * 4:(core + 1) * 4].reshape(NT, D)
    xs = f("x_sample")[bs]
    m["xg"] = np.ascontiguousarray(np.stack([xp, xs], 0))
    cond = np.stack([f("c_ctx"), f("c")[bs]], 0)
    m["condT"] = np.ascontiguousarray(cond.reshape(2, 8, 128).transpose(2, 1, 0))
    m["ident"] = np.eye(128, dtype=np.float32)
    for nm, src, n in [("l0_norm1", "l0_norm1_w", 8), ("l0_norm2", "l0_norm2_w", 8), ("l1_norm1", "l1_norm1_w", 8),
                       ("l1_norm2", "l1_norm2_w", 8), ("final_norm", "final_norm_w", 8),
                       ("l0_mod_b", "l0_mod_b", 48), ("l1_mod_b", "l1_mod_b", 48)]:
        m[nm] = _fm(inp[src], n)
    for k in ["l0_mod_w", "l1_mod_w", "l0_w_in", "l0_w_out", "l1_w_in", "l1_w_out", "l0_ffn_w1", "l0_ffn_w3", "l0_ffn_w2",
              "l1_ffn_w1", "l1_ffn_w3", "l1_ffn_w2"]:
        m[k] = f(k)
    m["k_ctx"] = f("cache_l0_na_k")[bs].reshape(256, 512)
    m["v_ctx"] = f("cache_l0_na_v")[bs].reshape(256, 512)
    m["ssd_state"] = f("state_l0_ssd")[bs]
    rb = f("l0_na_bias")
    p = np.arange(128)
    e, ck = p // 64, p % 64
    dd = np.arange(16)
    cq = np.arange(64)
    dr = dd[None, :] - 8 + e[:, None]
    vr = (dr >= -7) & (dr <= 7)
    dc = np.clip(ck[:, None] - cq[None, :] + 15, 0, 30)
    c0 = np.clip(cq - 8, 0, 48)
    colin = (ck[:, None] >= c0[None, :]) & (ck[:, None] < c0[None, :] + 16)
    G = rb[:, np.clip(dr, -7, 7)[:, :, None] + 7, dc[:, None, :]]
    G = np.where(vr[None, :, :, None], G, np.float32(0.0))
    m["na_biasG"] = np.ascontiguousarray(G.transpose(1, 0, 2, 3).astype(np.float32))
    m["na_mask"] = np.ascontiguousarray((vr[:, :, None] & colin[:, None, :]).astype(np.float32).reshape(128, 1024))
    tt_, ii_ = np.arange(128)[:, None], np.arange(128)[None, :]
    m["tri"] = np.ascontiguousarray(np.stack([tt_ <= ii_, tt_ >= ii_, tt_ < ii_, tt_ > ii_, np.ones((128, 128), bool)], 1).astype(np.float32))
    m["l0_conv_w"] = np.ascontiguousarray(f("l0_conv_w").T.reshape(12, 128, 5).transpose(1, 0, 2))
    m["l0_conv_b"] = _fm(inp["l0_conv_b"], 12)
    m["l0_dtbias"] = f("l0_ssd_dt_bias").reshape(1, 32)
    m["l0_alog"] = f("l0_ssd_a_log").reshape(1, 32)
    m["l0_dskip"] = _fm(np.repeat(f("l0_ssd_d"), 64), 8)
    m["l0_ssdnw"] = _fm(inp["l0_ssd_norm_w"], 8)
    m["ret_decay_b"] = f("l1_ret_decay").reshape(1, 8)
    jj = np.arange(128)[:, None].astype(np.float32)
    ii = np.arange(128)[None, :].astype(np.float32)
    m["ret_idx"] = np.ascontiguousarray(np.stack([np.maximum(ii - jj, 0), np.maximum(jj - ii, 0), (jj <= ii).astype(np.float32),
                                                  (jj >= ii).astype(np.float32)], 1).astype(np.float32))
    m["ret_row"] = np.ascontiguousarray(np.stack([np.broadcast_to(ii + 1, (128, 128)), np.broadcast_to(128 - ii, (128, 128))], 1).astype(np.float32))
    m["ret_col"] = np.ascontiguousarray(np.concatenate([127 - jj, jj], 1).astype(np.float32))
    m["ret_state"] = f("state_l1_ret")[bs]
    m["l1_ret_norm"] = _fm(inp["l1_ret_norm_w"], 16)
    t = np.arange(NT)
    row = (t // 64).astype(np.float32)
    col = (t % 64).astype(np.float32)
    freqs = (10000.0 ** (-np.arange(0, 128, 2, dtype=np.float32) / 128)).astype(np.float32)
    ang = np.concatenate([row[:, None] * freqs, col[:, None] * freqs], -1)
    angd = np.repeat(ang, 2, axis=1)
    m["rope_cos"] = np.ascontiguousarray(np.cos(angd).T.reshape(2, 128, NT).transpose(1, 0, 2).astype(np.float32))
    m["rope_sin"] = np.ascontiguousarray(np.sin(angd).T.reshape(2, 128, NT).transpose(1, 0, 2).astype(np.float32))
    P = np.zeros((128, 128), np.float32)
    for i in range(64):
        P[2 * i + 1, 2 * i] = -1.0
        P[2 * i, 2 * i + 1] = 1.0
    m["permP"] = P
    return m


_CACHE = {}


def kernel(**inputs):
    if "b" not in _CACHE:
        _CACHE["b"] = build()
    b = _CACHE["b"]
    in_maps = []
    for core in range(8):
        m = prep_core(inputs, core)
        in_maps.append({k: v for k, v in m.items() if k in b.din})
    res = run_bass_kernel_spmd(b.nc, in_maps, core_ids=list(range(8)))
    R = res.results
    y_prompt = np.concatenate([np.asarray(R[c]["y"][0]).reshape(4, 256, D) for c in range(8)], 0).astype(np.float32)
    y_sample = np.stack([np.asarray(R[c]["y"][1]) for c in range(4)], 0).astype(np.float32)
    new_k = np.concatenate([np.asarray(R[c]["new_k"]).reshape(4, 256, 8, 64) for c in range(8)], 0).astype(np.float32)
    new_v = np.concatenate([np.asarray(R[c]["new_v"]).reshape(4, 256, 8, 64) for c in range(8)], 0).astype(np.float32)
    new_ssd = np.concatenate([np.asarray(R[c]["new_ssd"]) for c in range(8)], 0).astype(np.float32)
    new_ret = np.concatenate([np.asarray(R[c]["new_ret"]) for c in range(8)], 0).astype(np.float32)
    return (y_prompt, y_sample, new_k, new_v, new_ssd, new_ret)
```

```python
import numpy as np
import concourse.bass as bass
import concourse.mybir as mybir
from concourse.bass_utils import run_bass_kernel_spmd

F32 = mybir.dt.float32
BF16 = mybir.dt.bfloat16
AF = mybir.ActivationFunctionType
ALU = mybir.AluOpType

D = 1024
NT = 1024
KT = 8
FFN_H = 2816
EPS = 1e-6
L0_IN = 4128
L1_IN = 6144


class Sched:
    ENG = ("pe", "act", "dve", "pool", "sp")

    def __init__(self, nc):
        self.nc = nc
        self.q = {k: [] for k in self.ENG}
        self.esem = {k: nc.alloc_semaphore(name=f"s_{k}") for k in self.ENG}
        self.ecnt = {k: 0 for k in self.ENG}
        self.open = {k: False for k in self.ENG}
        self.known = {k: {} for k in self.ENG}
        self.pending = {k: [] for k in self.ENG}
        self.lastw = {}
        self.readers = {}
        self.dsem = {}
        self.dcnt = {}
        self.dpersist = set()
        self.out_sems = set()

    def _need(self, eng, reads, writes, use_pending=True):
        need = {}

        def add(sv, kind):
            sem, val, owner = sv
            if owner == eng and (eng == "pe" or kind == "war"):
                return
            nm = sem.name
            if nm not in need or need[nm][1] < val:
                need[nm] = (sem, val)

        for k in reads:
            if k in self.lastw:
                add(self.lastw[k], "raw")
        for k in writes:
            if k in self.lastw:
                add(self.lastw[k], "waw")
            for sv in self.readers.get(k, {}).values():
                add(sv, "war")
        if use_pending and self.pending[eng]:
            for (sem, val) in self.pending[eng]:
                nm = sem.name
                if nm not in need or need[nm][1] < val:
                    need[nm] = (sem, val)
            self.pending[eng] = []
        out = []
        kn = self.known[eng]
        for nm, (sem, val) in need.items():
            if kn.get(nm, 0) < val:
                kn[nm] = val
                out.append((sem, val))
        return out

    def op(self, eng, fn, reads=(), writes=(), inc=True, persistent=False):
        waits = self._need(eng, reads, writes, use_pending=not persistent)
        if eng == "pool" and self.ecnt[eng] > 0:
            nm = self.esem[eng].name
            if self.known[eng].get(nm, 0) < self.ecnt[eng]:
                self.known[eng][nm] = self.ecnt[eng]
                waits.append((self.esem[eng], self.ecnt[eng]))
        if inc:
            self.ecnt[eng] += 1
            val = self.ecnt[eng]
            self.open[eng] = False
        else:
            val = self.ecnt[eng] + 1
            self.open[eng] = True
        sem = self.esem[eng]
        self.q[eng].append((waits, fn, sem, 1 if inc else 0))
        sv = (sem, val, eng)
        for k in writes:
            self.lastw[k] = sv
            self.readers[k] = {}
        for k in reads:
            self.readers.setdefault(k, {})[eng] = sv

    def dma(self, qeng, out, in_, reads=(), writes=(), semkey=None, is_output=False, persistent=False, **kw):
        waits = self._need(qeng, reads, writes, use_pending=not persistent)
        sk = semkey if semkey is not None else (tuple(writes) + tuple(reads))
        if sk not in self.dsem:
            self.dsem[sk] = self.nc.alloc_semaphore(name=f"d{len(self.dsem)}")
            self.dcnt[sk] = 0
        if persistent:
            self.dpersist.add(sk)
        sem = self.dsem[sk]
        self.dcnt[sk] += 16
        val = self.dcnt[sk]
        if is_output:
            self.out_sems.add(sk)

        def fn(e, out=out, in_=in_, kw=kw):
            return e.dma_start(out=out, in_=in_, **kw)

        self.q[qeng].append((waits, fn, sem, 16))
        sv = (sem, val, "dma:" + str(sk))
        for k in writes:
            self.lastw[k] = sv
            self.readers[k] = {}
        for k in reads:
            self.readers.setdefault(k, {})["dma:" + str(sk)] = sv

    def barrier(self):
        tg = [(self.esem[e], self.ecnt[e]) for e in ("pe", "act", "dve") if self.ecnt[e] > 0]
        tg += [(self.dsem[k], self.dcnt[k]) for k in self.dsem if k not in self.dpersist]
        tg += [(self.esem["pool"], self.ecnt["pool"])] if self.ecnt["pool"] > 0 else []
        for e in ("pe", "act", "dve", "sp", "pool"):
            self.pending[e] = list(tg)

    def finish(self):
        nc = self.nc
        for e in self.ENG:
            assert not self.open[e], e
        fin = [(self.dsem[sk], self.dcnt[sk]) for sk in self.out_sems]
        with nc.Block() as block:
            def emit(name):
                def body(e):
                    for waits, fn, sem, inc in self.q[name]:
                        for (s, v) in waits:
                            e.wait_ge(s, v)
                        inst = fn(e)
                        if inc:
                            inst.then_inc(sem, inc)
                    if name == "sp":
                        for (s, v) in fin:
                            e.wait_ge(s, v)
                return body
            block.tensor(emit("pe"))
            block.scalar(emit("act"))
            block.vector(emit("dve"))
            block.gpsimd(emit("pool"))
            block.sync(emit("sp"))


class B:
    def __init__(self, dbg=None, stages=99, plan=None):
        self.plan = plan
        self.wrec = []
        self.wissued = 0
        self.dbg = dbg or []
        self.stages = stages
        nc = self.nc = bass.Bass("TRN2", target_bir_lowering=False)
        self.S = Sched(nc)
        self.din = {}
        self.dout = {}
        self.psn = 0
        self.ps = [nc.alloc_psum_tensor(f"psb{i}", [128, 512], F32) for i in range(8)]
        self.wn = 0
        self.wslot = [nc.alloc_sbuf_tensor(f"wslot{i}", [128, 2048], BF16) for i in range(self.NSLOT)]

    def inp(self, name, shape):
        t = self.nc.dram_tensor(name, list(shape), F32, kind="ExternalInput").ap()
        self.din[name] = t
        return t

    def outp(self, name, shape, dt=F32):
        t = self.nc.dram_tensor(name, list(shape), dt, kind="ExternalOutput").ap()
        self.dout[name] = t
        return t

    def sb(self, name, shape, dt=F32):
        return self.nc.alloc_sbuf_tensor("s_" + name, list(shape), dt)

    def tmps(self, *specs):
        import contextlib

        @contextlib.contextmanager
        def cm():
            with contextlib.ExitStack() as st:
                yield [st.enter_context(self.tmp(*sp)) for sp in specs]
        return cm()

    def tmp(self, name, shape, dt=F32):
        self.tn = getattr(self, "tn", 0) + 1
        return self.nc.sbuf_tensor(f"t_{name}_{self.tn}", list(shape), dt)

    def PS(self):
        b = self.psn % 8
        self.psn += 1
        return self.ps[b], ("ps", b)

    def MM(self, out, lhsT, rhs, start=True, stop=True, r=(), w=(), inc=True):
        self.S.op("pe", lambda e: e.matmul(out, lhsT=lhsT, rhs=rhs, start=start, stop=stop), reads=r, writes=w, inc=inc)

    def TR(self, out, in_, ident, r=(), w=(), inc=True):
        self.S.op("pe", lambda e: e.transpose(out=out, in_=in_, identity=ident), reads=r, writes=w, inc=inc)

    def ACT(self, out, in_, func, r=(), w=(), **kw):
        self.S.op("act", lambda e: e.activation(out=out, in_=in_, func=func, **kw), reads=r, writes=w)

    def TT(self, eng, out, in0, in1, op, r=(), w=()):
        self.S.op(eng, lambda e: e.tensor_tensor(out=out, in0=in0, in1=in1, op=op), reads=r, writes=w)

    def TS(self, eng, out, in0, s1, s2, op0, op1=None, r=(), w=()):
        if op1 is None:
            self.S.op(eng, lambda e: e.tensor_scalar(out=out, in0=in0, scalar1=s1, scalar2=None, op0=op0), reads=r, writes=w)
        else:
            self.S.op(eng, lambda e: e.tensor_scalar(out=out, in0=in0, scalar1=s1, scalar2=s2, op0=op0, op1=op1), reads=r, writes=w)

    def STT(self, out, in0, scalar, in1, op0, op1, r=(), w=()):
        self.S.op("dve", lambda e: e.scalar_tensor_tensor(out=out, in0=in0, scalar=scalar, in1=in1, op0=op0, op1=op1), reads=r, writes=w)

    def CP(self, eng, out, in_, r=(), w=()):
        if eng == "act":
            self.S.op("act", lambda e: e.copy(out=out, in_=in_), reads=r, writes=w)
        else:
            self.S.op(eng, lambda e: e.tensor_copy(out=out, in_=in_), reads=r, writes=w)

    def RECIP(self, out, in_, r=(), w=()):
        self.S.op("dve", lambda e: e.reciprocal(out=out, in_=in_), reads=r, writes=w)

    def MEMSET(self, eng, ap, val, w=()):
        self.S.op(eng, lambda e: e.memset(ap, val), writes=w)

    def LOAD(self, out, in_, w, **kw):
        self.S.dma("sp", out, in_, writes=w, **kw)

    def STORE(self, out, in_, r, key=None):
        self.S.dma("sp", out, in_, reads=r, is_output=True, semkey=key)

    def dump(self, name, ap, shape, r):
        if name in self.dbg:
            o = self.outp("dbg_" + name, shape, ap.dtype)
            self.S.dma("sp", o, ap, reads=r, is_output=True, semkey=("dbg", name))

    NSLOT, LA = 4, 2

    def _wissue(self, j):
        name, k0, kt, c0, nco = self.plan[j]
        w3 = self.din[name].rearrange("(kt p) c -> p kt c", p=128)[:, k0:k0 + kt, c0:c0 + nco]
        sl = j % self.NSLOT
        slot = self.wslot[sl][:, 0:kt * nco].rearrange("p (k c) -> p k c", k=kt)
        self.S.dma("pool", slot, w3, writes=[("wslot", sl)], persistent=True)

    def wload(self, desc):
        name, k0, kt, c0, nco = desc
        assert kt * nco <= 2048
        i = self.wn
        self.wn += 1
        self.wrec.append(desc)
        if self.plan is None:
            self.plan_tmp = getattr(self, "plan_tmp", [])
            self.plan_tmp.append(desc)
            plan_saved, self.plan = self.plan, self.plan_tmp
            self._wissue(i)
            self.plan = plan_saved
        else:
            assert self.plan[i] == desc, (i, desc, self.plan[i])
            while self.wissued <= min(i + self.LA, len(self.plan) - 1):
                self._wissue(self.wissued)
                self.wissued += 1
        sl = i % self.NSLOT
        slot = self.wslot[sl][:, 0:kt * nco].rearrange("p (k c) -> p k c", k=kt)
        return slot, [("wslot", sl)]

    @staticmethod
    def wview(W, k0, nk, c0, nco):
        return (W.name, k0, nk, c0, nco)

    def proj_fm(self, *a, **kw):
        for _ in self.proj_fm_it(*a, **kw):
            pass

    def proj_fm_it(self, W, c0, ntile, inT, in_keys, nk, evac, tblocks=(0, 1)):
        per = max(1, 2048 // (nk * 128))
        m = 0
        while m < ntile:
            n = min(per, ntile - m)
            slot, wk = self.wload(self.wview(W, 0, nk, c0 + m * 128, n * 128))
            for j in range(n):
                for tb in tblocks:
                    ps, pk = self.PS()
                    for kt in range(nk):
                        self.MM(ps[:, :], slot[:, kt, j * 128:(j + 1) * 128], inT[:, kt, tb * 512:(tb + 1) * 512],
                                start=(kt == 0), stop=(kt == nk - 1), r=wk + list(in_keys), w=[pk], inc=(kt == nk - 1))
                    evac(m + j, tb, ps[:, :], pk)
                    yield
            m += n

    def proj_tm(self, W, c0, nco, inT, in_keys, evac, nk=8):
        assert nk * nco <= 2048
        slot, wk = self.wload(self.wview(W, 0, nk, c0, nco))
        for tt in range(NT // 128):
            ps, pk = self.PS()
            for kt in range(nk):
                self.MM(ps[:, 0:nco], inT[:, kt, tt * 128:(tt + 1) * 128], slot[:, kt, :],
                        start=(kt == 0), stop=(kt == nk - 1), r=wk + list(in_keys), w=[pk], inc=(kt == nk - 1))
            evac(tt, ps[:, 0:nco], pk)


ALL_STAGES = ("l0", "ffn0", "l1", "ffn1")


def build(dbg=None, stages=ALL_STAGES, plan=None):
    if plan is None:
        plan = build(dbg=dbg, stages=stages, plan=[]).wrec
        return build(dbg=dbg, stages=stages, plan=plan)
    b = B(dbg, stages, plan if plan else None)
    nc, S = b.nc, b.S
    I = {}
    for nm, shp in [("xg", (2, NT, D)), ("condT", (128, 8, 2)), ("ident", (128, 128)),
                    ("l0_norm1", (128, 8)), ("l0_norm2", (128, 8)), ("l1_norm1", (128, 8)), ("l1_norm2", (128, 8)),
                    ("final_norm", (128, 8)), ("l0_mod_b", (128, 48)), ("l1_mod_b", (128, 48)),
                    ("l0_mod_w", (D, 6 * D)), ("l1_mod_w", (D, 6 * D)),
                    ("l0_w_in", (D, L0_IN)), ("l0_w_out", (1536, D)), ("l1_w_in", (D, L1_IN)), ("l1_w_out", (2048, D)),
                    ("l0_ffn_w1", (D, FFN_H)), ("l0_ffn_w3", (D, FFN_H)), ("l0_ffn_w2", (FFN_H, D)),
                    ("l1_ffn_w1", (D, FFN_H)), ("l1_ffn_w3", (D, FFN_H)), ("l1_ffn_w2", (FFN_H, D)),
                    ("k_ctx", (256, 512)), ("v_ctx", (256, 512)), ("na_biasG", (128, 8, 16, 64)), ("na_mask", (128, 1024)),
                    ("ssd_state", (2, 16, 128, 64)), ("tri", (128, 5, 128)), ("l0_conv_w", (128, 12, 5)), ("l0_conv_b", (128, 12)),
                    ("l0_dtbias", (1, 32)), ("l0_alog", (1, 32)), ("l0_dskip", (128, 8)), ("l0_ssdnw", (128, 8)),
                    ("ret_decay_b", (1, 8)), ("ret_idx", (128, 4, 128)), ("ret_row", (128, 2, 128)), ("ret_col", (128, 2)),
                    ("ret_state", (2, 4, 256, 512)), ("l1_ret_norm", (128, 16)), ("rope_cos", (128, 2, NT)),
                    ("rope_sin", (128, 2, NT)), ("permP", (128, 128))]:
        I[nm] = b.inp(nm, shp)
    y_out = b.outp("y", (2, NT, D))
    new_ret = b.outp("new_ret", (4, 2, 4, 256, 512))
    new_k = b.outp("new_k", (NT, 512))
    new_v = b.outp("new_v", (NT, 512))
    new_ssd = b.outp("new_ssd", (4, 2, 16, 128, 64))

    ident = b.sb("c_ident", [128, 128], F32)
    identb = b.sb("c_identb", [128, 128], BF16)
    onesb = b.sb("c_onesb", [128, 128], BF16)
    b.LOAD(ident[:, :], I["ident"][:, :], w=["ident"])
    b.CP("dve", identb[:, :], ident[:, :], r=["ident"], w=["identb"])
    b.MEMSET("dve", onesb[:, :], 1.0, w=["onesb"])
    vecs = {}
    for nm, n in [("l0_norm1", 8), ("l0_norm2", 8), ("l1_norm1", 8), ("l1_norm2", 8), ("final_norm", 8),
                  ("l0_mod_b", 48), ("l1_mod_b", 48)]:
        vecs[nm] = b.sb("v_" + nm, [128, n], F32)
        b.LOAD(vecs[nm][:, :], I[nm][:, :], w=["v_" + nm])

    condT = b.sb("condT", [128, 8, 2], F32)
    scb = b.sb("scb", [128, 8, 2], BF16)
    b.LOAD(condT[:, :, :], I["condT"][:, :, :], w=["condT"])
    b.ACT(scb[:, :, :], condT[:, :, :], AF.Silu, r=["condT"], w=["scb"])
    mod = [b.sb(f"mod{l}", [128, 48, 2], F32) for l in range(2)]
    modA = [[[b.sb(f"modA{l}{g}{j}", [128, 8], F32) for j in range(2)] for g in range(2)] for l in range(2)]
    def adaln_it(l):
        W = I[f"l{l}_mod_w"]

        def mk_modA(j):
            sc_i, nw = [(1, f"l{l}_norm1"), (4, f"l{l}_norm2")][j]
            for g in range(2):
                b.S.op("dve", lambda e, o=modA[l][g][j][:, :], i0=mod[l][:, sc_i * 8:(sc_i + 1) * 8, g], i1=vecs[nw][:, :]:
                       e.scalar_tensor_tensor(out=o, in0=i0, scalar=1.0, in1=i1, op0=ALU.add, op1=ALU.mult),
                       reads=[f"mod{l}", "v_" + nw], writes=[f"modA{l}{g}{j}"])
        for c in range(24):
            slot, wk = b.wload(b.wview(W, 0, 8, c * 256, 256))
            for j in range(2):
                ft = c * 2 + j
                ps, pk = b.PS()
                for kt in range(8):
                    b.MM(ps[:, 0:2], slot[:, kt, j * 128:(j + 1) * 128], scb[:, kt, :], start=(kt == 0), stop=(kt == 7),
                         r=wk + ["scb"], w=[pk], inc=(kt == 7))
                b.TS("dve", mod[l][:, ft, :], ps[:, 0:2], vecs[f"l{l}_mod_b"][:, ft:ft + 1], None, ALU.add,
                     r=[pk, f"v_l{l}_mod_b"], w=[f"mod{l}"])
            if c == 7:
                mk_modA(0)
            if c == 23:
                mk_modA(1)
            yield

    def modv(l, g, idx):
        return mod[l][:, idx * 8:(idx + 1) * 8, g]

    xT = b.sb("xT", [128, 8, NT], F32)

    def load_x(g):
        with b.tmp("xtok", [128, 2, D], F32) as xtok:
            for tt in range(8):
                bi = tt % 2
                b.LOAD(xtok[:, bi, :], I["xg"][g, tt * 128:(tt + 1) * 128, :], w=[("xtok", bi)])
                for half in range(2):
                    ps, pk = b.PS()
                    for q in range(4):
                        kt = half * 4 + q
                        b.TR(ps[:, q * 128:(q + 1) * 128], xtok[:, bi, kt * 128:(kt + 1) * 128], ident[:, :],
                             r=[("xtok", bi), "ident"], w=[pk], inc=(q == 3))
                    eng = "dve" if half == 0 else "act"
                    b.CP(eng, xT[:, half * 4:(half + 1) * 4, tt * 128:(tt + 1) * 128],
                         ps[:, :].rearrange("p (q t) -> p q t", q=4), r=[pk], w=[("xT", tt)])
            S.barrier()

    XK = [("xT", tt) for tt in range(8)]

    def rstd_block(src, src_keys, tb, nkt, rs, rs_key, inv_n):
        ps, pk = b.PS()
        with b.tmp("sqt", [128, 2, 512], BF16) as sq:
            for kt in range(nkt):
                bi = kt % 2
                b.ACT(sq[:, bi, :], src[:, kt, tb * 512:(tb + 1) * 512], AF.Square, r=src_keys, w=[("sq", bi)])
                b.MM(ps[:, :], onesb[:, :], sq[:, bi, :], start=(kt == 0), stop=(kt == nkt - 1),
                     r=[("sq", bi), "onesb"], w=[pk], inc=True)
        b.ACT(rs, ps[:, :], AF.Sqrt, r=[pk], w=[rs_key], scale=inv_n, bias=EPS)
        b.RECIP(rs, rs, r=[rs_key], w=[rs_key])

    def norm_mod(l, g, j, hT, hkey):
        A = modA[l][g][j]
        Bv = modv(l, g, 0 if j == 0 else 3)
        with b.tmp("rs", [128, 512], F32) as rs, b.tmp("ntmp", [128, 2, 512], F32) as tmp:
            for tb in range(2):
                rstd_block(xT, XK, tb, 8, rs[:, :], ("rs", tb), 1.0 / D)
                for kt in range(8):
                    bi = kt % 2
                    b.TT("dve", tmp[:, bi, :], xT[:, kt, tb * 512:(tb + 1) * 512], rs[:, :], ALU.mult,
                         r=XK + [("rs", tb)], w=[("ntmp", bi)])
                    b.ACT(hT[:, kt, tb * 512:(tb + 1) * 512], tmp[:, bi, :], AF.Identity,
                          r=[("ntmp", bi), f"modA{l}{g}{j}", f"mod{l}"], w=[hkey], scale=A[:, kt:kt + 1], bias=Bv[:, kt:kt + 1])
        S.barrier()

    def resid_evac(l, g, gi):
        gate = modv(l, g, gi)

        def ev(m, tb, ps, pk):
            b.STT(xT[:, m, tb * 512:(tb + 1) * 512], ps, gate[:, m:m + 1], xT[:, m, tb * 512:(tb + 1) * 512],
                  ALU.mult, ALU.add, r=[pk, f"mod{l}"] + XK, w=XK)
        return ev

    def ffn(l, g, other=None):
        def tick():
            if other is not None:
                try:
                    next(other)
                except StopIteration:
                    pass
        with b.tmp("h2T", [128, 8, NT], BF16) as h2T, b.tmp("gT", [128, 22, NT], BF16) as gT, \
                b.tmp("s1", [128, 2, 512], F32) as s1:
            norm_mod(l, g, 1, h2T, "h2T")
            W1, W3, W2 = I[f"l{l}_ffn_w1"], I[f"l{l}_ffn_w3"], I[f"l{l}_ffn_w2"]
            cnt = [0]
            for c in range(11):
                s_1, k1 = b.wload(b.wview(W1, 0, 8, c * 256, 256))
                s_3, k3 = b.wload(b.wview(W3, 0, 8, c * 256, 256))
                for j in range(2):
                    m = c * 2 + j
                    for tb in range(2):
                        p1, pk1 = b.PS()
                        p3, pk3 = b.PS()
                        for kt in range(8):
                            b.MM(p1[:, :], s_1[:, kt, j * 128:(j + 1) * 128], h2T[:, kt, tb * 512:(tb + 1) * 512],
                                 start=(kt == 0), stop=(kt == 7), r=k1 + ["h2T"], w=[pk1], inc=(kt == 7))
                        for kt in range(8):
                            b.MM(p3[:, :], s_3[:, kt, j * 128:(j + 1) * 128], h2T[:, kt, tb * 512:(tb + 1) * 512],
                                 start=(kt == 0), stop=(kt == 7), r=k3 + ["h2T"], w=[pk3], inc=(kt == 7))
                        bi = cnt[0] % 2
                        cnt[0] += 1
                        b.ACT(s1[:, bi, :], p1[:, :], AF.Silu, r=[pk1], w=[("s1", bi)])
                        b.TT("dve", gT[:, m, tb * 512:(tb + 1) * 512], s1[:, bi, :], p3[:, :], ALU.mult,
                             r=[("s1", bi), pk3], w=[("gT", m)])
                tick()
                tick()
            ev = resid_evac(l, g, 5)
            GK = [("gT", m) for m in range(22)]
            for mo in range(8):
                sa, ka = b.wload(b.wview(W2, 0, 11, mo * 128, 128))
                sb_, kb = b.wload(b.wview(W2, 11, 11, mo * 128, 128))
                for tb in range(2):
                    ps, pk = b.PS()
                    for kt in range(22):
                        sl, kk = (sa, ka) if kt < 11 else (sb_, kb)
                        b.MM(ps[:, :], sl[:, kt % 11, :], gT[:, kt, tb * 512:(tb + 1) * 512], start=(kt == 0), stop=(kt == 21),
                             r=kk + GK, w=[pk], inc=(kt == 21))
                    ev(mo, tb, ps[:, :], pk)
                tick()
            if other is not None:
                for _ in other:
                    pass
            S.barrier()

    def final_out(g, gnext=None):
        fw = vecs["final_norm"]
        with b.tmp("rs", [128, 512], F32) as rs, b.tmp("ytmp", [128, 2, 512], F32) as tmp, \
                b.tmp("ytok", [128, 2, D], F32) as ytok, b.tmp("xtokn", [128, 2, D if gnext is not None else 2], F32) as xtok:
            for tb in range(2):
                rstd_block(xT, XK, tb, 8, rs[:, :], ("rs", tb), 1.0 / D)
                for kt in range(8):
                    b.STT(xT[:, kt, tb * 512:(tb + 1) * 512], xT[:, kt, tb * 512:(tb + 1) * 512], fw[:, kt:kt + 1], rs[:, :],
                          ALU.mult, ALU.mult, r=XK + ["v_final_norm", ("rs", tb)], w=XK)
            for tt in range(8):
                bi = tt % 2
                for half in range(2):
                    ps, pk = b.PS()
                    for q in range(4):
                        kt = half * 4 + q
                        b.TR(ps[:, q * 128:(q + 1) * 128], xT[:, kt, tt * 128:(tt + 1) * 128], ident[:, :],
                             r=[("xT", tt), "ident"], w=[pk], inc=(q == 3))
                    eng = "dve" if half == 0 else "act"
                    b.CP(eng, ytok[:, bi, half * 512:(half + 1) * 512], ps[:, :], r=[pk], w=[("ytok", bi, half)])
                b.STORE(y_out[g, tt * 128:(tt + 1) * 128, :], ytok[:, bi, :], r=[("ytok", bi, 0), ("ytok", bi, 1)], key=("ytok", bi))
                if gnext is not None:
                    if tt == 0:
                        b.LOAD(xtok[:, 0, :], I["xg"][gnext, 0:128, :], w=[("xtokn", 0)])
                    if tt + 1 < 8:
                        b.LOAD(xtok[:, (tt + 1) % 2, :], I["xg"][gnext, (tt + 1) * 128:(tt + 2) * 128, :], w=[("xtokn", (tt + 1) % 2)])
                    for half in range(2):
                        ps, pk = b.PS()
                        for q in range(4):
                            kt = half * 4 + q
                            b.TR(ps[:, q * 128:(q + 1) * 128], xtok[:, bi, kt * 128:(kt + 1) * 128], ident[:, :],
                                 r=[("xtokn", bi), "ident"], w=[pk], inc=(q == 3))
                        eng = "act" if half == 0 else "dve"
                        b.CP(eng, xT[:, half * 4:(half + 1) * 4, tt * 128:(tt + 1) * 128],
                             ps[:, :].rearrange("p (q t) -> p q t", q=4), r=[pk], w=[("xT", tt)])
            S.barrier()

    l0c = {}
    if "l0" in stages:
        l0c["tri"] = b.sb("tri", [128, 5, 128], F32)
        l0c["trib"] = b.sb("trib", [128, 5, 128], BF16)
        l0c["conv_w"] = b.sb("convw", [128, 12, 5], F32)
        l0c["conv_b"] = b.sb("convb", [128, 12], F32)
        l0c["dtbias"] = b.sb("dtbias", [128, 32], F32)
        l0c["negA"] = b.sb("negA", [128, 32], F32)
        l0c["dskip"] = b.sb("dskip", [128, 8], F32)
        l0c["ssdnw"] = b.sb("ssdnw", [128, 8], F32)
        b.LOAD(l0c["tri"][:, :, :], I["tri"][:, :, :], w=["tri"])
        b.CP("dve", l0c["trib"][:, :, :], l0c["tri"][:, :, :], r=["tri"], w=["tri"])
        l0c["trir"] = b.sb("trir", [128, 5, 128], mybir.dt.float32r)
        b.CP("dve", l0c["trir"][:, :, :], l0c["tri"][:, :, :], r=["tri"], w=["tri"])
        b.LOAD(l0c["conv_w"][:, :, :], I["l0_conv_w"][:, :, :], w=["l0c"], semkey="l0c1")
        b.LOAD(l0c["conv_b"][:, :], I["l0_conv_b"][:, :], w=["l0c"], semkey="l0c2")
        b.LOAD(l0c["dtbias"][:, :], I["l0_dtbias"][0:1, :].partition_broadcast(128), w=["l0c"], semkey="l0c3")
        b.LOAD(l0c["negA"][:, :], I["l0_alog"][0:1, :].partition_broadcast(128), w=["l0c"], semkey="l0c4")
        b.LOAD(l0c["dskip"][:, :], I["l0_dskip"][:, :], w=["l0c"], semkey="l0c5")
        b.LOAD(l0c["ssdnw"][:, :], I["l0_ssdnw"][:, :], w=["l0c"], semkey="l0c6")
        b.ACT(l0c["negA"][:, :], l0c["negA"][:, :], AF.Exp, r=["l0c"], w=["l0c"])
        b.TS("dve", l0c["negA"][:, :], l0c["negA"][:, :], -1.0, None, ALU.mult, r=["l0c"], w=["l0c"])

    ret = {}
    if "l1" in stages:
        lgb = b.sb("lgb", [128, 8], F32)
        ridx = b.sb("ridx", [128, 4, 128], F32)
        rrow = b.sb("rrow", [128, 2, 128], F32)
        rcol = b.sb("rcol", [128, 2], F32)
        ret["normw"] = b.sb("retnw", [128, 16], F32)
        ret["permP"] = b.sb("permP", [128, 128], F32)
        ret["dtot"] = b.sb("dtot", [128, 4, 128], BF16)
        ret["ecs"] = b.sb("ecs", [128, 8, 128], BF16)
        ret["tail"] = b.sb("rtail", [128, 8], F32)
        ret["etot"] = b.sb("retot", [128, 8], F32)
        b.LOAD(lgb[:, :], I["ret_decay_b"][0:1, :].partition_broadcast(128), w=["lgb"])
        b.LOAD(ridx[:, :, :], I["ret_idx"][:, :, :], w=["ridx"])
        b.LOAD(rrow[:, :, :], I["ret_row"][:, :, :], w=["rrow"])
        b.LOAD(rcol[:, :], I["ret_col"][:, :], w=["rcol"])
        b.LOAD(ret["normw"][:, :], I["l1_ret_norm"][:, :], w=["retnw"])
        b.LOAD(ret["permP"][:, :], I["permP"][:, :], w=["permP"])
        ret["permPr"] = b.sb("permPr", [128, 128], mybir.dt.float32r)
        b.CP("dve", ret["permPr"][:, :], ret["permP"][:, :], r=["permP"], w=["permP"])
        b.ACT(lgb[:, :], lgb[:, :], AF.Exp, r=["lgb"], w=["lgb"], scale=-1.0)
        b.ACT(lgb[:, :], lgb[:, :], AF.Ln, r=["lgb"], w=["lgb"], bias=1.0)
        b.TS("dve", lgb[:, :], lgb[:, :], -1.0, None, ALU.mult, r=["lgb"], w=["lgb"])
        with b.tmp("rtmp", [128, 2, 128], F32) as rtmp:
            for hd in range(4):
                for d in range(2):
                    b.ACT(rtmp[:, d, :], ridx[:, d, :], AF.Exp, r=["ridx", "lgb"], w=[("rtmp", d)], scale=lgb[:, d * 4 + hd:d * 4 + hd + 1])
                    b.TT("dve", rtmp[:, d, :], rtmp[:, d, :], ridx[:, 2 + d, :], ALU.mult, r=[("rtmp", d), "ridx"], w=[("rtmp", d)])
                    b.ACT(ret["ecs"][:, d * 4 + hd, :], rrow[:, d, :], AF.Exp, r=["rrow", "lgb"], w=["retecs"], scale=lgb[:, d * 4 + hd:d * 4 + hd + 1])
                b.TT("dve", ret["dtot"][:, hd, :], rtmp[:, 0, :], rtmp[:, 1, :], ALU.add, r=[("rtmp", 0), ("rtmp", 1)], w=["retdtot"])
            for d in range(2):
                b.ACT(ret["tail"][:, d * 4:(d + 1) * 4], lgb[:, d * 4:(d + 1) * 4], AF.Exp, r=["lgb", "rcol"], w=["rettail"], scale=rcol[:, d:d + 1])
            b.ACT(ret["etot"][:, :], lgb[:, :], AF.Exp, r=["lgb"], w=["retetot"], scale=128.0)
            S.barrier()

    b.env = dict(ret=ret, new_ret=new_ret, l0c=l0c, new_k=new_k, new_v=new_v, new_ssd=new_ssd, I=I, vecs=vecs, ident=ident, identb=identb, onesb=onesb, mod=mod, modv=modv, xT=xT, XK=XK,
                 rstd_block=rstd_block, norm_mod=norm_mod, resid_evac=resid_evac)

    groups = [0] if 'g0' in stages else [1] if 'g1' in stages else [0, 1]
    import itertools
    load_x(groups[0])
    ada0 = adaln_it(0)
    for _ in range(8):
        next(ada0)
    ada1 = itertools.chain(ada0, adaln_it(1))
    for gi, g in enumerate(groups):
        if "l0" in stages:
            layer0(b, g, other=ada1 if gi == 0 else None)
        if gi == 0:
            for _ in ada1:
                pass
        if "ffn0" in stages:
            ffn(0, g)
        if "l1" in stages:
            layer1(b, g)
        if "ffn1" in stages:
            ffn(1, g)
        final_out(g, groups[gi + 1] if gi + 1 < len(groups) else None)
    S.finish()
    return b


def layer0(b, g, other=None):
    nc, S, E = b.nc, b.S, b.env

    def tick():
        if other is not None:
            try:
                next(other)
            except StopIteration:
                pass
    I, xT, XK, onesb, identb, ident = E["I"], E["xT"], E["XK"], E["onesb"], E["identb"], E["ident"]
    is_s = (g == 1)
    W = I["l0_w_in"]
    Wo = I["l0_w_out"]
    C0 = E["l0c"]
    nseq, CS, L = (1, 8, 1024) if is_s else (4, 2, 256)
    gate = E["modv"](0, g, 2)
    cnt = [0]

    def nxt():
        cnt[0] += 1
        return cnt[0] % 2
    with b.tmp("mixT", [128, 12, NT], BF16) as mixT:
        if "nona" in b.stages or "nossd" in b.stages:
            b.MEMSET("dve", mixT[:, :, :], 0.0, w=["mixT"])
        with b.tmp("szT", [128, 8, NT], BF16) as szT, b.tmp("xcT", [128, 12, NT], BF16) as xcT, \
                b.tmp("dta", [128, 2, 8, 32], F32) as dta, b.tmp("teall", [128, 8, 64], F32) as teall, \
                b.tmp("ahl", [128, 8, 2, 32], BF16) as ahl:
            dt_tok, a_tok = dta[:, 0, :, :], dta[:, 1, :, :]
            with b.tmp("qT", [128, 4, NT], BF16) as qT, b.tmp("kT", [128, 4, NT], BF16) as kT, \
                    b.tmp("vtok", [128, 8, 512], BF16) as vtok, b.tmp("ostg", [128, 2, 256], F32) as ostg, \
                    b.tmp("Eb", [128, 2, 512], BF16) as Eb, b.tmp("rden", [128, 2, 256], F32) as rden:
                with b.tmp("hT", [128, 8, NT], BF16) as hT, b.tmp("raw", [128, 2, 1024 + 4 * nseq], BF16) as raw, b.tmp("DG", [128, 2, 5, 128], BF16) as DG, \
                        b.tmp("dtt", [128, 4, 32], F32) as dtt:
                    E["norm_mod"](0, g, 0, hT, "hT")

                    def cp_evac(dst, dkey):
                        def ev(m, tb, ps, pk):
                            b.CP("act" if (m + tb) % 2 else "dve", dst[:, m, tb * 512:(tb + 1) * 512], ps, r=[pk], w=[dkey])
                        return ev
                    if "noqk" not in b.stages:
                        b.proj_fm(W, 0, 4, hT, ["hT"], 8, cp_evac(qT, "qT"))
                        b.proj_fm(W, 512, 4, hT, ["hT"], 8, cp_evac(kT, "kT"))
                    for half in range(2 if "nov" not in b.stages else 0):
                        def v_evac(tt, ps, pk, half=half):
                            if is_s:
                                b.CP("act", vtok[:, tt, half * 256:(half + 1) * 256], ps, r=[pk], w=[("vtok", tt)])
                            else:
                                i = nxt()
                                b.CP("dve", ostg[:, i, :], ps, r=[pk], w=[("ostg", i)])
                                b.CP("act", vtok[:, tt, half * 256:(half + 1) * 256], ostg[:, i, :], r=[("ostg", i)], w=[("vtok", tt)])
                                b.STORE(E["new_v"][tt * 128:(tt + 1) * 128, half * 256:(half + 1) * 256], ostg[:, i, :], r=[("ostg", i)], key=("ostg", i))
                        b.proj_tm(W, 1024 + half * 256, 256, hT, ["hT"], v_evac)
                    if not is_s:
                        for half in range(2 if "nok" not in b.stages else 0):
                            def k_evac(tt, ps, pk, half=half):
                                i = nxt()
                                b.CP("dve", ostg[:, i, :], ps, r=[pk], w=[("ostg", i)])
                                b.STORE(E["new_k"][tt * 128:(tt + 1) * 128, half * 256:(half + 1) * 256], ostg[:, i, :], r=[("ostg", i)], key=("ostg", i))
                            b.proj_tm(W, 512 + half * 256, 256, hT, ["hT"], k_evac)

                    def z_evac(m, tb, ps, pk):
                        b.ACT(szT[:, m, tb * 512:(tb + 1) * 512], ps, AF.Silu, r=[pk], w=["szT"])
                    if "noz" not in b.stages:
                        b.proj_fm(W, 1536, 8, hT, ["hT"], 8, z_evac)
                    for i in range(2):
                        b.MEMSET("dve", raw[:, i, :], 0.0, w=[("raw", i)])
                    spb = nseq // 2 if nseq > 1 else 1

                    def x_evac(m, tb, ps, pk):
                        i = m % 2
                        r3 = raw[:, i, :].rearrange("p (s l) -> p s l", s=nseq)
                        if nseq == 1:
                            b.CP("act", raw[:, i, 2 + tb * 512:2 + (tb + 1) * 512], ps, r=[pk], w=[("raw", i)])
                        else:
                            b.CP("act", r3[:, tb * 2:(tb + 1) * 2, 2:2 + L], ps.rearrange("p (s l) -> p s l", s=2), r=[pk], w=[("raw", i)])
                        if tb == 1:
                            b.TT("dve", DG[:, i, :, :], identb[:, :].unsqueeze(1).to_broadcast([128, 5, 128]),
                                 C0["conv_w"][:, m, :].unsqueeze(2).to_broadcast([128, 5, 128]), ALU.mult, r=["identb", "l0c"], w=[("DG", i)])
                            for t2 in range(2):
                                pc, pkc = b.PS()
                                if nseq == 1:
                                    for k in range(5):
                                        b.MM(pc[:, :], DG[:, i, k, :], raw[:, i, t2 * 512 + k:t2 * 512 + k + 512], start=(k == 0), stop=(k == 4),
                                             r=[("raw", i), ("DG", i)], w=[pkc], inc=(k == 4))
                                else:
                                    for sq in range(2):
                                        for k in range(5):
                                            b.MM(pc[:, sq * 256:(sq + 1) * 256], DG[:, i, k, :], r3[:, t2 * 2 + sq, k:k + L], start=(k == 0), stop=(k == 4),
                                                 r=[("raw", i), ("DG", i)], w=[pkc], inc=(sq == 1 and k == 4))
                                b.ACT(xcT[:, m, t2 * 512:(t2 + 1) * 512], pc[:, :], AF.Silu, r=[pkc, "l0c"], w=["xcT"], bias=C0["conv_b"][:, m:m + 1])
                    if "nox" not in b.stages:
                        b.proj_fm(W, 2560, 12, hT, ["hT"], 8, x_evac)

                    def dt_evac(tt, ps, pk):
                        b.TT("dve", dt_tok[:, tt, :], ps, C0["dtbias"][:, :], ALU.add, r=[pk, "l0c"], w=["dt"])
                        u_, s_, s2, p_ = dtt[:, 0, :], dtt[:, 1, :], dtt[:, 2, :], dtt[:, 3, :]
                        K_ = ["dtt"]
                        b.ACT(u_, dt_tok[:, tt, :], AF.Exp, r=["dt"], w=K_)
                        b.TS("dve", s_, u_, 2.0, None, ALU.add, r=K_, w=K_)
                        b.RECIP(s_, s_, r=K_, w=K_)
                        b.TT("dve", s_, s_, u_, ALU.mult, r=K_, w=K_)
                        b.TT("dve", s2, s_, s_, ALU.mult, r=K_, w=K_)
                        b.TS("dve", p_, s2, 1.0 / 11.0, 1.0 / 9.0, ALU.mult, ALU.add, r=K_, w=K_)
                        for cf in (1.0 / 7.0, 1.0 / 5.0, 1.0 / 3.0, 1.0):
                            b.TT("dve", p_, p_, s2, ALU.mult, r=K_, w=K_)
                            b.TS("dve", p_, p_, cf, None, ALU.add, r=K_, w=K_)
                        b.TT("dve", p_, p_, s_, ALU.mult, r=K_, w=K_)
                        b.TS("dve", dt_tok[:, tt, :], p_, 2.0, None, ALU.mult, r=K_, w=["dt"])
                        b.TT("dve", a_tok[:, tt, :], dt_tok[:, tt, :], C0["negA"][:, :], ALU.mult, r=["dt", "l0c"], w=["dt"])
                        b.CP("dve", ahl[:, tt, 0, :], a_tok[:, tt, :], r=["dt"], w=["dt"])
                        b.TT("dve", ahl[:, tt, 1, :], a_tok[:, tt, :], ahl[:, tt, 0, :], ALU.subtract, r=["dt"], w=["dt"])
                    if "nodt" not in b.stages:
                        b.proj_tm(W, 4096, 32, hT, ["hT"], dt_evac)
                    b.dump(f"h{g}", hT[:, :, :], (128, 8, NT), ["hT"])
                    b.dump(f"xc{g}", xcT[:, :, :], (128, 12, NT), ["xcT"])
                    b.dump(f"sz{g}", szT[:, :, :], (128, 8, NT), ["szT"])
                    b.dump(f"dt{g}", dta[:, :, :, :], (128, 2, 8, 32), ["dt"])
                    S.barrier()
                if not is_s:
                    def ctx_s1(s, h, i):
                        hp, ht = (h % 2) * 64, h // 2
                        tq = slice(s * 256, (s + 1) * 256)
                        ps, pk = b.PS()
                        for c in range(2):
                            tkk = slice(s * 256 + c * 128, s * 256 + (c + 1) * 128)
                            b.MM(ps[:, c * 256:(c + 1) * 256], kT[hp:hp + 64, ht, tkk], qT[hp:hp + 64, ht, tq], r=["kT", "qT"], w=[pk], inc=(c == 1))
                        b.ACT(Eb[:, i, :], ps[:, :], AF.Exp, r=[pk], w=[("Eb", i)], scale=0.125)

                    def ctx_s2(s, h, i):
                        hp, ht = (h % 2) * 64, h // 2
                        tq = slice(s * 256, (s + 1) * 256)
                        po, pko = b.PS()
                        for c in range(2):
                            b.MM(po[hp:hp + 64, 0:256], vtok[:, s * 2 + c, h * 64:(h + 1) * 64], Eb[:, i, c * 256:(c + 1) * 256], start=(c == 0), stop=(c == 1),
                                 r=[("vtok", s * 2 + c), ("Eb", i)], w=[pko], inc=False)
                        for c in range(2):
                            b.MM(po[hp:hp + 64, 256:512], onesb[:, 0:64], Eb[:, i, c * 256:(c + 1) * 256], start=(c == 0), stop=(c == 1),
                                 r=["onesb", ("Eb", i)], w=[pko], inc=(c == 1))
                        b.ACT(rden[hp:hp + 64, i, :], po[hp:hp + 64, 256:512], AF.Ln, r=[pko], w=[("rden", i)])
                        b.ACT(rden[hp:hp + 64, i, :], rden[hp:hp + 64, i, :], AF.Exp, r=[("rden", i)], w=[("rden", i)], scale=-1.0)
                        b.TT("dve", mixT[hp:hp + 64, ht, tq], po[hp:hp + 64, 0:256], rden[hp:hp + 64, i, :], ALU.mult, r=[pko, ("rden", i)], w=["mixT"])
                    its = [(s, h) for s in range(4 if "nona" not in b.stages else 0) for h in range(8)]
                    for n in range(len(its) + 1):
                        if n < len(its):
                            ctx_s1(its[n][0], its[n][1], n % 2)
                        if n >= 1:
                            ctx_s2(its[n - 1][0], its[n - 1][1], (n - 1) % 2)
                        tick()
                else:
                    with b.tmp("kcT", [128, 4, 256], BF16) as kcT, b.tmp("vc", [128, 2, 512], BF16) as vc, \
                            b.tmp("expB", [128, 8, 16, 64], BF16) as expB:
                      with b.tmp("ctmp", [128, 2, 1024], F32) as ctmp, b.tmp("namask", [128, 1024], F32) as nmask:
                        for c in range(2):
                            b.LOAD(ctmp[:, 0, 0:512], I["k_ctx"][c * 128:(c + 1) * 128, :], w=[("ctmp", 0)])
                            b.LOAD(ctmp[:, 1, 0:512], I["v_ctx"][c * 128:(c + 1) * 128, :], w=[("ctmp", 1)])
                            b.CP("dve", vc[:, c, :], ctmp[:, 1, 0:512], r=[("ctmp", 1)], w=["vc"])
                            ps, pk = b.PS()
                            for m in range(4):
                                b.TR(ps[:, m * 128:(m + 1) * 128], ctmp[:, 0, m * 128:(m + 1) * 128], ident[:, :], r=[("ctmp", 0), "ident"], w=[pk], inc=(m == 3))
                            b.CP("act", kcT[:, :, c * 128:(c + 1) * 128], ps[:, :].rearrange("p (m t) -> p m t", m=4), r=[pk], w=["kcT"])
                        b.LOAD(nmask[:, :], I["na_mask"][:, :], w=["na_mask"])
                        for h in range(8):
                            i = h % 2
                            b.LOAD(ctmp[:, i, :], I["na_biasG"][:, h].rearrange("p d c -> p (d c)"), w=[("ctmp", i)])
                            b.ACT(ctmp[:, i, :], ctmp[:, i, :], AF.Exp, r=[("ctmp", i)], w=[("ctmp", i)])
                            b.TT("dve", expB[:, h, :, :].rearrange("p d c -> p (d c)"), ctmp[:, i, :], nmask[:, :], ALU.mult,
                                 r=[("ctmp", i), "na_mask"], w=["expB"])
                        S.barrier()
                      if True:
                        def lat_info(r_):
                            r0 = min(max(r_ - 4, 0), 8)
                            kcs = list(range(r0 // 2, (r0 + 7) // 2 + 1))
                            return r0, kcs, len(kcs), 2 * kcs[0] - r_ + 8

                        def lat_s1(r_, h, i):
                            r0, kcs, nl, d0 = lat_info(r_)
                            hp, ht = (h % 2) * 64, h // 2
                            tq = slice(r_ * 64, (r_ + 1) * 64)
                            ps, pk = b.PS()
                            n = nl + 2
                            for mi, kc in enumerate(kcs):
                                b.MM(ps[:, mi * 64:(mi + 1) * 64], kT[hp:hp + 64, ht, kc * 128:(kc + 1) * 128], qT[hp:hp + 64, ht, tq], r=["kT", "qT"], w=[pk], inc=False)
                            for cc in range(2):
                                b.MM(ps[:, (nl + cc) * 64:(nl + cc + 1) * 64], kcT[hp:hp + 64, ht, cc * 128:(cc + 1) * 128], qT[hp:hp + 64, ht, tq], r=["kcT", "qT"], w=[pk], inc=(cc == 1))
                            b.ACT(Eb[:, i, 0:n * 64], ps[:, 0:n * 64], AF.Exp, r=[pk], w=[("Eb", i)], scale=0.125)
                            b.TT("dve", Eb[:, i, 0:nl * 64].rearrange("p (m c) -> p m c", m=nl), Eb[:, i, 0:nl * 64].rearrange("p (m c) -> p m c", m=nl),
                                 expB[:, h, d0:d0 + 2 * nl - 1:2, :], ALU.mult, r=[("Eb", i), "expB"], w=[("Eb", i)])

                        def lat_s2(r_, h, i):
                            r0, kcs, nl, d0 = lat_info(r_)
                            hp, ht = (h % 2) * 64, h // 2
                            tq = slice(r_ * 64, (r_ + 1) * 64)
                            n = nl + 2
                            po, pko = b.PS()
                            for which in range(2):
                                for mi in range(n):
                                    if mi < nl:
                                        kc = kcs[mi]
                                        e0 = r0 <= 2 * kc <= r0 + 7
                                        e1 = r0 <= 2 * kc + 1 <= r0 + 7
                                        lo, hi = (0 if e0 else 64), (128 if e1 else 64)
                                        lhs = vtok[lo:hi, kc, h * 64:(h + 1) * 64] if which == 0 else onesb[lo:hi, 0:64]
                                        rk = ("vtok", kc)
                                    else:
                                        lo, hi = 0, 128
                                        lhs = vc[:, mi - nl, h * 64:(h + 1) * 64] if which == 0 else onesb[:, 0:64]
                                        rk = "vc"
                                    b.MM(po[hp:hp + 64, which * 64:(which + 1) * 64], lhs, Eb[lo:hi, i, mi * 64:(mi + 1) * 64], start=(mi == 0), stop=(mi == n - 1),
                                         r=[rk, "onesb", ("Eb", i)], w=[pko], inc=(which == 1 and mi == n - 1))
                            b.ACT(rden[hp:hp + 64, i, 0:64], po[hp:hp + 64, 64:128], AF.Ln, r=[pko], w=[("rden", i)])
                            b.ACT(rden[hp:hp + 64, i, 0:64], rden[hp:hp + 64, i, 0:64], AF.Exp, r=[("rden", i)], w=[("rden", i)], scale=-1.0)
                            b.TT("dve", mixT[hp:hp + 64, ht, tq], po[hp:hp + 64, 0:64], rden[hp:hp + 64, i, 0:64], ALU.mult, r=[pko, ("rden", i)], w=["mixT"])
                        its = [(r_, h) for r_ in range(16 if "nona" not in b.stages else 0) for h in range(8)]
                        for n_ in range(len(its) + 1):
                            if n_ < len(its):
                                lat_s1(its[n_][0], its[n_][1], n_ % 2)
                            if n_ >= 1:
                                lat_s2(its[n_ - 1][0], its[n_ - 1][1], (n_ - 1) % 2)
                            if n_ % 4 == 0:
                                tick()
                        S.barrier()
                S.barrier()
            b.dump(f"att{g}", mixT[:, 0:4, :], (128, 4, NT), ["mixT"])
            with b.tmps(("xtok", [128, 2, 1024], BF16), ("Btok", [128, 2, 256], BF16), ("Sst", [128, 2, 1024], F32),
                        ("Sfb", [128, 1024], BF16), ("Sbin", [128, CS, 1024], BF16), ("R1", [128, 2, 2, 512], mybir.dt.float32r),
                        ("DE", [128, 2, 2, 2, 512], BF16), ("MC", [128, 2, 2, 2, 512], BF16), ("vd", [128, 2, 2, 256], BF16),
                        ("vt", [128, 1, 1024], BF16), ("Gm", [128, 2, 2, 2, 128], BF16), ("yz", [128, 2, 512], F32),
                        ("sq", [128, 2, 512], BF16), ("rs", [128, 2, 128], F32), ("wdt", [128, 2, 32], F32),
                        ("stmp", [128, 1, 512], F32)) as (xtok2, Btok2, Sst, Sfb, Sbin, R1, DE, MC, vd, vtl, Gm, yz, sqb, rs, wdt, stmp):
                tri, trib = C0["tri"], C0["trib"]
                tokc = [0]

                def tok_tiles(c):
                    tokc[0] += 1
                    i = tokc[0] % 2
                    tk = slice(c * 128, (c + 1) * 128)
                    ps, pk = b.PS()
                    psb = ps[:, :].bitcast(BF16)
                    for m in range(8):
                        b.TR(psb[:, m * 128:(m + 1) * 128], xcT[:, m, tk], identb[:, :], r=["xcT", "identb"], w=[pk], inc=(m == 7))
                    b.CP("act" if c % 2 else "dve", xtok2[:, i, :], psb[:, :], r=[pk], w=[("xtok", i)])
                    ps, pk = b.PS()
                    psb = ps[:, :].bitcast(BF16)
                    for m in range(2):
                        b.TR(psb[:, m * 128:(m + 1) * 128], xcT[:, 8 + m, tk], identb[:, :], r=["xcT", "identb"], w=[pk], inc=(m == 1))
                    b.CP("dve" if c % 2 else "act", Btok2[:, i, :], psb[:, 0:256], r=[pk], w=[("Btok", i)])
                    return i
                for c in range(8 if "notails" not in b.stages else 0):
                    pt, pkt = b.PS()
                    for hl in range(2):
                        b.MM(pt[:, 0:16], trib[:, 3, :], ahl[:, c, hl, 0:16], start=(hl == 0), stop=(hl == 1), r=["dt", "tri"], w=[pkt], inc=False)
                    for hl in range(2):
                        b.MM(pt[:, 16:32], trib[:, 2, :], ahl[:, c, hl, 16:32], start=(hl == 0), stop=(hl == 1), r=["dt", "tri"], w=[pkt], inc=False)
                    for hl in range(2):
                        b.MM(pt[:, 32:64], trib[:, 4, :], ahl[:, c, hl, 0:32], start=(hl == 0), stop=(hl == 1), r=["dt", "tri"], w=[pkt], inc=(hl == 1))
                    b.ACT(teall[:, c, :], pt[:, 0:64], AF.Exp, r=[pkt], w=[("te", c)])

                def upd(d, c, bi):
                    i = nxt()
                    vi = 0
                    b.TT("dve", wdt[:, i, 0:16], dt_tok[:, c, d * 16:(d + 1) * 16], teall[:, c, d * 16:(d + 1) * 16], ALU.mult, r=["dt", ("te", c)], w=[("wdt", i)])
                    b.TT("dve", vtl[:, vi, :].rearrange("p (h v) -> p h v", h=16), xtok2[:, bi, :].rearrange("p (h v) -> p h v", h=16),
                         wdt[:, i, 0:16].unsqueeze(2).to_broadcast([128, 16, 64]), ALU.mult, r=[("xtok", bi), ("wdt", i)], w=[("vtl", vi)])
                    for grp in range(2):
                        gs = slice(grp * 512, (grp + 1) * 512)
                        ps, pk = b.PS()
                        b.MM(ps[:, :], Btok2[:, bi, grp * 128:(grp + 1) * 128], vtl[:, vi, gs], r=[("Btok", bi), ("vtl", vi)], w=[pk])
                        j = 0
                        b.TT("dve", stmp[:, j, :].rearrange("p (h v) -> p h v", h=8), Sst[:, d, gs].rearrange("p (h v) -> p h v", h=8),
                             teall[:, c, 32 + d * 16 + grp * 8:32 + d * 16 + grp * 8 + 8].unsqueeze(2).to_broadcast([128, 8, 64]), ALU.mult,
                             r=[("S", d), ("te", c)], w=[("stmp", j)])
                        b.TT("dve", Sst[:, d, gs], stmp[:, j, :], ps[:, :], ALU.add, r=[("stmp", j), pk], w=[("S", d)])
                for s in range(nseq if "nossd" not in b.stages else 0):
                    if is_s:
                        for d in range(2):
                            b.LOAD(Sst[:, d, :].rearrange("p (h v) -> p h v", h=16), I["ssd_state"][d].rearrange("h n v -> n h v"), w=[("S", d)])
                    else:
                        b.MEMSET("dve", Sst[:, 0, :], 0.0, w=[("S", 0)])
                        b.MEMSET("dve", Sst[:, 1, :], 0.0, w=[("S", 1)])
                    for cc in reversed(range(CS)):
                        c = s * CS + cc
                        b.CP("act", Sbin[:, cc, :], Sst[:, 1, :], r=[("S", 1)], w=[("Sbin", cc)])
                        bi = tok_tiles(c)
                        upd(1, c, bi)
                    if not is_s:
                        b.STORE(E["new_ssd"][s, 1].rearrange("h n v -> n h v"), Sst[:, 1, :].rearrange("p (h v) -> p h v", h=16), r=[("S", 1)], key=("So", 1))
                    info = {}

                    def stageA(cc, qd, par):
                        c = s * CS + cc
                        tk = slice(c * 128, (c + 1) * 128)
                        grp, cp = qd // 2, cc % 2
                        if qd == 0:
                            info[cc] = tok_tiles(c)
                            for g2 in range(2):
                                pg, pkg = b.PS()
                                b.MM(pg[:, 0:128], xcT[:, 8 + g2, tk], xcT[:, 10 + g2, tk], r=["xcT"], w=[pkg])
                                for d in range(2):
                                    b.TT("dve", Gm[:, cp, g2, d, :], pg[:, 0:128], trib[:, d, :], ALU.mult, r=[pkg, "tri"], w=[("Gm", cp, g2, d)])
                        bi = info[cc]
                        pds = {}
                        for d in range(2):
                            for hh in range(4):
                                dh = d * 16 + qd * 4 + hh
                                b.ACT(R1[:, par, d, hh * 128:(hh + 1) * 128], tri[:, d, :], AF.Identity, r=["dt", "tri"], w=[("R1", par, d)],
                                      scale=a_tok[:, c, dh:dh + 1])
                            b.TT("dve", vd[:, par, d, :].rearrange("p (h v) -> p h v", h=4), xtok2[:, bi, qd * 256:(qd + 1) * 256].rearrange("p (h v) -> p h v", h=4),
                                 dt_tok[:, c, d * 16 + qd * 4:d * 16 + qd * 4 + 4].unsqueeze(2).to_broadcast([128, 4, 64]), ALU.mult,
                                 r=[("xtok", bi), "dt"], w=[("vd", par, d)])
                        for d in range(2):
                            pd, pkd = b.PS()
                            b.MM(pd[:, :], C0["trir"][:, 3 - d, :], R1[:, par, d, :], r=[("R1", par, d), "tri"], w=[pkd])
                            pc, pkc = b.PS()
                            b.MM(pc[:, :], C0["trir"][:, 4, :], R1[:, par, d, :], r=[("R1", par, d), "tri"], w=[pkc])
                            pds[d] = (pd, pkd, pc, pkc)
                        for d in range(2):
                            pd, pkd, pc, pkc = pds[d]
                            b.ACT(DE[:, par, d, 0, :], pd[:, :], AF.Exp, r=[pkd], w=[("DE", par, d, 0)])
                            b.ACT(DE[:, par, d, 1, :], pc[:, :], AF.Exp, r=[pkc], w=[("DE", par, d, 1)])

                    def stageB(cc, qd, par):
                        c = s * CS + cc
                        tk = slice(c * 128, (c + 1) * 128)
                        grp, cp, q2 = qd // 2, cc % 2, qd % 2
                        if qd == 0:
                            b.CP("act", Sfb[:, :], Sst[:, 0, :], r=[("S", 0)], w=["Sfb"])
                        for d in range(2):
                            b.TT("dve", MC[:, par, d, 0, :].rearrange("p (h t) -> p h t", h=4), DE[:, par, d, 0, :].rearrange("p (h t) -> p h t", h=4),
                                 Gm[:, cp, grp, d:d + 1, :].to_broadcast([128, 4, 128]), ALU.mult, r=[("DE", par, d, 0), ("Gm", cp, grp, d)], w=[("MC", par, d, 0)])
                            b.TT("dve", MC[:, par, d, 1, :].rearrange("p (h t) -> p h t", h=4), DE[:, par, d, 1, :].rearrange("p (h t) -> p h t", h=4),
                                 xcT[:, 10 + grp:11 + grp, tk].to_broadcast([128, 4, 128]), ALU.mult, r=[("DE", par, d, 1), "xcT"], w=[("MC", par, d, 1)])
                        py, pky = b.PS()
                        for hh in range(4):
                            h = qd * 4 + hh
                            hp = (h % 2) * 64
                            o = py[hp:hp + 64, (hh // 2) * 128:(hh // 2 + 1) * 128]
                            hs = slice(hh * 128, (hh + 1) * 128)
                            for d in range(2):
                                st = (Sfb[:, h * 64:(h + 1) * 64], "Sfb") if d == 0 else (Sbin[:, cc, h * 64:(h + 1) * 64], ("Sbin", cc))
                                b.MM(o, vd[:, par, d, hh * 64:(hh + 1) * 64], MC[:, par, d, 0, hs], start=(d == 0), stop=False,
                                     r=[("vd", par, d), ("MC", par, d, 0)], w=[pky], inc=False)
                                b.MM(o, st[0], MC[:, par, d, 1, hs], start=False, stop=(d == 1), r=[st[1], ("MC", par, d, 1)], w=[pky], inc=(d == 1))
                        for mm in range(2):
                            m = qd * 2 + mm
                            b.STT(yz[:, grp, (q2 * 2 + mm) * 128:(q2 * 2 + mm + 1) * 128], xcT[:, m, tk], C0["dskip"][:, m:m + 1], py[:, mm * 128:(mm + 1) * 128],
                                  ALU.mult, ALU.add, r=["xcT", "l0c", pky], w=[("yz", grp, q2)])
                        b.TT("dve", yz[:, grp, q2 * 256:(q2 + 1) * 256].rearrange("p (m t) -> p m t", m=2), yz[:, grp, q2 * 256:(q2 + 1) * 256].rearrange("p (m t) -> p m t", m=2),
                             szT[:, qd * 2:qd * 2 + 2, tk], ALU.mult, r=[("yz", grp, q2), "szT"], w=[("yz", grp, q2)])
                        if q2 == 1:
                            b.ACT(sqb[:, grp, :], yz[:, grp, :], AF.Square, r=[("yz", grp, 0), ("yz", grp, 1)], w=[("sqb", grp)])
                            pn, pkn = b.PS()
                            for t in range(4):
                                b.MM(pn[:, 0:128], onesb[:, :], sqb[:, grp, t * 128:(t + 1) * 128], start=(t == 0), stop=(t == 3), r=[("sqb", grp), "onesb"], w=[pkn], inc=(t == 3))
                            b.ACT(rs[:, grp, :], pn[:, 0:128], AF.Ln, r=[pkn], w=[("rs0", grp)], scale=1.0 / 512.0, bias=EPS)
                            b.ACT(rs[:, grp, :], rs[:, grp, :], AF.Exp, r=[("rs0", grp)], w=[("rs0", grp)], scale=-0.5)
                            for t in range(4):
                                m = grp * 4 + t
                                b.STT(mixT[:, 4 + m, tk], yz[:, grp, t * 128:(t + 1) * 128], C0["ssdnw"][:, m:m + 1], rs[:, grp, :], ALU.mult, ALU.mult,
                                      r=[("yz", grp, 0), ("yz", grp, 1), "l0c", ("rs0", grp)], w=["mixT"])
                        if qd == 3:
                            upd(0, c, info[cc])
                    units = [(cc, qd) for cc in range(CS) for qd in range(4)]
                    for n_ in range(len(units) + 1):
                        if n_ < len(units):
                            stageA(units[n_][0], units[n_][1], n_ % 2)
                        if n_ >= 1:
                            stageB(units[n_ - 1][0], units[n_ - 1][1], (n_ - 1) % 2)
                        tick()
                    if not is_s:
                        b.STORE(E["new_ssd"][s, 0].rearrange("h n v -> n h v"), Sst[:, 0, :].rearrange("p (h v) -> p h v", h=16), r=[("S", 0)], key=("So", 0))
                S.barrier()
            S.barrier()
        b.dump(f"mix{g}", mixT[:, :, :], (128, 12, NT), ["mixT"])
        for m in range(8):
            slot, wk = b.wload(b.wview(Wo, 0, 12, m * 128, 128))
            for tb in range(2):
                ps, pk = b.PS()
                for kt in range(12):
                    b.MM(ps[:, :], slot[:, kt, :], mixT[:, kt, tb * 512:(tb + 1) * 512], start=(kt == 0), stop=(kt == 11), r=wk + ["mixT"], w=[pk], inc=(kt == 11))
                b.STT(xT[:, m, tb * 512:(tb + 1) * 512], ps[:, :], gate[:, m:m + 1], xT[:, m, tb * 512:(tb + 1) * 512], ALU.mult, ALU.add,
                      r=[pk, "mod0"] + XK, w=XK)
    S.barrier()


def layer1(b, g):
    nc, S, E = b.nc, b.S, b.env
    I, xT, XK, onesb, identb = E["I"], E["xT"], E["XK"], E["onesb"], E["identb"]
    is_s = (g == 1)
    W = I["l1_w_in"]
    Wo = I["l1_w_out"]
    R = E["ret"]
    nseq, CS = (1, 8) if is_s else (4, 2)
    gate = E["modv"](1, g, 2)
    with b.tmp("hT", [128, 8, NT], BF16) as hT, b.tmp("ropec", [128, 2, NT if is_s else 2], F32) as ropec, \
            b.tmp("ropes", [128, 2, NT if is_s else 2], F32) as ropes:
        if is_s:
            R = dict(R)
            R["cos"], R["sin"] = ropec, ropes
            b.LOAD(ropec[:, :, :], I["rope_cos"][:, :, :], w=["rope"], semkey="ropec")
            b.LOAD(ropes[:, :, :], I["rope_sin"][:, :, :], w=["rope"], semkey="ropes")
        E["norm_mod"](1, g, 0, hT, "hT")
        with b.tmps(("qT", [128, 2, NT], BF16), ("kT", [128, 2, NT], BF16), ("vtok", [128, 8, 512], BF16),
                    ("sgT", [128, 4, NT], BF16), ("ktok", [128, 8, 256], BF16), ("ygT", [128, 4, NT], BF16),
                    ("Sst", [128, 2, 2, 512], F32), ("Sfin", [128, 8, 2, 512], BF16), ("Sbin", [128, 8, 2, 512], BF16),
                    ("rt", [128, 6, 512 if is_s else 2], F32), ("sm", [128, 3, 768], BF16), ("sq", [128, 3, 512], BF16),
                    ("rs", [128, 3, 128], F32), ("yt", [128, 3, 512], F32),
                    ("ktl", [128, 2, 256], BF16), ("rawr", [128, 2, 512 if is_s else 2], mybir.dt.float32r)) as (qT, kT, vtok, sgT, ktok, ygT, Sst, Sfin, Sbin, rt, sm, sqb, rs, yt, ktl, rawr):
            for hd in range(4):
                cnt = [0]

                def qk_evac(dst, dkey, scl):
                    def ev(m, tb, ps, pk):
                        sl = slice(tb * 512, (tb + 1) * 512)
                        if not is_s:
                            b.ACT(dst[:, m, sl], ps, AF.Copy, r=[pk], w=[dkey], scale=scl)
                            return
                        i = cnt[0] % 2
                        cnt[0] += 1
                        raw, t1, t2 = rawr[:, i, :], rt[:, i * 3 + 1, :], rt[:, i * 3 + 2, :]
                        b.ACT(raw, ps, AF.Copy, r=[pk], w=[("rt", i, 0)], scale=scl)
                        p2, pk2 = b.PS()
                        b.MM(p2[:, :], R["permPr"][:, :], raw, r=[("rt", i, 0), "permP"], w=[pk2])
                        b.TT("dve", t1, raw, R["cos"][:, m, sl], ALU.mult, r=[("rt", i, 0), "rope"], w=[("rt", i, 1)])
                        b.TT("dve", t2, p2[:, :], R["sin"][:, m, sl], ALU.mult, r=[pk2, "rope"], w=[("rt", i, 2)])
                        b.TT("dve", dst[:, m, sl], t1, t2, ALU.add, r=[("rt", i, 1), ("rt", i, 2)], w=[dkey])
                    return ev
                b.proj_fm(W, hd * 256, 2, hT, ["hT"], 8, qk_evac(qT, "qT", 1.0))
                b.proj_fm(W, 1024 + hd * 256, 2, hT, ["hT"], 8, qk_evac(kT, "kT", 1.0 / 16.0))
                for half in range(2):
                    def v_evac(tt, ps, pk, half=half):
                        b.CP("act" if tt % 2 else "dve", vtok[:, tt, half * 256:(half + 1) * 256], ps, r=[pk], w=[("vtok", tt)])
                    b.proj_tm(W, 2048 + hd * 512 + half * 256, 256, hT, ["hT"], v_evac)
                def g_evac(m, tb, ps, pk):
                    i = cnt[0] % 2
                    cnt[0] += 1
                    b.ACT(yt[:, i, :], ps, AF.Silu, r=[pk], w=[("yt", i)])
                    b.ACT(sgT[:, m, tb * 512:(tb + 1) * 512], yt[:, i, :], AF.Copy, r=[("yt", i), "retnw"], w=["sgT"],
                          scale=R["normw"][:, hd * 4 + m:hd * 4 + m + 1])
                g_it = b.proj_fm_it(W, 4096 + hd * 512, 4, hT, ["hT"], 8, g_evac)
                for c in range(8):
                    ps, pk = b.PS()
                    psb = ps[:, :].bitcast(BF16)
                    for kt in range(2):
                        b.TR(psb[:, kt * 128:(kt + 1) * 128], kT[:, kt, c * 128:(c + 1) * 128], identb[:, :], r=["kT", "identb"], w=[pk], inc=(kt == 1))
                    b.CP("act" if c % 2 else "dve", ktok[:, c, :], psb[:, 0:256], r=[pk], w=[("ktok", c)])
                Sf, Sb = Sst[:, 0, :, :], Sst[:, 1, :, :]
                EtF, EtB = R["etot"][:, hd:hd + 1], R["etot"][:, 4 + hd:5 + hd]

                def upd(d, c, tailcol, et):
                    Sd = Sst[:, d, :, :]
                    i = cnt[0] % 2
                    cnt[0] += 1
                    b.ACT(ktl[:, i, :], ktok[:, c, :], AF.Copy, r=[("ktok", c), "rettail"], w=[("ktl", i)], scale=tailcol)
                    for kt in range(2):
                        ps, pk = b.PS()
                        b.MM(ps[:, :], ktl[:, i, kt * 128:(kt + 1) * 128], vtok[:, c, :], r=[("ktl", i), ("vtok", c)], w=[pk])
                        b.STT(Sd[:, kt, :], Sd[:, kt, :], et, ps[:, :], ALU.mult, ALU.add, r=[("S", d), pk, "retetot"], w=[("S", d)])

                def state_steps():
                    for s in range(nseq):
                        if is_s:
                            for d in range(2):
                                b.LOAD(Sst[:, d, :, :], I["ret_state"][d, hd].rearrange("(kt p) v -> p kt v", p=128), w=[("S", d)])
                        else:
                            b.MEMSET("dve", Sst[:, 0, :, :], 0.0, w=[("S", 0)])
                            b.MEMSET("dve", Sst[:, 1, :, :], 0.0, w=[("S", 1)])
                        yield
                        for k in range(CS):
                            cb, cf = s * CS + CS - 1 - k, s * CS + k
                            b.CP("act", Sbin[:, cb, :, :], Sb, r=[("S", 1)], w=[("Sbin", cb)])
                            upd(1, cb, R["tail"][:, 4 + hd:5 + hd], EtB)
                            b.CP("act", Sfin[:, cf, :, :], Sf, r=[("S", 0)], w=[("Sfin", cf)])
                            upd(0, cf, R["tail"][:, hd:hd + 1], EtF)
                            yield
                        if not is_s:
                            b.STORE(E["new_ret"][s, 1, hd].rearrange("(kt p) v -> p kt v", p=128), Sb, r=[("S", 1)], key=("So", 1))
                            b.STORE(E["new_ret"][s, 0, hd].rearrange("(kt p) v -> p kt v", p=128), Sf, r=[("S", 0)], key=("So", 0))
                its_ = [g_it, state_steps()]
                while its_:
                    for it_ in list(its_):
                        try:
                            next(it_)
                        except StopIteration:
                            its_.remove(it_)
                if True:
                    s = 0
                    CSs = CS
                    CS = 8
                    pys = {}

                    def stage1(cc):
                        c = s * CS + cc
                        tk = slice(c * 128, (c + 1) * 128)
                        i = c % 3
                        pg, pkg = b.PS()
                        for kt in range(2):
                            b.MM(pg[:, 0:128], kT[:, kt, tk], qT[:, kt, tk], start=(kt == 0), stop=(kt == 1), r=["kT", "qT"], w=[pkg], inc=(kt == 1))
                        M = sm[:, i, 0:128]
                        qdf = sm[:, i, 256:512].rearrange("p (k t) -> p k t", k=2)
                        qdb = sm[:, i, 512:768].rearrange("p (k t) -> p k t", k=2)
                        b.TT("dve", M, pg[:, 0:128], R["dtot"][:, hd, :], ALU.mult, r=[pkg, "retdtot"], w=[("sm", i, 0)])
                        b.TT("dve", qdf, qT[:, :, tk], R["ecs"][:, hd:hd + 1, :].to_broadcast([128, 2, 128]), ALU.mult, r=["qT", "retecs"], w=[("sm", i, 1)])
                        b.TT("dve", qdb, qT[:, :, tk], R["ecs"][:, 4 + hd:5 + hd, :].to_broadcast([128, 2, 128]), ALU.mult, r=["qT", "retecs"], w=[("sm", i, 2)])

                    def stage2(cc):
                        c = s * CS + cc
                        i = c % 3
                        M = sm[:, i, 0:128]
                        qdf = sm[:, i, 256:512].rearrange("p (k t) -> p k t", k=2)
                        qdb = sm[:, i, 512:768].rearrange("p (k t) -> p k t", k=2)
                        py, pky = b.PS()
                        pys[cc] = (py, pky)
                        for vt in range(4):
                            vs = slice(vt * 128, (vt + 1) * 128)
                            o = py[:, vs]
                            b.MM(o, vtok[:, c, vs], M, start=True, stop=False, r=[("vtok", c), ("sm", i, 0)], w=[pky], inc=False)
                            for kt in range(2):
                                b.MM(o, Sfin[:, cc, kt, vs], qdf[:, kt, :], start=False, stop=False, r=[("Sfin", cc), ("sm", i, 1)], w=[pky], inc=False)
                            for kt in range(2):
                                b.MM(o, Sbin[:, cc, kt, vs], qdb[:, kt, :], start=False, stop=(kt == 1), r=[("Sbin", cc), ("sm", i, 2)], w=[pky], inc=(kt == 1))
                        b.ACT(sqb[:, i, :], py[:, :], AF.Square, r=[pky], w=[("sqb", i)])

                    def stage3(cc):
                        c = s * CS + cc
                        tk = slice(c * 128, (c + 1) * 128)
                        i = c % 3
                        py, pky = pys.pop(cc)
                        pn, pkn = b.PS()
                        for vt in range(4):
                            b.MM(pn[:, 0:128], onesb[:, :], sqb[:, i, vt * 128:(vt + 1) * 128], start=(vt == 0), stop=(vt == 3), r=[("sqb", i), "onesb"], w=[pkn], inc=(vt == 3))
                        b.ACT(rs[:, i, :], pn[:, 0:128], AF.Ln, r=[pkn], w=[("rs1", i)], scale=1.0 / 512.0, bias=EPS)
                        b.ACT(rs[:, i, :], rs[:, i, :], AF.Exp, r=[("rs1", i)], w=[("rs1", i)], scale=-0.5)
                        y3 = yt[:, i, :].rearrange("p (v t) -> p v t", v=4)
                        b.TT("dve", y3, py[:, :].rearrange("p (v t) -> p v t", v=4), rs[:, i:i + 1, :].to_broadcast([128, 4, 128]), ALU.mult,
                             r=[pky, ("rs1", i)], w=[("yt", i)])
                        b.TT("dve", ygT[:, :, tk], y3, sgT[:, :, tk], ALU.mult, r=[("yt", i), "sgT"], w=["ygT"])
                    for step in range(CS + 2):
                        if step < CS:
                            stage1(step)
                        if 0 <= step - 1 < CS:
                            stage2(step - 1)
                        if 0 <= step - 2 < CS:
                            stage3(step - 2)
                CS = CSs
                for half in range(2):
                    slot, wk = b.wload(b.wview(Wo, hd * 4, 4, half * 512, 512))
                    for j in range(4):
                        m = half * 4 + j
                        for tb in range(2):
                            ps, pk = b.PS()
                            for kt in range(4):
                                b.MM(ps[:, :], slot[:, kt, j * 128:(j + 1) * 128], ygT[:, kt, tb * 512:(tb + 1) * 512], start=(kt == 0), stop=(kt == 3),
                                     r=wk + ["ygT"], w=[pk], inc=(kt == 3))
                            b.STT(xT[:, m, tb * 512:(tb + 1) * 512], ps[:, :], gate[:, m:m + 1], xT[:, m, tb * 512:(tb + 1) * 512],
                                  ALU.mult, ALU.add, r=[pk, "mod1"] + XK, w=XK)
            S.barrier()
    S.barrier()


def _fm(v, n):
    return np.ascontiguousarray(np.asarray(v, np.float32).reshape(n, 128).T)


def prep_core(inp, core):
    f = lambda k: np.ascontiguousarray(np.asarray(inp[k], np.float32))
    bs = core % 4
    m = {}
    xp = f("x_prompt")[core * 4:(core + 1) * 4].reshape(NT, D)
    xs = f("x_sample")[bs]
    m["xg"] = np.ascontiguousarray(np.stack([xp, xs], 0))
    cond = np.stack([f("c_ctx"), f("c")[bs]], 0)
    m["condT"] = np.ascontiguousarray(cond.reshape(2, 8, 128).transpose(2, 1, 0))
    m["ident"] = np.eye(128, dtype=np.float32)
    for nm, src, n in [("l0_norm1", "l0_norm1_w", 8), ("l0_norm2", "l0_norm2_w", 8), ("l1_norm1", "l1_norm1_w", 8),
                       ("l1_norm2", "l1_norm2_w", 8), ("final_norm", "final_norm_w", 8),
                       ("l0_mod_b", "l0_mod_b", 48), ("l1_mod_b", "l1_mod_b", 48)]:
        m[nm] = _fm(inp[src], n)
    for k in ["l0_mod_w", "l1_mod_w", "l0_w_in", "l0_w_out", "l1_w_in", "l1_w_out", "l0_ffn_w1", "l0_ffn_w3", "l0_ffn_w2",
              "l1_ffn_w1", "l1_ffn_w3", "l1_ffn_w2"]:
        m[k] = f(k)
    m["k_ctx"] = f("cache_l0_na_k")[bs].reshape(256, 512)
    m["v_ctx"] = f("cache_l0_na_v")[bs].reshape(256, 512)
    m["ssd_state"] = f("state_l0_ssd")[bs]
    rb = f("l0_na_bias")
    p = np.arange(128)
    e, ck = p // 64, p % 64
    dd = np.arange(16)
    cq = np.arange(64)
    dr = dd[None, :] - 8 + e[:, None]
    vr = (dr >= -7) & (dr <= 7)
    dc = np.clip(ck[:, None] - cq[None, :] + 15, 0, 30)
    c0 = np.clip(cq - 8, 0, 48)
    colin = (ck[:, None] >= c0[None, :]) & (ck[:, None] < c0[None, :] + 16)
    G = rb[:, np.clip(dr, -7, 7)[:, :, None] + 7, dc[:, None, :]]
    G = np.where(vr[None, :, :, None], G, np.float32(0.0))
    m["na_biasG"] = np.ascontiguousarray(G.transpose(1, 0, 2, 3).astype(np.float32))
    m["na_mask"] = np.ascontiguousarray((vr[:, :, None] & colin[:, None, :]).astype(np.float32).reshape(128, 1024))
    tt_, ii_ = np.arange(128)[:, None], np.arange(128)[None, :]
    m["tri"] = np.ascontiguousarray(np.stack([tt_ <= ii_, tt_ >= ii_, tt_ < ii_, tt_ > ii_, np.ones((128, 128), bool)], 1).astype(np.float32))
    m["l0_conv_w"] = np.ascontiguousarray(f("l0_conv_w").T.reshape(12, 128, 5).transpose(1, 0, 2))
    m["l0_conv_b"] = _fm(inp["l0_conv_b"], 12)
    m["l0_dtbias"] = f("l0_ssd_dt_bias").reshape(1, 32)
    m["l0_alog"] = f("l0_ssd_a_log").reshape(1, 32)
    m["l0_dskip"] = _fm(np.repeat(f("l0_ssd_d"), 64), 8)
    m["l0_ssdnw"] = _fm(inp["l0_ssd_norm_w"], 8)
    m["ret_decay_b"] = f("l1_ret_decay").reshape(1, 8)
    jj = np.arange(128)[:, None].astype(np.float32)
    ii = np.arange(128)[None, :].astype(np.float32)
    m["ret_idx"] = np.ascontiguousarray(np.stack([np.maximum(ii - jj, 0), np.maximum(jj - ii, 0), (jj <= ii).astype(np.float32),
                                                  (jj >= ii).astype(np.float32)], 1).astype(np.float32))
    m["ret_row"] = np.ascontiguousarray(np.stack([np.broadcast_to(ii + 1, (128, 128)), np.broadcast_to(128 - ii, (128, 128))], 1).astype(np.float32))
    m["ret_col"] = np.ascontiguousarray(np.concatenate([127 - jj, jj], 1).astype(np.float32))
    m["ret_state"] = f("state_l1_ret")[bs]
    m["l1_ret_norm"] = _fm(inp["l1_ret_norm_w"], 16)
    t = np.arange(NT)
    row = (t // 64).astype(np.float32)
    col = (t % 64).astype(np.float32)
    freqs = (10000.0 ** (-np.arange(0, 128, 2, dtype=np.float32) / 128)).astype(np.float32)
    ang = np.concatenate([row[:, None] * freqs, col[:, None] * freqs], -1)
    angd = np.repeat(ang, 2, axis=1)
    m["rope_cos"] = np.ascontiguousarray(np.cos(angd).T.reshape(2, 128, NT).transpose(1, 0, 2).astype(np.float32))
    m["rope_sin"] = np.ascontiguousarray(np.sin(angd).T.reshape(2, 128, NT).transpose(1, 0, 2).astype(np.float32))
    P = np.zeros((128, 128), np.float32)
    for i in range(64):
        P[2 * i + 1, 2 * i] = -1.0
        P[2 * i, 2 * i + 1] = 1.0
    m["permP"] = P
    return m


_CACHE = {}


def kernel(**inputs):
    if "b" not in _CACHE:
        _CACHE["b"] = build()
    b = _CACHE["b"]
    in_maps = []
    for core in range(8):
        m = prep_core(inputs, core)
        in_maps.append({k: v for k, v in m.items() if k in b.din})
    res = run_bass_kernel_spmd(b.nc, in_maps, core_ids=list(range(8)))
    R = res.results
    y_prompt = np.concatenate([np.asarray(R[c]["y"][0]).reshape(4, 256, D) for c in range(8)], 0).astype(np.float32)
    y_sample = np.stack([np.asarray(R[c]["y"][1]) for c in range(4)], 0).astype(np.float32)
    new_k = np.concatenate([np.asarray(R[c]["new_k"]).reshape(4, 256, 8, 64) for c in range(8)], 0).astype(np.float32)
    new_v = np.concatenate([np.asarray(R[c]["new_v"]).reshape(4, 256, 8, 64) for c in range(8)], 0).astype(np.float32)
    new_ssd = np.concatenate([np.asarray(R[c]["new_ssd"]) for c in range(8)], 0).astype(np.float32)
    new_ret = np.concatenate([np.asarray(R[c]["new_ret"]) for c in range(8)], 0).astype(np.float32)
    return (y_prompt, y_sample, new_k, new_v, new_ssd, new_ret)
```
